# Optimizing a Trainium2 kernel written in Bass

```python
import math
import jax
import jax.numpy as jnp
from jax import lax
import numpy as np

D_MODEL = 1024
BATCH = 32
SEQ = 2048
DEPTH = 4

CTX_LEN = 256
GRID_W = 64
N_GROUPS = 4
GROUP_W = D_MODEL // N_GROUPS
MIX_W = N_GROUPS * GROUP_W
HEAD_DIM = 64
ROPE_DIM = 32
ROPE_THETA = 10000.0
QBLOCK = 128
NORM_EPS = 1e-6
NEG_INF = -1e30

NA_HEADS = GROUP_W // HEAD_DIM
NA_KH = 8
NA_KW = 16
MLA_HEADS = GROUP_W // HEAD_DIM
MLA_Q_RANK = GROUP_W
MLA_KV_RANK = D_MODEL // 8
MLA_NOPE = HEAD_DIM
MLA_V = HEAD_DIM
RWKV_HEADS = GROUP_W // HEAD_DIM
RWKV_N = HEAD_DIM
RWKV_W_RANK = 32
RWKV_A_RANK = 32
RWKV_G_RANK = 64
RWKV_LN_EPS = 64e-5
DIFF_HEADS = GROUP_W // HEAD_DIM
DIFF_QK = HEAD_DIM // 2
DIFF_V = HEAD_DIM
DIFF_LN_EPS = 1e-5
D_FF = ((8 * D_MODEL // 3 + 127) // 128) * 128

NA_COLS = 3 * GROUP_W
MLA_COLS = MLA_Q_RANK + MLA_KV_RANK + ROPE_DIM
RWKV_COLS = 3 * GROUP_W + RWKV_W_RANK + RWKV_A_RANK + RWKV_G_RANK
DIFF_COLS = 3 * GROUP_W
IN_COLS = NA_COLS + MLA_COLS + RWKV_COLS + DIFF_COLS
IN_SPLITS = (NA_COLS, NA_COLS + MLA_COLS, NA_COLS + MLA_COLS + RWKV_COLS)
RWKV_SPLITS = (GROUP_W, 2 * GROUP_W, 3 * GROUP_W, 3 * GROUP_W + RWKV_W_RANK,
               3 * GROUP_W + RWKV_W_RANK + RWKV_A_RANK)

kernel_name = 'hybrid_parallel_group_dit'


def rmsnorm(x, g, eps=NORM_EPS):
    xf = x.astype(jnp.float32)
    y = xf * lax.rsqrt(jnp.mean(xf * xf, axis=-1, keepdims=True) + eps)
    return (y * g.astype(jnp.float32)).astype(x.dtype)


def shift_prev(x):
    return jnp.pad(x, ((0, 0), (1, 0), (0, 0)))[:, :-1]


def shift_next(x):
    return jnp.pad(x, ((0, 0), (0, 1), (0, 0)))[:, 1:]


def axial_rope_tables(n_tokens, dtype):
    t = jnp.arange(n_tokens)
    row = (t // GRID_W).astype(jnp.float32)
    col = (t % GRID_W).astype(jnp.float32)
    half = ROPE_DIM // 2
    freqs = ROPE_THETA ** (-jnp.arange(0, half, 2, dtype=jnp.float32) / half)
    ar = row[:, None] * freqs[None, :]
    ac = col[:, None] * freqs[None, :]
    ang = jnp.concatenate([ar, ar, ac, ac], axis=-1)
    return jnp.cos(ang).astype(dtype), jnp.sin(ang).astype(dtype)


def apply_axial_rope(x, cos, sin):
    r1, r2, c1, c2 = jnp.split(x, 4, axis=-1)
    rot = jnp.concatenate([-r2, r1, -c2, c1], axis=-1)
    shape = (1, cos.shape[0]) + (1,) * (x.ndim - 3) + (cos.shape[1],)
    return x * cos.reshape(shape) + rot * sin.reshape(shape)


def map_query_blocks(fn, q):
    B, T = q.shape[0], q.shape[1]
    nb = T // QBLOCK
    qb = jnp.moveaxis(q.reshape((B, nb, QBLOCK) + q.shape[2:]), 1, 0)
    o = jnp.moveaxis(lax.map(fn, qb), 0, 1)
    return o.reshape((B, T) + o.shape[3:])


def softmax_attention(q, k, v, scale):
    s = jnp.einsum('bqhd,bkhd->bhqk', q, k).astype(jnp.float32) * scale
    p = jax.nn.softmax(s, axis=-1).astype(v.dtype)
    return jnp.einsum('bhqk,bkhd->bqhd', p, v)


def neighbourhood_attention(q, k, v, kc, vc, rpb):
    B, T, H, d = q.shape
    rows = T // GRID_W
    kh = min(NA_KH, rows)
    scale = d ** -0.5
    qg = q.reshape(B, rows, GRID_W, H, d)
    kg = k.reshape(B, rows, GRID_W, H, d)
    vg = v.reshape(B, rows, GRID_W, H, d)
    cpos = np.arange(GRID_W)
    cstart = np.clip(cpos - NA_KW // 2, 0, GRID_W - NA_KW)
    col_mask = (cpos[None, :] >= cstart[:, None]) & (cpos[None, :] < cstart[:, None] + NA_KW)
    col_idx = np.clip(cpos[None, :] - cpos[:, None] + NA_KW - 1, 0, 2 * NA_KW - 2)
    mask = jnp.asarray(np.tile(col_mask, (1, kh)))
    n_lat = kh * GRID_W

    def row_block(r):
        rs = jnp.clip(r - kh // 2, 0, rows - kh)
        q_r = lax.dynamic_index_in_dim(qg, r, axis=1, keepdims=False)
        k_r = lax.dynamic_slice_in_dim(kg, rs, kh, axis=1).reshape(B, n_lat, H, d)
        v_r = lax.dynamic_slice_in_dim(vg, rs, kh, axis=1).reshape(B, n_lat, H, d)
        row_off = rs + jnp.arange(kh) - r + NA_KH - 1
        bias = rpb[:, row_off][:, :, col_idx]
        bias = bias.transpose(0, 2, 1, 3).reshape(H, GRID_W, n_lat)
        s_lat = jnp.einsum('bqhd,bkhd->bhqk', q_r, k_r).astype(jnp.float32) * scale + bias.astype(jnp.float32)
        s_lat = jnp.where(mask, s_lat, NEG_INF)
        s_ctx = jnp.einsum('bqhd,bkhd->bhqk', q_r, kc).astype(jnp.float32) * scale
        p = jax.nn.softmax(jnp.concatenate([s_lat, s_ctx], axis=-1), axis=-1).astype(v.dtype)
        return (jnp.einsum('bhqk,bkhd->bqhd', p[..., :n_lat], v_r)
                + jnp.einsum('bhqk,bkhd->bqhd', p[..., n_lat:], vc))

    out = lax.map(row_block, jnp.arange(rows))
    return out.transpose(1, 0, 2, 3, 4).reshape(B, T, H * d)


def na_mixer(z_lat, z_ctx, rpb, need_ctx):
    B = z_lat.shape[0]
    ql, kl, vl = [t.reshape(B, -1, NA_HEADS, HEAD_DIM) for t in jnp.split(z_lat, 3, axis=-1)]
    qc, kc, vc = [t.reshape(B, -1, NA_HEADS, HEAD_DIM) for t in jnp.split(z_ctx, 3, axis=-1)]
    y_lat = neighbourhood_attention(ql, kl, vl, kc, vc, rpb)
    y_ctx = softmax_attention(qc, kc, vc, HEAD_DIM ** -0.5).reshape(B, -1, GROUP_W) if need_ctx else None
    return y_lat, y_ctx


def mla_project(z, q_norm, kv_norm, w_uq, w_ukv, cos, sin, rotary):
    B, T, _ = z.shape
    cq, ckv, k_rope = jnp.split(z, (MLA_Q_RANK, MLA_Q_RANK + MLA_KV_RANK), axis=-1)
    q = (rmsnorm(cq, q_norm) @ w_uq).reshape(B, T, MLA_HEADS, MLA_NOPE + ROPE_DIM)
    kv = (rmsnorm(ckv, kv_norm) @ w_ukv).reshape(B, T, MLA_HEADS, MLA_NOPE + MLA_V)
    q_nope, q_rope = jnp.split(q, (MLA_NOPE,), axis=-1)
    k_nope, v = jnp.split(kv, (MLA_NOPE,), axis=-1)
    if rotary:
        q_rope = apply_axial_rope(q_rope, cos, sin)
        k_rope = apply_axial_rope(k_rope, cos, sin)
    k_rope = jnp.broadcast_to(k_rope[:, :, None, :], (B, T, MLA_HEADS, ROPE_DIM))
    return (jnp.concatenate([q_nope, q_rope], axis=-1),
            jnp.concatenate([k_nope, k_rope], axis=-1), v)


def mla_mixer(z_lat, z_ctx, cos, sin, q_norm, kv_norm, w_uq, w_ukv, need_ctx):
    B, T, _ = z_lat.shape
    ql, kl, vl = mla_project(z_lat, q_norm, kv_norm, w_uq, w_ukv, cos, sin, True)
    qc, kc, vc = mla_project(z_ctx, q_norm, kv_norm, w_uq, w_ukv, cos, sin, False)
    scale = (MLA_NOPE + ROPE_DIM) ** -0.5
    k_all = jnp.concatenate([kl, kc], axis=1)
    v_all = jnp.concatenate([vl, vc], axis=1)
    y_lat = map_query_blocks(lambda qi: softmax_attention(qi, k_all, v_all, scale), ql).reshape(B, T, GROUP_W)
    y_ctx = softmax_attention(qc, kc, vc, scale).reshape(B, -1, GROUP_W) if need_ctx else None
    return y_lat, y_ctx


def rwkv7_prepare(z, mu, w0, w_up, a0, a_up, g_up, k_k, k_a):
    B, T, _ = z.shape
    zs = z + mu[0] * (shift_prev(z) - z) + mu[1] * (shift_next(z) - z)
    r, k, v, wd, ad, gd = jnp.split(zs, RWKV_SPLITS, axis=-1)

    def heads(t):
        return t.reshape(B, T, RWKV_HEADS, RWKV_N)

    kk = heads(k * k_k).astype(jnp.float32)
    kk = (kk * lax.rsqrt(jnp.maximum(jnp.sum(kk * kk, axis=-1, keepdims=True), 1e-24))).astype(z.dtype)
    g = jax.nn.sigmoid(gd) @ g_up
    per_dir = []
    for d in range(2):
        w = -jax.nn.softplus(-(w0[d] + jnp.tanh(wd) @ w_up[d])) - 0.5
        decay = jnp.exp(-jnp.exp(w.astype(jnp.float32)))
        a = jax.nn.sigmoid(a0[d] + ad @ a_up[d])
        kd = k * (1.0 + (a - 1.0) * k_a)
        per_dir.append((heads(decay), heads(kd), -kk, kk * heads(a)))
    return heads(r), heads(k), heads(v), g, per_dir


def rwkv7_scan(r, decay, k, v, a, b, s0, reverse):
    def step(S, inp):
        r_t, w_t, k_t, v_t, a_t, b_t = inp
        sa = jnp.einsum('bhij,bhj->bhi', S, a_t)
        S = S * w_t[:, :, None, :] + sa[..., None] * b_t[:, :, None, :] + v_t[..., None] * k_t[:, :, None, :]
        return S, jnp.einsum('bhij,bhj->bhi', S, r_t)

    xs = tuple(jnp.moveaxis(t.astype(jnp.float32), 1, 0) for t in (r, decay, k, v, a, b))
    s_final, y = lax.scan(step, s0, xs, reverse=reverse)
    return jnp.moveaxis(y, 0, 1), s_final


def rwkv7_output(y, r, k, v, g, r_k, ln_w, ln_b):
    B, T = y.shape[0], y.shape[1]
    mean = jnp.mean(y, axis=-1, keepdims=True)
    var = jnp.mean(jnp.square(y - mean), axis=-1, keepdims=True)
    yn = ((y - mean) * lax.rsqrt(var + RWKV_LN_EPS)).reshape(B, T, GROUP_W).astype(v.dtype) * ln_w + ln_b
    bonus = (jnp.sum(r * k * r_k, axis=-1, keepdims=True) * v).reshape(B, T, GROUP_W)
    return (yn + bonus) * g


def rwkv7_mixer(z_lat, z_ctx, mu, w0, w_up, a0, a_up, g_up, k_k, k_a, r_k, ln_w, ln_b, need_ctx):
    B = z_lat.shape[0]
    rl, kl, vl, gl, dirs_l = rwkv7_prepare(z_lat, mu, w0, w_up, a0, a_up, g_up, k_k, k_a)
    rc, kc, vc, gc, dirs_c = rwkv7_prepare(z_ctx, mu, w0, w_up, a0, a_up, g_up, k_k, k_a)
    s0 = jnp.zeros((B, RWKV_HEADS, RWKV_N, RWKV_N), jnp.float32)
    ys_lat, ys_ctx = [], []
    for d in range(2):
        reverse = d == 1
        dec_c, kd_c, a_c, b_c = dirs_c[d]
        dec_l, kd_l, a_l, b_l = dirs_l[d]
        y_c, s_ctx = rwkv7_scan(rc, dec_c, kd_c, vc, a_c, b_c, s0, reverse)
        y_l, _ = rwkv7_scan(rl, dec_l, kd_l, vl, a_l, b_l, s_ctx, reverse)
        ys_lat.append(y_l)
        ys_ctx.append(y_c)
    y_lat = rwkv7_output(ys_lat[0] + ys_lat[1], rl, kl, vl, gl, r_k, ln_w, ln_b)
    y_ctx = rwkv7_output(ys_ctx[0] + ys_ctx[1], rc, kc, vc, gc, r_k, ln_w, ln_b) if need_ctx else None
    return y_lat, y_ctx


def diff_attention(q, k, v, lam, scale):
    s = jnp.einsum('bqhnd,bkhnd->bhnqk', q, k).astype(jnp.float32) * scale
    p = jax.nn.softmax(s, axis=-1)
    p = p[:, :, 0] - lam * p[:, :, 1]
    return jnp.einsum('bhqk,bkhd->bqhd', p.astype(v.dtype), v)


def diff_mixer(z_lat, z_ctx, cos, sin, lam_p, subln, lam_init, need_ctx):
    B = z_lat.shape[0]

    def split_heads(z):
        T = z.shape[1]
        q, k, v = jnp.split(z, 3, axis=-1)
        return (q.reshape(B, T, DIFF_HEADS, 2, DIFF_QK), k.reshape(B, T, DIFF_HEADS, 2, DIFF_QK),
                v.reshape(B, T, DIFF_HEADS, DIFF_V))

    ql, kl, vl = split_heads(z_lat)
    qc, kc, vc = split_heads(z_ctx)
    ql = apply_axial_rope(ql, cos, sin)
    kl = apply_axial_rope(kl, cos, sin)
    lp = lam_p.astype(jnp.float32)
    lam = jnp.exp(jnp.sum(lp[0] * lp[1])) - jnp.exp(jnp.sum(lp[2] * lp[3])) + lam_init
    scale = DIFF_QK ** -0.5
    k_all = jnp.concatenate([kl, kc], axis=1)
    v_all = jnp.concatenate([vl, vc], axis=1)
    o_lat = map_query_blocks(lambda qi: diff_attention(qi, k_all, v_all, lam, scale), ql)

    def finish(o):
        return (rmsnorm(o, subln, DIFF_LN_EPS) * (1.0 - lam_init)).reshape(B, o.shape[1], GROUP_W)

    y_ctx = finish(diff_attention(qc, kc, vc, lam, scale)) if need_ctx else None
    return finish(o_lat), y_ctx


def conv_glu(h, w_up, conv_w, conv_b, w_down):
    a, b = jnp.split(h @ w_up, 2, axis=-1)
    a = conv_w[0] * shift_prev(a) + conv_w[1] * a + conv_w[2] * shift_next(a) + conv_b
    return (jax.nn.silu(a) * b) @ w_down


def setup_inputs(seed: int = 0) -> dict:
    key = jax.random.key(seed)
    keys = jax.random.split(key, 40)
    L, D = DEPTH, D_MODEL

    def nrm(i, shape, scale):
        return jax.random.normal(keys[i], shape, jnp.float32) * scale

    return {
        'x': nrm(0, (BATCH, SEQ, D), 1.0),
        'c': nrm(1, (BATCH, D), 1.0),
        'ctx': nrm(2, (BATCH, CTX_LEN, D), 1.0),
        'c_ctx': nrm(3, (D,), 1.0),
        'norm1_g': 1.0 + nrm(4, (L, D), 0.05),
        'norm2_g': 1.0 + nrm(5, (L, D), 0.05),
        'ada_w': nrm(6, (L, D, 6 * D), 0.3 * D ** -0.5),
        'ada_b': nrm(7, (L, 6 * D), 0.02),
        'w_in': nrm(8, (L, D, IN_COLS), D ** -0.5),
        'w_out': nrm(9, (L, MIX_W, D), MIX_W ** -0.5),
        'na_rpb': nrm(10, (L, NA_HEADS, 2 * NA_KH - 1, 2 * NA_KW - 1), 0.5),
        'mla_q_norm': 1.0 + nrm(11, (L, MLA_Q_RANK), 0.05),
        'mla_kv_norm': 1.0 + nrm(12, (L, MLA_KV_RANK), 0.05),
        'mla_w_uq': nrm(13, (L, MLA_Q_RANK, MLA_HEADS * (MLA_NOPE + ROPE_DIM)), MLA_Q_RANK ** -0.5),
        'mla_w_ukv': nrm(14, (L, MLA_KV_RANK, MLA_HEADS * (MLA_NOPE + MLA_V)), MLA_KV_RANK ** -0.5),
        'rwkv_mu': jax.random.uniform(keys[15], (L, 2, RWKV_COLS), jnp.float32, 0.0, 0.5),
        'rwkv_w0': -2.0 + nrm(16, (L, 2, GROUP_W), 0.5),
        'rwkv_w_up': nrm(17, (L, 2, RWKV_W_RANK, GROUP_W), RWKV_W_RANK ** -0.5),
        'rwkv_a0': nrm(18, (L, 2, GROUP_W), 0.5),
        'rwkv_a_up': nrm(19, (L, 2, RWKV_A_RANK, GROUP_W), RWKV_A_RANK ** -0.5),
        'rwkv_g_up': nrm(20, (L, RWKV_G_RANK, GROUP_W), RWKV_G_RANK ** -0.5),
        'rwkv_k_k': 0.85 + nrm(21, (L, GROUP_W), 0.05),
        'rwkv_k_a': 1.0 + nrm(22, (L, GROUP_W), 0.05),
        'rwkv_r_k': nrm(23, (L, RWKV_HEADS, RWKV_N), 0.1),
        'rwkv_ln_w': 1.0 + nrm(24, (L, GROUP_W), 0.05),
        'rwkv_ln_b': nrm(25, (L, GROUP_W), 0.02),
        'diff_lambda': nrm(26, (L, 4, DIFF_QK), 0.1),
        'diff_subln': 1.0 + nrm(27, (L, DIFF_V), 0.05),
        'mlp_w_up': nrm(28, (L, D, 2 * D_FF), D ** -0.5),
        'mlp_conv_w': nrm(29, (L, 3, D_FF), 3 ** -0.5),
        'mlp_conv_b': nrm(30, (L, D_FF), 0.02),
        'mlp_w_down': nrm(31, (L, D_FF, D), D_FF ** -0.5),
        'final_norm_g': 1.0 + nrm(32, (D,), 0.05),
    }


def reference(x, c, ctx, c_ctx, norm1_g, norm2_g, ada_w, ada_b, w_in, w_out, na_rpb,
              mla_q_norm, mla_kv_norm, mla_w_uq, mla_w_ukv,
              rwkv_mu, rwkv_w0, rwkv_w_up, rwkv_a0, rwkv_a_up, rwkv_g_up, rwkv_k_k, rwkv_k_a,
              rwkv_r_k, rwkv_ln_w, rwkv_ln_b, diff_lambda, diff_subln,
              mlp_w_up, mlp_conv_w, mlp_conv_b, mlp_w_down, final_norm_g):
    n_lat = x.shape[1]
    cos, sin = axial_rope_tables(n_lat, x.dtype)
    s_lat = jax.nn.silu(c)
    s_ctx = jax.nn.silu(c_ctx)[None]
    xl, xc = x, ctx
    for l in range(DEPTH):
        need_ctx = l < DEPTH - 1
        sh1, sc1, g1, sh2, sc2, g2 = jnp.split((s_lat @ ada_w[l] + ada_b[l])[:, None, :], 6, axis=-1)
        csh1, csc1, cg1, csh2, csc2, cg2 = jnp.split((s_ctx @ ada_w[l] + ada_b[l])[:, None, :], 6, axis=-1)
        h_lat = rmsnorm(xl, norm1_g[l]) * (1.0 + sc1) + sh1
        h_ctx = rmsnorm(xc, norm1_g[l]) * (1.0 + csc1) + csh1
        z_lat = jnp.split(h_lat @ w_in[l], IN_SPLITS, axis=-1)
        z_ctx = jnp.split(h_ctx @ w_in[l], IN_SPLITS, axis=-1)
        na_l, na_c = na_mixer(z_lat[0], z_ctx[0], na_rpb[l], need_ctx)
        mla_l, mla_c = mla_mixer(z_lat[1], z_ctx[1], cos, sin, mla_q_norm[l], mla_kv_norm[l],
                                 mla_w_uq[l], mla_w_ukv[l], need_ctx)
        rk_l, rk_c = rwkv7_mixer(z_lat[2], z_ctx[2], rwkv_mu[l], rwkv_w0[l], rwkv_w_up[l], rwkv_a0[l],
                                 rwkv_a_up[l], rwkv_g_up[l], rwkv_k_k[l], rwkv_k_a[l], rwkv_r_k[l],
                                 rwkv_ln_w[l], rwkv_ln_b[l], need_ctx)
        lam_init = 0.8 - 0.6 * math.exp(-0.3 * l)
        df_l, df_c = diff_mixer(z_lat[3], z_ctx[3], cos, sin, diff_lambda[l], diff_subln[l], lam_init, need_ctx)
        xl = xl + g1 * (jnp.concatenate([na_l, mla_l, rk_l, df_l], axis=-1) @ w_out[l])
        h2 = rmsnorm(xl, norm2_g[l]) * (1.0 + sc2) + sh2
        xl = xl + g2 * conv_glu(h2, mlp_w_up[l], mlp_conv_w[l], mlp_conv_b[l], mlp_w_down[l])
        if need_ctx:
            xc = xc + cg1 * (jnp.concatenate([na_c, mla_c, rk_c, df_c], axis=-1) @ w_out[l])
            hc2 = rmsnorm(xc, norm2_g[l]) * (1.0 + csc2) + csh2
            xc = xc + cg2 * conv_glu(hc2, mlp_w_up[l], mlp_conv_w[l], mlp_conv_b[l], mlp_w_down[l])
    return rmsnorm(xl, final_norm_g)
```

```python
import contextlib
import numpy as np
import concourse.bass as bass
import concourse.mybir as mybir
from concourse.bass_utils import run_bass_kernel_spmd

F32 = mybir.dt.float32
BF16 = mybir.dt.bfloat16
AF = mybir.ActivationFunctionType
ALU = mybir.AluOpType
AX = mybir.AxisListType

D = 1024
TC = 256
TL = 2048
T = TC + TL
NLAY = 4
DFF = 2816
INC = 2848
C_NA, C_MLA, C_RK, C_DF = 0, 768, 1184, 2080


class V:
    def __init__(self, ap, buf):
        self.ap, self.buf = ap, buf

    def __getitem__(self, idx):
        return V(self.ap[idx], self.buf)

    def f(self, fn):
        return V(fn(self.ap), self.buf)

    def r(self, pat, **kw):
        return V(self.ap.rearrange(pat, **kw), self.buf)

    def bc(self, axis, n):
        a = self.ap.unsqueeze(axis)
        sh = list(a.shape)
        sh[axis] = n
        return V(a.broadcast_to(sh), self.buf)


class Buf:
    def __init__(self, name, t):
        self.name, self.t = name, t
        self.lw = None
        self.rd = {}

    def __getitem__(self, idx):
        return V(self.t[idx], self)

    @property
    def v(self):
        return V(self.t[:], self)


def _ap(x):
    return x.ap if isinstance(x, V) else x


class FW:
    def __init__(self, nc, n_dma_sems=20):
        self.nc = nc
        self.engs = {}
        for name, h in (("pe", nc.tensor), ("act", nc.scalar), ("dve", nc.vector), ("pool", nc.gpsimd),
                        ("sp", nc.sync)):
            self.engs[name] = dict(name=name, h=h, sem=nc.alloc_semaphore("s_" + name), cnt=0, waited={})
        self.dq = {}
        for q in ("sp", "act", "pool"):
            sems = [nc.alloc_semaphore(f"d_{q}_{i}") for i in range(n_dma_sems)]
            self.dq[q] = dict(sems=sems, vals=[0] * n_dma_sems, nxt=0)
        self.ninst = 0
        self.nwait = 0
        self.stack = None

    def sb(self, name, shape, dt):
        self.nsb = getattr(self, "nsb", 0) + 1
        name = f"{name}_{self.nsb}"
        if self.stack is not None:
            t = self.stack.enter_context(self.nc.sbuf_tensor(name, list(shape), dt))
        else:
            t = self.nc.alloc_sbuf_tensor(name, list(shape), dt)
        return Buf(name, t)

    def ps(self, name, shape, dt=F32):
        return Buf(name, self.nc.alloc_psum_tensor(name, list(shape), dt))

    def dram(self, name, shape, dt, kind="Internal"):
        return Buf(name, self.nc.dram_tensor(name, list(shape), dt, kind=kind))

    @contextlib.contextmanager
    def phase(self):
        old = self.stack
        with contextlib.ExitStack() as st:
            self.stack = st
            yield
            self.barrier()
        self.stack = old

    def _wait(self, e, key, sem, val):
        if e["waited"].get(key, 0) >= val:
            return
        e["h"].wait_ge(sem, val)
        e["waited"][key] = val
        self.nwait += 1

    def _deps(self, e, reads, writes, mykey):
        deps = {}

        def add(k, s, v):
            if k not in deps or deps[k][1] < v:
                deps[k] = (s, v)
        for b in reads:
            if b.lw is not None:
                add(*b.lw)
        for b in writes:
            if b.lw is not None:
                add(*b.lw)
            for k, (s, v) in b.rd.items():
                add(k, s, v)
        need = []
        for k, (s, v) in deps.items():
            if k == mykey and e["name"] == "pe":
                continue
            if e["waited"].get(k, 0) >= v:
                continue
            need.append((k, s, v))
        for (k, s, v) in need[:-1]:
            self._wait(e, k, s, v)
        return need[-1] if need else None

    def _fold(self, e, ins, last):
        if last is None:
            return
        k, s, v = last
        ins.wait_op(s, v, "sem-ge")
        e["waited"][k] = v

    def _mark(self, reads, writes, key, sem, val):
        for b in reads:
            b.rd[key] = (sem, val)
        for b in writes:
            b.lw = (key, sem, val)
            b.rd = {}

    def op(self, eng, fn, reads, writes):
        e = self.engs[eng]
        reads = [x.buf for x in reads if isinstance(x, V)]
        writes = [x.buf for x in writes if isinstance(x, V)]
        last = self._deps(e, reads, writes, eng)
        ins = fn(e["h"])
        self._fold(e, ins, last)
        e["cnt"] += 1
        ins.then_inc(e["sem"], 1)
        self._mark(reads, writes, eng, e["sem"], e["cnt"])
        self.ninst += 1
        return ins

    def dma(self, q, out, in_, **kw):
        e = self.engs[q]
        d = self.dq[q]
        i = d["nxt"]
        d["nxt"] = (i + 1) % len(d["sems"])
        sem = d["sems"][i]
        key = f"dma_{q}_{i}"
        if d["vals"][i] > 0:
            self._wait(e, key, sem, d["vals"][i])
        reads, writes = [in_.buf], [out.buf]
        last = self._deps(e, reads, writes, key)
        ins = e["h"].dma_start(out=out.ap, in_=in_.ap, **kw)
        self._fold(e, ins, last)
        d["vals"][i] += 16
        ins.then_inc(sem, 16)
        self._mark(reads, writes, key, sem, d["vals"][i])
        self.ninst += 1
        return ins

    def barrier(self):
        sp = self.engs["sp"]
        for q, d in self.dq.items():
            for i, sem in enumerate(d["sems"]):
                if d["vals"][i] > 0:
                    self._wait(sp, f"dma_{q}_{i}", sem, d["vals"][i])
        for n in ("pe", "act", "dve", "pool"):
            o = self.engs[n]
            if o["cnt"] > 0:
                self._wait(sp, n, o["sem"], o["cnt"])
        sp["cnt"] += 1
        sp["h"].sem_inc(sp["sem"], 1)
        for n in ("pe", "act", "dve", "pool"):
            self._wait(self.engs[n], "sp", sp["sem"], sp["cnt"])
            for m in ("pe", "act", "dve", "pool"):
                self.engs[n]["waited"][m] = max(self.engs[n]["waited"].get(m, 0), self.engs[m]["cnt"])
            for q, d in self.dq.items():
                for i in range(len(d["sems"])):
                    k = f"dma_{q}_{i}"
                    self.engs[n]["waited"][k] = max(self.engs[n]["waited"].get(k, 0), d["vals"][i])

    def mm(self, out, lhsT, rhs, start=True, stop=True):
        return self.op("pe", lambda h: h.matmul(out.ap, lhsT=lhsT.ap, rhs=rhs.ap, start=start, stop=stop),
                       [lhsT, rhs], [out])

    def act(self, out, in_, func, scale=None, bias=None, accum_out=None, eng="act"):
        kw = {}
        if scale is not None:
            kw["scale"] = _ap(scale)
        if bias is not None:
            kw["bias"] = _ap(bias)
        if accum_out is not None:
            kw["accum_out"] = _ap(accum_out)
        return self.op("act", lambda h: h.activation(out=out.ap, in_=in_.ap, func=func, **kw),
                       [in_, scale, bias], [out, accum_out])

    def tt(self, out, in0, in1, op, eng="dve"):
        return self.op(eng, lambda h: h.tensor_tensor(out=out.ap, in0=in0.ap, in1=in1.ap, op=op), [in0, in1], [out])

    def ts(self, out, in0, s1, s2, op0, op1=None, eng="dve"):
        kw = dict(op1=op1) if op1 is not None else {}
        return self.op(eng, lambda h: h.tensor_scalar(out=out.ap, in0=in0.ap, scalar1=_ap(s1), scalar2=_ap(s2),
                                                      op0=op0, **kw), [in0, s1, s2], [out])

    def stt(self, out, in0, scalar, in1, op0, op1):
        return self.op("dve", lambda h: h.scalar_tensor_tensor(out=out.ap, in0=in0.ap, scalar=_ap(scalar),
                                                               in1=in1.ap, op0=op0, op1=op1),
                       [in0, scalar, in1], [out])

    def copy(self, out, in_, eng="dve"):
        if eng == "act":
            return self.act(out, in_, AF.Copy)
        return self.op(eng, lambda h: h.tensor_copy(out=out.ap, in_=in_.ap), [in_], [out])

    def memset(self, out, val, eng="dve"):
        return self.op(eng, lambda h: h.memset(out.ap, val), [], [out])

    def scan(self, out, d0, d1, op0=ALU.mult, op1=ALU.add):
        return self.op("dve", lambda h: h.tensor_tensor_scan(out=out.ap, data0=d0.ap, data1=d1.ap, initial=0.0,
                                                             op0=op0, op1=op1), [d0, d1], [out])

    def recip(self, out, in_):
        return self.op("dve", lambda h: h.reciprocal(out=out.ap, in_=in_.ap), [in_], [out])

    def reduce(self, out, in_, op=ALU.add, axis=AX.X):
        return self.op("dve", lambda h: h.tensor_reduce(out=out.ap, in_=in_.ap, axis=axis, op=op), [in_], [out])


def _rope_tables():
    t = np.arange(TL)
    row = (t // 64).astype(np.float32)
    col = (t % 64).astype(np.float32)
    freqs = (np.float32(10000.0) ** (-np.arange(0, 16, 2, dtype=np.float32) / np.float32(16))).astype(np.float32)
    ar = row[:, None] * freqs[None, :]
    ac = col[:, None] * freqs[None, :]
    ang = np.concatenate([ar, ar, ac, ac], axis=-1).astype(np.float32)
    return np.cos(ang).astype(np.float32).T.copy(), np.sin(ang).astype(np.float32).T.copy()


def _rot32():
    R = np.zeros((32, 32), np.float32)
    for m in range(8):
        R[8 + m, m] = -1.0
        R[m, 8 + m] = 1.0
        R[24 + m, 16 + m] = -1.0
        R[16 + m, 24 + m] = 1.0
    return R


def _consts():
    cos, sin = _rope_tables()
    c = {}
    c["cos128"] = np.tile(cos, (4, 1))
    c["sin128"] = np.tile(sin, (4, 1))
    c["cos96"] = np.concatenate([np.ones((64, TL), np.float32), cos], 0)
    c["sin96"] = np.concatenate([np.zeros((64, TL), np.float32), sin], 0)
    R = _rot32()
    R128 = np.zeros((128, 128), np.float32)
    for g in range(4):
        R128[g * 32:(g + 1) * 32, g * 32:(g + 1) * 32] = R
    R96 = np.zeros((96, 96), np.float32)
    R96[64:, 64:] = R
    c["R128"] = R128
    c["R96"] = R96
    c["ident"] = np.eye(128, dtype=np.float32)
    bo = np.zeros((128, 128), np.float32)
    bo[:64, :64] = 1.0
    bo[64:, 64:] = 1.0
    c["blockones"] = bo
    es = np.zeros((65, 64), np.float32)
    es[64, :] = 1.0
    c["esel"] = es
    cpos = np.arange(64)
    cstart = np.clip(cpos - 8, 0, 48)
    m = (cpos[None, :] >= cstart[:, None]) & (cpos[None, :] < cstart[:, None] + 16)
    c["na_mask"] = np.ascontiguousarray(m.T.astype(np.float32))
    s = np.arange(64)[:, None]
    t = np.arange(64)[None, :]
    mk = np.zeros((128, 2, 2, 64), np.float32)
    for hh in range(2):
        mk[hh * 64:(hh + 1) * 64, 0, 0] = (s < t)
        mk[hh * 64:(hh + 1) * 64, 0, 1] = (s <= t)
        mk[hh * 64:(hh + 1) * 64, 1, 0] = (s > t)
        mk[hh * 64:(hh + 1) * 64, 1, 1] = (s >= t)
    c["rk_mask"] = mk
    gmm = np.zeros((128, 4), np.float32)
    for g in range(4):
        gmm[g * 32:(g + 1) * 32, g] = 1.0
    c["df_gm"] = gmm
    c["rk_eye"] = np.concatenate([np.eye(64, dtype=np.float32)] * 2, 0)
    rm = np.ones((128, 2, 512), np.float32)
    rm[:, 0, 0::64] = 0.0
    rm[:, 1, 63::64] = 0.0
    c["rk_rm"] = rm
    return c


CONST_SHAPES = dict(cos128=(128, TL), sin128=(128, TL), cos96=(96, TL), sin96=(96, TL), R128=(128, 128),
                    R96=(96, 96), ident=(128, 128), blockones=(128, 128), esel=(65, 64), na_mask=(64, 64),
                    rk_mask=(128, 2, 2, 64), rk_eye=(128, 64), rk_rm=(128, 2, 512), df_gm=(128, 4))


def _fm(a, k):
    sh = a.shape[:-1]
    return np.ascontiguousarray(np.swapaxes(a.reshape(sh + (k, 128)), -1, -2))


def _layout_inputs(inp, nb, core):
    b0 = core * nb
    m = {}
    m["x"] = np.ascontiguousarray(inp["x"][b0:b0 + nb])
    m["ctx"] = np.ascontiguousarray(inp["ctx"][b0:b0 + nb])
    cv = np.concatenate([inp["c"][b0:b0 + nb], inp["c_ctx"][None]], 0)
    m["cvec"] = np.ascontiguousarray(cv.reshape(nb + 1, 8, 128).transpose(2, 1, 0))
    m["ada_w"] = inp["ada_w"]
    m["ada_bT"] = _fm(inp["ada_b"], 48)
    m["g1T"] = _fm(inp["norm1_g"], 8)
    m["g2T"] = _fm(inp["norm2_g"], 8)
    m["gfT"] = _fm(inp["final_norm_g"], 8)
    m["w_in"] = inp["w_in"]
    m["w_out"] = inp["w_out"]
    m["w_up"] = inp["mlp_w_up"]
    m["w_down"] = inp["mlp_w_down"]
    m["mla_qn"] = _fm(inp["mla_q_norm"], 2)
    m["mla_kvn"] = _fm(inp["mla_kv_norm"], 1)
    m["w_uq"] = inp["mla_w_uq"]
    m["w_ukv"] = inp["mla_w_ukv"]
    m["rk_mu"] = np.ascontiguousarray(inp["rwkv_mu"].reshape(NLAY, 2, 7, 128).transpose(0, 3, 1, 2))
    m["rk_w0"] = np.ascontiguousarray(inp["rwkv_w0"].reshape(NLAY, 2, 2, 128).transpose(0, 3, 1, 2))
    m["rk_a0"] = np.ascontiguousarray(inp["rwkv_a0"].reshape(NLAY, 2, 2, 128).transpose(0, 3, 1, 2))
    m["rk_wup"] = inp["rwkv_w_up"]
    m["rk_aup"] = inp["rwkv_a_up"]
    m["rk_gup"] = inp["rwkv_g_up"]
    m["rk_kk"] = _fm(inp["rwkv_k_k"], 2)
    m["rk_ka"] = _fm(inp["rwkv_k_a"], 2)
    m["rk_rk"] = _fm(inp["rwkv_r_k"].reshape(NLAY, 256), 2)
    m["rk_lnw"] = inp["rwkv_ln_w"]
    m["rk_lnb"] = inp["rwkv_ln_b"]
    m["df_lam"] = inp["diff_lambda"].reshape(NLAY, 1, 128)
    m["df_sub"] = np.ascontiguousarray(inp["diff_subln"].reshape(NLAY, 64, 1))
    m["conv_w"] = np.ascontiguousarray(inp["mlp_conv_w"].reshape(NLAY, 3, 22, 128).transpose(0, 3, 1, 2))
    m["conv_b"] = _fm(inp["mlp_conv_b"], 22)
    rpb = inp["na_rpb"]
    kc = np.arange(64)[:, None]
    qc = np.arange(64)[None, :]
    cidx = np.clip(kc - qc + 15, 0, 30)
    g = rpb[:, :, ::-1, :][:, :, :, cidx]
    m["na_bias"] = np.ascontiguousarray(g.transpose(0, 3, 1, 2, 4))
    for k, v in _consts().items():
        m["c_" + k] = v
    return {k: np.ascontiguousarray(v) for k, v in m.items()}


class NS:
    pass


IN_SHAPES = None


def build_program(nb, nlay=NLAY, dbg=(), parts=("na", "mla", "rk", "df", "mlp"), upto="end"):
    nc = bass.Bass("TRN2", target_bir_lowering=False)
    fw = FW(nc)
    S = NS()
    S.nc, S.fw, S.nb, S.nlay, S.dbg, S.parts = nc, fw, nb, nlay, set(dbg), parts
    NV = nb + 1
    S.NV = NV

    def din(name, shape, dt=F32):
        return fw.dram(name, shape, dt, kind="ExternalInput")

    I = NS()
    S.I = I
    I.x = din("x", [nb, TL, D])
    I.ctx = din("ctx", [nb, TC, D])
    I.cvec = din("cvec", [128, 8, NV])
    I.ada_w = din("ada_w", [NLAY, D, 6 * D])
    I.ada_bT = din("ada_bT", [NLAY, 128, 48])
    I.g1T = din("g1T", [NLAY, 128, 8])
    I.g2T = din("g2T", [NLAY, 128, 8])
    I.gfT = din("gfT", [128, 8])
    I.w_in = din("w_in", [NLAY, D, INC])
    I.w_out = din("w_out", [NLAY, D, D])
    I.w_up = din("w_up", [NLAY, D, 2 * DFF])
    I.w_down = din("w_down", [NLAY, DFF, D])
    I.mla_qn = din("mla_qn", [NLAY, 128, 2])
    I.mla_kvn = din("mla_kvn", [NLAY, 128, 1])
    I.w_uq = din("w_uq", [NLAY, 256, 384])
    I.w_ukv = din("w_ukv", [NLAY, 128, 512])
    I.rk_mu = din("rk_mu", [NLAY, 128, 2, 7])
    I.rk_w0 = din("rk_w0", [NLAY, 128, 2, 2])
    I.rk_a0 = din("rk_a0", [NLAY, 128, 2, 2])
    I.rk_wup = din("rk_wup", [NLAY, 2, 32, 256])
    I.rk_aup = din("rk_aup", [NLAY, 2, 32, 256])
    I.rk_gup = din("rk_gup", [NLAY, 64, 256])
    I.rk_kk = din("rk_kk", [NLAY, 128, 2])
    I.rk_ka = din("rk_ka", [NLAY, 128, 2])
    I.rk_rk = din("rk_rk", [NLAY, 128, 2])
    I.rk_lnw = din("rk_lnw", [NLAY, 256])
    I.rk_lnb = din("rk_lnb", [NLAY, 256])
    I.df_lam = din("df_lam", [NLAY, 1, 128])
    I.df_sub = din("df_sub", [NLAY, 64, 1])
    I.conv_w = din("conv_w", [NLAY, 128, 3, 22])
    I.conv_b = din("conv_b", [NLAY, 128, 22])
    I.na_bias = din("na_bias", [NLAY, 64, 4, 15, 64])
    I.c = {}
    for k, sh in CONST_SHAPES.items():
        I.c[k] = din("c_" + k, list(sh))
    S.out = fw.dram("out", [nb, TL, D], F32, kind="ExternalOutput")

    def scratch(name, shape, dt):
        return fw.dram(name, shape, dt, kind="ExternalOutput" if name in S.dbg else "Internal")
    S.scratch = scratch
    S.w_in_bf = scratch("w_in_bf", [NLAY, D, INC], BF16)
    S.w_out_bf = scratch("w_out_bf", [NLAY, D, D], BF16)
    S.w_up_bf = scratch("w_up_bf", [NLAY, 22, 128, 2, 8, 128], BF16)
    S.w_down_bf = scratch("w_down_bf", [NLAY, 8, 128, 22, 128], BF16)
    S.w_uq_bf = scratch("w_uq_bf", [NLAY, 256, 384], BF16)
    S.w_ukv_bf = scratch("w_ukv_bf", [NLAY, 128, 512], BF16)
    S.XT = scratch("XT", [nb, 128, 8, T], F32)
    S.QK_na = scratch("QK_na", [128, 4, T], BF16)
    S.V_na = scratch("V_na", [18, 128, 4, 128], BF16)
    S.QT_mla = scratch("QT_mla", [96, 4, T], BF16)
    S.KT_mla = scratch("KT_mla", [96, 4, T], BF16)
    S.V_mla = scratch("V_mla", [18, 128, 4, 128], BF16)
    S.QK_df = scratch("QK_df", [128, 4, T], BF16)
    S.V_df = scratch("V_df", [18, 128, 4, 128], BF16)
    S.ZT_rk = scratch("ZT_rk", [128, 7, T], F32)
    S.MIXT = scratch("MIXT", [128, 8, T], BF16)

    S.P = [fw.ps(f"P{i}", [128, 512]) for i in range(8)]

    G = NS()
    S.G = G
    G.ident = fw.sb("ident", [128, 128], F32)
    G.ones_bf = fw.sb("ones_bf", [128, 128], BF16)
    G.blockones = fw.sb("blockones", [128, 128], F32)
    G.esel = fw.sb("esel", [65, 64], F32)
    G.mods = fw.sb("mods", [128, nlay, 48, NV], F32)
    G.A = fw.sb("Amod", [128, nlay, 2, 8, NV], F32)
    fw.dma("sp", G.ident.v, I.c["ident"].v)
    fw.dma("sp", G.blockones.v, I.c["blockones"].v)
    fw.dma("sp", G.esel.v, I.c["esel"].v)
    fw.memset(G.ones_bf.v, 1.0)

    phase_prep(S)
    phase_mod(S)
    for b in range(nb):
        for l in range(nlay):
            need_ctx = l < NLAY - 1
            phase_A(S, b, l)
            if upto == "A":
                continue
            if "mla" in parts:
                phase_mla(S, b, l, need_ctx)
            if "df" in parts:
                phase_df(S, b, l, need_ctx)
            if "na" in parts:
                phase_na(S, b, l, need_ctx)
            if "rk" in parts:
                phase_rk(S, b, l, need_ctx)
            if upto == "att":
                continue
            phase_O(S, b, l, need_ctx)
            if upto == "O":
                continue
            if "mlp" in parts:
                phase_mlp(S, b, l, need_ctx)
    if upto == "end":
        phase_final(S)
    fw.barrier()
    return nc, fw


def phase_prep(S):
    fw, I, nb = S.fw, S.I, S.nb
    for l in range(S.nlay):
        for j in range(22):
            for ab in range(2):
                fw.dma("pool", S.w_up_bf[l, j, :, ab, :, :],
                       I.w_up[l, :, ab * DFF + j * 128:ab * DFF + (j + 1) * 128].r("(k p) n -> p k n", p=128))
        for m in range(8):
            fw.dma("pool", S.w_down_bf[l, m], I.w_down[l, :, m * 128:(m + 1) * 128].r("(j p) n -> p j n", p=128))
        for src, dst, rows, cols in ((I.w_in, S.w_in_bf, D, INC), (I.w_out, S.w_out_bf, D, D),
                                     (I.w_uq, S.w_uq_bf, 256, 384), (I.w_ukv, S.w_ukv_bf, 128, 512)):
            r0 = 0
            while r0 < rows:
                nr = min(512, rows - r0)
                fw.dma("pool", dst[l, r0:r0 + nr, :], src[l, r0:r0 + nr, :], max_dma_last_dim=4096)
                r0 += nr
    with fw.phase():
        xin = [fw.sb(f"xin{i}", [128, D], F32) for i in range(2)]
        xo = [fw.sb(f"xo{i}", [128, 8, 128], F32) for i in range(2)]
        it = 0
        for b in range(nb):
            for tt_ in range(T // 128):
                xi, xoo = xin[it % 2], xo[it % 2]
                if tt_ < 2:
                    src = I.ctx[b, tt_ * 128:(tt_ + 1) * 128, :]
                else:
                    src = I.x[b, (tt_ - 2) * 128:(tt_ - 1) * 128, :]
                fw.dma("sp" if it % 2 == 0 else "act", xi.v, src)
                for half in range(2):
                    ps = S.P[(it * 2 + half) % 4]
                    for kk in range(4):
                        k = half * 4 + kk
                        fw.op("pe", lambda h: h.transpose(ps.v[:, kk * 128:(kk + 1) * 128].ap,
                                                           xi.v[:, k * 128:(k + 1) * 128].ap, S.G.ident.v.ap),
                              [xi.v, S.G.ident.v], [ps.v])
                    dst = xoo.v[:, half * 4:(half + 1) * 4, :]
                    srcp = ps.v.r("p (k t) -> p k t", k=4)
                    if half == 0:
                        fw.copy(dst, srcp, "dve")
                    else:
                        fw.copy(dst, srcp, "act")
                fw.dma("sp", S.XT[b, :, :, tt_ * 128:(tt_ + 1) * 128], xoo.v)
                it += 1


def phase_mod(S):
    fw, I, G, NV = S.fw, S.I, S.G, S.NV
    with fw.phase():
        cv = fw.sb("cv", [128, 8, NV], F32)
        sv = fw.sb("sv", [128, 8, NV], F32)
        fw.dma("sp", cv.v, I.cvec.v)
        fw.act(sv.v, cv.v, AF.Silu)
        wb = [fw.sb(f"adaw{i}", [128, 8, 768], F32) for i in range(2)]
        for l in range(S.nlay):
            abT = fw.sb(f"abT{l}", [128, 48], F32)
            fw.dma("act", abT.v, I.ada_bT[l])
            gT = fw.sb(f"gT{l}", [128, 2, 8], F32)
            fw.dma("act", gT.v[:, 0, :], I.g1T[l])
            fw.dma("act", gT.v[:, 1, :], I.g2T[l])
            ps = S.P[4 + (l % 2)]
            for blk in range(8):
                w = wb[(l * 8 + blk) % 2]
                fw.dma("sp" if blk % 2 == 0 else "act", w.v,
                       I.ada_w[l, :, blk * 768:(blk + 1) * 768].r("(k p) n -> p k n", p=128))
                for cc in range(6):
                    ch = blk * 6 + cc
                    for k in range(8):
                        fw.mm(ps.v[:, ch * NV:(ch + 1) * NV], w.v[:, k, cc * 128:(cc + 1) * 128], sv.v[:, k, :],
                              start=(k == 0), stop=(k == 7))
            fw.tt(G.mods.v[:, l], ps.v[:, 0:48 * NV].r("p (c v) -> p c v", v=NV), abT.v.bc(2, NV), ALU.add)
            for n, c0 in ((0, 8), (1, 32)):
                tmp = fw.sb(f"modtmp{l}{n}", [128, 8, NV], F32)
                fw.ts(tmp.v, G.mods.v[:, l, c0:c0 + 8, :], 1.0, None, ALU.add)
                fw.tt(G.A.v[:, l, n], tmp.v, gT.v[:, n, :].bc(2, NV), ALU.mult)


TOK_TILES = [(0, 256), (256, 512), (768, 512), (1280, 512), (1792, 512)]
EPS = 1e-6


def nps(S):
    S.pi = getattr(S, "pi", 0) + 1
    return S.P[S.pi % 8]


def rstd_from_sum(S, out, ssum, n, eps, tmp):
    fw = S.fw
    fw.act(tmp, ssum, AF.Ln, scale=1.0 / n, bias=float(eps))
    fw.act(out, tmp, AF.Exp, scale=-0.5)


def phase_A(S, b, l):
    fw, I, G = S.fw, S.I, S.G
    with fw.phase():
        win = fw.sb("win", [128, 8, INC], BF16)
        for k in range(8):
            fw.dma("sp" if k % 2 == 0 else "act", win.v[:, k, :], S.w_in_bf[l, k * 128:(k + 1) * 128, :])
        wuq = fw.sb("wuq", [128, 2, 384], BF16)
        fw.dma("sp", wuq.v, S.w_uq_bf[l].r("(k p) n -> p k n", p=128))
        wukv = fw.sb("wukv", [128, 512], BF16)
        fw.dma("act", wukv.v, S.w_ukv_bf[l])
        qn = fw.sb("qn", [128, 2], F32)
        kvn = fw.sb("kvn", [128, 1], F32)
        fw.dma("sp", qn.v, I.mla_qn[l])
        fw.dma("sp", kvn.v, I.mla_kvn[l])
        tabs = {}
        for nm, rows in (("cos128", 128), ("sin128", 128), ("cos96", 96), ("sin96", 96)):
            tabs[nm] = fw.sb(nm, [rows, TL], F32)
            fw.dma("act", tabs[nm].v, I.c[nm].v)
        R128 = fw.sb("R128", [128, 128], F32)
        R96 = fw.sb("R96", [96, 96], F32)
        fw.dma("sp", R128.v, I.c["R128"].v)
        fw.dma("sp", R96.v, I.c["R96"].v)

        xt = fw.sb("xt", [128, 8, 512], F32)
        sq = fw.sb("sq", [128, 8, 512], BF16)
        xn = fw.sb("xn", [128, 8, 512], F32)
        hT = fw.sb("hT", [128, 8, 512], BF16)
        rstd = fw.sb("rstd", [128, 512], F32)
        lnt = fw.sb("lnt", [128, 512], F32)
        st_na = fw.sb("st_na", [128, 4, 512], BF16)
        st_df = fw.sb("st_df", [128, 4, 512], BF16)
        st_rk = fw.sb("st_rk", [128, 7, 512], F32)
        vts = [fw.sb(f"vt{i}", [128, 4, 128], BF16) for i in range(6)]
        for vt in vts:
            fw.memset(vt.v, 1.0, "pool")
        cq = fw.sb("cq", [128, 3, 512], F32)
        sqm = fw.sb("sqm", [128, 3, 512], BF16)
        rq = fw.sb("rq", [128, 2, 512], F32)
        cqn = fw.sb("cqn", [128, 3, 512], BF16)
        qst = fw.sb("qst", [96, 4, 512], BF16)
        kst = fw.sb("kst", [96, 4, 512], BF16)
        xsb = [fw.sb(f"xsb{i}", [128, 512], F32) for i in range(2)]
        t1s = [fw.sb(f"t1s{i}", [128, 512], F32) for i in range(2)]
        t2s = [fw.sb(f"t2s{i}", [128, 512], F32) for i in range(2)]
        krt = fw.sb("krt", [96, 512], BF16)
        S.rc = 0
        S.vc = 0

        def rope(psv, M, N, tl0, Rm, cosn, sinn, outv):
            i = S.rc % 2
            S.rc += 1
            x, t1, t2 = xsb[i].v[0:M, 0:N], t1s[i].v[0:M, 0:N], t2s[i].v[0:M, 0:N]
            fw.copy(x, psv, "act")
            rp = nps(S).v[0:M, 0:N]
            fw.mm(rp, Rm.v[0:M, 0:M], x)
            fw.tt(t1, x, tabs[cosn].v[0:M, tl0:tl0 + N], ALU.mult)
            fw.tt(t2, rp, tabs[sinn].v[0:M, tl0:tl0 + N], ALU.mult)
            fw.tt(outv, t1, t2, ALU.add, eng="pool")

        for (t0, N) in TOK_TILES:
            lat = t0 >= TC
            vec = b if lat else S.nb
            tl0 = t0 - TC
            fw.dma("sp", xt.v[:, :, 0:N], S.XT[b, :, :, t0:t0 + N])
            fw.act(sq.v[:, :, 0:N], xt.v[:, :, 0:N], AF.Square)
            ps = nps(S)
            for k in range(8):
                fw.mm(ps.v[:, 0:N], G.ones_bf.v, sq.v[:, k, 0:N], start=(k == 0), stop=(k == 7))
            rstd_from_sum(S, rstd.v[:, 0:N], ps.v[:, 0:N], 1024.0, EPS, lnt.v[:, 0:N])
            fw.tt(xn.v[:, :, 0:N], xt.v[:, :, 0:N], rstd.v[:, 0:N].bc(1, 8), ALU.mult)
            for k in range(8):
                fw.act(hT.v[:, k, 0:N], xn.v[:, k, 0:N], AF.Identity, scale=G.A.v[:, l, 0, k, vec:vec + 1],
                       bias=G.mods.v[:, l, 0 + k, vec:vec + 1])

            def zchunk(c0, M):
                p = nps(S)
                for k in range(8):
                    fw.mm(p.v[0:M, 0:N], win.v[:, k, c0:c0 + M], hT.v[:, k, 0:N], start=(k == 0), stop=(k == 7))
                return p.v[0:M, 0:N]

            def vproj(c0, dst):
                for sub in range(N // 128):
                    p = nps(S)
                    for k in range(8):
                        fw.mm(p.v[:, 0:256], hT.v[:, k, sub * 128:(sub + 1) * 128], win.v[:, k, c0:c0 + 256],
                              start=(k == 0), stop=(k == 7))
                    vt = vts[S.vc % 6]
                    S.vc += 1
                    fw.copy(vt.v[:, :, 0:64], p.v[:, 0:256].r("p (h c) -> p h c", h=4), "dve")
                    fw.dma("act", dst[(t0 + sub * 128) // 128], vt.v)

            for c in range(4):
                p = zchunk(C_NA + c * 128, 128)
                fw.copy(st_na.v[:, c, 0:N], p, "act" if c % 2 else "dve")
            fw.dma("sp", S.QK_na[:, :, t0:t0 + N], st_na.v[:, :, 0:N])
            vproj(C_NA + 512, S.V_na)
            for c in range(4):
                p = zchunk(C_DF + c * 128, 128)
                if lat:
                    rope(p, 128, N, tl0, R128, "cos128", "sin128", st_df.v[:, c, 0:N])
                else:
                    fw.copy(st_df.v[:, c, 0:N], p, "act" if c % 2 else "dve")
            fw.dma("sp", S.QK_df[:, :, t0:t0 + N], st_df.v[:, :, 0:N])
            vproj(C_DF + 512, S.V_df)
            for c in range(7):
                p = zchunk(C_RK + c * 128, 128)
                fw.copy(st_rk.v[:, c, 0:N], p, "act" if c % 2 else "dve")
            fw.dma("sp", S.ZT_rk[:, :, t0:t0 + N], st_rk.v[:, :, 0:N])
            for c in range(3):
                p = zchunk(C_MLA + c * 128, 128)
                fw.copy(cq.v[:, c, 0:N], p, "act" if c % 2 else "dve")
            fw.act(sqm.v[:, :, 0:N], cq.v[:, :, 0:N], AF.Square)
            pq, pk = nps(S), nps(S)
            fw.mm(pq.v[:, 0:N], G.ones_bf.v, sqm.v[:, 0, 0:N], start=True, stop=False)
            fw.mm(pq.v[:, 0:N], G.ones_bf.v, sqm.v[:, 1, 0:N], start=False, stop=True)
            fw.mm(pk.v[:, 0:N], G.ones_bf.v, sqm.v[:, 2, 0:N])
            rstd_from_sum(S, rq.v[:, 0, 0:N], pq.v[:, 0:N], 256.0, EPS, lnt.v[:, 0:N])
            rstd_from_sum(S, rq.v[:, 1, 0:N], pk.v[:, 0:N], 128.0, EPS, lnt.v[:, 0:N])
            for j in range(3):
                sc = qn.v[:, j:j + 1] if j < 2 else kvn.v[:, 0:1]
                rr = rq.v[:, 0, 0:N] if j < 2 else rq.v[:, 1, 0:N]
                fw.stt(cqn.v[:, j, 0:N], cq.v[:, j, 0:N], sc, rr, ALU.mult, ALU.mult)
            for h in range(4):
                p = nps(S).v[0:96, 0:N]
                for j in range(2):
                    fw.mm(p, wuq.v[:, j, h * 96:(h + 1) * 96], cqn.v[:, j, 0:N], start=(j == 0), stop=(j == 1))
                if lat:
                    rope(p, 96, N, tl0, R96, "cos96", "sin96", qst.v[:, h, 0:N])
                else:
                    fw.copy(qst.v[:, h, 0:N], p, "act")
                p2 = nps(S).v[0:64, 0:N]
                fw.mm(p2, wukv.v[:, h * 128:h * 128 + 64], cqn.v[:, 2, 0:N])
                fw.copy(kst.v[0:64, h, 0:N], p2, "dve")
            p = zchunk(C_MLA + 320, 96)
            if lat:
                rope(p, 96, N, tl0, R96, "cos96", "sin96", krt.v[:, 0:N])
            else:
                fw.copy(krt.v[:, 0:N], p, "act")
            for h in range(4):
                fw.copy(kst.v[64:96, h, 0:N], krt.v[64:96, 0:N], "pool")
            fw.dma("sp", S.QT_mla[:, :, t0:t0 + N], qst.v[:, :, 0:N])
            fw.dma("sp", S.KT_mla[:, :, t0:t0 + N], kst.v[:, :, 0:N])
            for sub in range(N // 128):
                pv = nps(S)
                fw.mm(pv.v[:, 0:256].r("p (h c) -> p h c", h=4), cqn.v[:, 2, sub * 128:(sub + 1) * 128],
                      wukv.v.r("p (h c) -> p h c", h=4)[:, :, 64:128])
                vt = vts[S.vc % 6]
                S.vc += 1
                fw.copy(vt.v[:, :, 0:64], pv.v[:, 0:256].r("p (h c) -> p h c", h=4), "dve")
                fw.dma("act", S.V_mla[(t0 + sub * 128) // 128], vt.v)


def _qtiles(need_ctx):
    qt = [(TC + i * 512, 512, 18) for i in range(4)]
    if need_ctx:
        qt.append((0, 256, 2))
    return qt


def _normalize(S, fw, O, NQ, osb, lnr, rec, outv, bcp, eng="pool"):
    G = S.G
    fw.copy(osb.v[0:65, 0:NQ], O.v[0:65, 0:NQ], "dve")
    fw.mm(bcp.v[0:64, 0:NQ], G.esel.v[0:65, 0:64], osb.v[0:65, 0:NQ])
    fw.act(lnr.v[0:64, 0:NQ], bcp.v[0:64, 0:NQ], AF.Ln)
    fw.act(rec.v[0:64, 0:NQ], lnr.v[0:64, 0:NQ], AF.Exp, scale=-1.0)
    fw.tt(outv, osb.v[0:64, 0:NQ], rec.v[0:64, 0:NQ], ALU.mult, eng=eng)


def phase_mla(S, b, l, need_ctx):
    fw, I, G, P = S.fw, S.I, S.G, S.P
    scale = 96.0 ** -0.5
    with fw.phase():
        KT = fw.sb("KT", [128, 4, T], BF16)
        QT = fw.sb("QT", [128, 4, T], BF16)
        Vv = fw.sb("Vv", [128, 18, 4, 128], BF16)
        fw.memset(KT.v, 0.0)
        fw.memset(QT.v, 0.0, "pool")
        fw.dma("sp", KT.v[0:96], S.KT_mla.v)
        fw.dma("act", QT.v[0:96], S.QT_mla.v)
        fw.dma("sp", Vv.v, S.V_mla.v.r("j p h c -> p j h c"))
        pts = [fw.sb(f"pt{i}", [128, 512], BF16) for i in range(3)]
        osb = fw.sb("osb", [65, 512], F32)
        lnr = fw.sb("lnr", [64, 512], F32)
        rec = fw.sb("rec", [64, 512], F32)
        osts = [fw.sb(f"ost{i}", [64, 4, 512], BF16) for i in range(2)]
        cnt = 0
        for qi, (q0, NQ, nk) in enumerate(_qtiles(need_ctx)):
            ost = osts[qi % 2]
            for h in range(4):
                O = P[4 + (h % 2)]
                for j in range(nk):
                    sp = P[cnt % 4]
                    pt = pts[cnt % 3]
                    cnt += 1
                    fw.mm(sp.v[:, 0:NQ], KT.v[:, h, j * 128:(j + 1) * 128], QT.v[:, h, q0:q0 + NQ])
                    fw.act(pt.v[:, 0:NQ], sp.v[:, 0:NQ], AF.Exp, scale=scale)
                    fw.mm(O.v[:, 0:NQ], Vv.v[:, j, h, :], pt.v[:, 0:NQ], start=(j == 0), stop=(j == nk - 1))
                _normalize(S, fw, O, NQ, osb, lnr, rec, ost.v[:, h, 0:NQ], P[6 + (h % 2)])
            for h in range(4):
                hb = (h % 2) * 64
                fw.dma("sp" if h % 2 else "act", S.MIXT[hb:hb + 64, 2 + h // 2, q0:q0 + NQ], ost.v[:, h, 0:NQ])


def phase_df(S, b, l, need_ctx):
    fw, I, G, P = S.fw, S.I, S.G, S.P
    import math
    scale = 32.0 ** -0.5
    lam_init = 0.8 - 0.6 * math.exp(-0.3 * l)
    with fw.phase():
        Kd = fw.sb("Kd", [128, 2, T], BF16)
        Qd = fw.sb("Qd", [128, 2, T], BF16)
        fw.dma("sp", Qd.v, S.QK_df[:, 0:2, :])
        fw.dma("act", Kd.v, S.QK_df[:, 2:4, :])
        gm = fw.sb("gm", [128, 4], F32)
        fw.dma("sp", gm.v, I.c["df_gm"].v)
        Qm = fw.sb("Qm", [128, 2, 4, T], BF16)
        for c in range(2):
            for g in range(4):
                if g % 2:
                    fw.ts(Qm.v[:, c, g, :], Qd.v[:, c, :], gm.v[:, g:g + 1], None, ALU.mult)
                else:
                    fw.act(Qm.v[:, c, g, :], Qd.v[:, c, :], AF.Identity, scale=gm.v[:, g:g + 1])
        Vv = fw.sb("Vv", [128, 18, 4, 128], BF16)
        fw.dma("sp", Vv.v, S.V_df.v.r("j p h c -> p j h c"))
        dl = fw.sb("dl", [64, 128], F32)
        fw.dma("act", dl.v, I.df_lam[l].f(lambda a: a.broadcast_to([64, 128])))
        pr = fw.sb("pr", [64, 2, 32], F32)
        fw.tt(pr.v, dl.v.r("p (a b d) -> p a b d", a=2, b=2)[:, :, 0, :], dl.v.r("p (a b d) -> p a b d", a=2, b=2)[:, :, 1, :],
              ALU.mult)
        ss = fw.sb("ss", [64, 2], F32)
        fw.reduce(ss.v, pr.v)
        ee = fw.sb("ee", [64, 2], F32)
        fw.act(ee.v, ss.v, AF.Exp)
        nlam = fw.sb("nlam", [64, 1], F32)
        fw.tt(nlam.v, ee.v[:, 1:2], ee.v[:, 0:1], ALU.subtract)
        fw.ts(nlam.v, nlam.v, -lam_init, None, ALU.add)
        sub = fw.sb("sub", [64, 1], F32)
        fw.dma("act", sub.v, I.df_sub[l])
        fw.ts(sub.v, sub.v, 1.0 - lam_init, None, ALU.mult)
        pts = [fw.sb(f"pt{i}", [128, 512], BF16) for i in range(3)]
        osb = [fw.sb(f"osb{i}", [65, 512], F32) for i in range(2)]
        lnr = fw.sb("lnr", [64, 512], F32)
        rec = fw.sb("rec", [64, 512], F32)
        o01 = [fw.sb(f"o01{i}", [64, 512], F32) for i in range(2)]
        oo = fw.sb("oo", [64, 512], F32)
        sqo = fw.sb("sqo", [64, 512], F32)
        osts = [fw.sb(f"ost{i}", [64, 4, 512], BF16) for i in range(2)]
        cnt = 0
        for qi, (q0, NQ, nk) in enumerate(_qtiles(need_ctx)):
            ost = osts[qi % 2]
            for h in range(4):
                c = h // 2
                for j in range(nk):
                    for n in range(2):
                        g = (h % 2) * 2 + n
                        sp = P[cnt % 4]
                        pt = pts[cnt % 3]
                        cnt += 1
                        fw.mm(sp.v[:, 0:NQ], Kd.v[:, c, j * 128:(j + 1) * 128], Qm.v[:, c, g, q0:q0 + NQ])
                        fw.act(pt.v[:, 0:NQ], sp.v[:, 0:NQ], AF.Exp, scale=scale)
                        fw.mm(P[4 + n].v[:, 0:NQ], Vv.v[:, j, h, :], pt.v[:, 0:NQ], start=(j == 0), stop=(j == nk - 1))
                for n in range(2):
                    _normalize(S, fw, P[4 + n], NQ, osb[n], lnr, rec, o01[n].v[:, 0:NQ], P[6 + n], eng="dve")
                fw.stt(oo.v[:, 0:NQ], o01[1].v[:, 0:NQ], nlam.v[:, 0:1], o01[0].v[:, 0:NQ], ALU.mult, ALU.add)
                fw.act(sqo.v[:, 0:NQ], oo.v[:, 0:NQ], AF.Square)
                fw.mm(P[6].v[0:64, 0:NQ], G.blockones.v[0:64, 0:64], sqo.v[:, 0:NQ])
                rstd_from_sum(S, rec.v[:, 0:NQ], P[6].v[0:64, 0:NQ], 64.0, 1e-5, lnr.v[:, 0:NQ])
                fw.stt(ost.v[:, h, 0:NQ], oo.v[:, 0:NQ], sub.v[:, 0:1], rec.v[:, 0:NQ], ALU.mult, ALU.mult)
            for h in range(4):
                hb = (h % 2) * 64
                fw.dma("sp" if h % 2 else "act", S.MIXT[hb:hb + 64, 6 + h // 2, q0:q0 + NQ], ost.v[:, h, 0:NQ])


def _na_rs(r):
    return min(max(r - 4, 0), 24)


def phase_na(S, b, l, need_ctx):
    fw, I, G, P = S.fw, S.I, S.G, S.P
    scale = 0.125
    with fw.phase():
        QK = fw.sb("QKn", [128, 4, T], BF16)
        fw.dma("sp", QK.v, S.QK_na.v)
        Vv = fw.sb("Vv", [128, 18, 4, 128], BF16)
        fw.dma("act", Vv.v, S.V_na.v.r("j p h c -> p j h c"))
        E = fw.sb("E", [128, 4, 15, 64], F32)
        msk = fw.sb("msk", [128, 64], F32)
        for hh in range(2):
            fw.dma("sp", E.v[hh * 64:(hh + 1) * 64], I.na_bias[l])
            fw.dma("act", msk.v[hh * 64:(hh + 1) * 64], I.c["na_mask"].v)
        Ef = E.v.r("p h d q -> p (h d) q")
        fw.act(Ef, Ef, AF.Exp)
        fw.tt(Ef, Ef, msk.v.bc(1, 60), ALU.mult)
        pts = [fw.sb(f"pt{i}", [128, 512], BF16) for i in range(3)]
        ptf = [fw.sb(f"ptf{i}", [128, 512], F32) for i in range(2)]
        osb = fw.sb("osb", [65, 512], F32)
        osm = fw.sb("osm", [65, 512], F32)
        lnr = fw.sb("lnr", [64, 512], F32)
        rec = fw.sb("rec", [64, 512], F32)
        osts = [fw.sb(f"ost{i}", [64, 4, 512], BF16) for i in range(2)]
        cnt = 0
        qtiles = [(i * 8, TC + i * 512, 512) for i in range(4)]
        if need_ctx:
            qtiles.append((None, 0, 256))
        for qi, (r0, q0, NQ) in enumerate(qtiles):
            ost = osts[qi % 2]
            work = []
            for i in range(4):
                work.append(((i % 2) * 64, i // 2, i * 64, 0, NQ, None))
            if r0 is not None:
                for kr in range(_na_rs(r0), _na_rs(r0 + 7) + 8):
                    rr = [r for r in range(r0, r0 + 8) if _na_rs(r) <= kr < _na_rs(r) + 8]
                    if not rr:
                        continue
                    ra, rb = rr[0], rr[-1]
                    work.append(((kr % 2) * 64, 2 + kr // 2, TC + kr * 64, (ra - r0) * 64, (rb - r0 + 1) * 64,
                                 (ra - kr + 7, rb - kr + 8)))
            last = {}
            for wi, w in enumerate(work):
                last[w[0]] = wi
            for h in range(4):
                hb = (h % 2) * 64
                cq, ck = h // 2, 2 + h // 2
                started = {0: False, 64: False}
                for wi, (kb, vtile, ktok, ca, cb, eidx) in enumerate(work):
                    sp = P[cnt % 4]
                    pt = pts[cnt % 3]
                    pf = ptf[cnt % 2]
                    cnt += 1
                    acc = P[4] if kb == 0 else P[5]
                    fw.mm(sp.v[kb:kb + 64, ca:cb], QK.v[hb:hb + 64, ck, ktok:ktok + 64],
                          QK.v[hb:hb + 64, cq, q0 + ca:q0 + cb])
                    if eidx is None:
                        fw.act(pt.v[kb:kb + 64, ca:cb], sp.v[kb:kb + 64, ca:cb], AF.Exp, scale=scale)
                    else:
                        fw.act(pf.v[kb:kb + 64, ca:cb], sp.v[kb:kb + 64, ca:cb], AF.Exp, scale=scale)
                        ev = E.v[kb:kb + 64, h, eidx[0]:eidx[1], :]
                        fw.tt(pt.v[kb:kb + 64, ca:cb].r("p (r q) -> p r q", q=64), pf.v[kb:kb + 64, ca:cb].r("p (r q) -> p r q", q=64),
                              ev, ALU.mult)
                    fw.mm(acc.v[:, ca:cb], Vv.v[kb:kb + 64, vtile, h, :], pt.v[kb:kb + 64, ca:cb],
                          start=(not started[kb]), stop=(last[kb] == wi))
                    started[kb] = True
                fw.copy(osb.v[0:65, 0:NQ], P[4].v[0:65, 0:NQ], "dve")
                fw.tt(osm.v[0:65, 0:NQ], osb.v[0:65, 0:NQ], P[5].v[0:65, 0:NQ], ALU.add)
                fw.mm(P[6].v[0:64, 0:NQ], G.esel.v[0:65, 0:64], osm.v[0:65, 0:NQ])
                fw.act(lnr.v[0:64, 0:NQ], P[6].v[0:64, 0:NQ], AF.Ln)
                fw.act(rec.v[0:64, 0:NQ], lnr.v[0:64, 0:NQ], AF.Exp, scale=-1.0)
                fw.tt(ost.v[:, h, 0:NQ], osm.v[0:64, 0:NQ], rec.v[0:64, 0:NQ], ALU.mult, eng="pool")
            for h in range(4):
                hb = (h % 2) * 64
                fw.dma("sp" if h % 2 else "act", S.MIXT[hb:hb + 64, 0 + h // 2, q0:q0 + NQ], ost.v[:, h, 0:NQ])


def phase_O(S, b, l, need_ctx):
    fw, I, G = S.fw, S.I, S.G
    with fw.phase():
        wo = fw.sb("wo", [128, 8, D], BF16)
        fw.dma("sp", wo.v, S.w_out_bf[l].r("(k p) n -> p k n", p=128))
        mxs = [fw.sb(f"mx{i}", [128, 8, 512], BF16) for i in range(2)]
        xts = [fw.sb(f"xto{i}", [128, 8, 512], F32) for i in range(2)]
        tiles = TOK_TILES[1:] + ([TOK_TILES[0]] if need_ctx else [])
        for ti, (t0, N) in enumerate(tiles):
            vec = b if t0 >= TC else S.nb
            mx, xt = mxs[ti % 2], xts[ti % 2]
            fw.dma("act", mx.v[:, :, 0:N], S.MIXT[:, :, t0:t0 + N])
            fw.dma("sp", xt.v[:, :, 0:N], S.XT[b, :, :, t0:t0 + N])
            for m in range(8):
                ps = nps(S)
                for k in range(8):
                    fw.mm(ps.v[:, 0:N], wo.v[:, k, m * 128:(m + 1) * 128], mx.v[:, k, 0:N], start=(k == 0), stop=(k == 7))
                fw.stt(xt.v[:, m, 0:N], ps.v[:, 0:N], G.mods.v[:, l, 16 + m, vec:vec + 1], xt.v[:, m, 0:N],
                       ALU.mult, ALU.add)
            fw.dma("sp", S.XT[b, :, :, t0:t0 + N], xt.v[:, :, 0:N])


def phase_mlp(S, b, l, need_ctx):
    fw, I, G, P = S.fw, S.I, S.G, S.P
    with fw.phase():
        cw = fw.sb("cw", [128, 3, 22], F32)
        cb = fw.sb("cb", [128, 22], F32)
        fw.dma("act", cw.v, I.conv_w[l])
        fw.dma("act", cb.v, I.conv_b[l])
        hwl = fw.sb("hwl", [128, 8, TL + 2], BF16)
        fw.memset(hwl.v[:, :, 0:1], 0.0, "pool")
        fw.memset(hwl.v[:, :, TL + 1:TL + 2], 0.0, "pool")
        if need_ctx:
            hwc = fw.sb("hwc", [128, 8, TC + 2], BF16)
            fw.memset(hwc.v[:, :, 0:1], 0.0, "pool")
            fw.memset(hwc.v[:, :, TC + 1:TC + 2], 0.0, "pool")
        tiles = TOK_TILES[1:] + ([TOK_TILES[0]] if need_ctx else [])
        with fw.phase():
            xw = fw.sb("xw", [128, 8, 512], F32)
            sq = fw.sb("sqw", [128, 8, 512], BF16)
            xn = fw.sb("xnw", [128, 8, 512], F32)
            rstd = fw.sb("rstdw", [128, 512], F32)
            lnt = fw.sb("lntw", [128, 512], F32)
            for (t0, n) in tiles:
                lat = t0 >= TC
                vec = b if lat else S.nb
                fw.dma("sp", xw.v[:, :, 0:n], S.XT[b, :, :, t0:t0 + n])
                fw.act(sq.v[:, :, 0:n], xw.v[:, :, 0:n], AF.Square)
                ps = nps(S)
                for k in range(8):
                    fw.mm(ps.v[:, 0:n], G.ones_bf.v, sq.v[:, k, 0:n], start=(k == 0), stop=(k == 7))
                rstd_from_sum(S, rstd.v[:, 0:n], ps.v[:, 0:n], 1024.0, EPS, lnt.v[:, 0:n])
                fw.tt(xn.v[:, :, 0:n], xw.v[:, :, 0:n], rstd.v[:, 0:n].bc(1, 8), ALU.mult)
                for k in range(8):
                    dst = hwl.v[:, k, 1 + t0 - TC:1 + t0 - TC + n] if lat else hwc.v[:, k, 1:1 + n]
                    fw.act(dst, xn.v[:, k, 0:n], AF.Identity, scale=G.A.v[:, l, 1, k, vec:vec + 1],
                           bias=G.mods.v[:, l, 24 + k, vec:vec + 1])
        gT = fw.sb("gT", [128, 22, TL], BF16)
        asb = fw.sb("asb", [128, TL + 2], F32)
        fw.memset(asb.v[:, 0:1], 0.0, "pool")
        fw.memset(asb.v[:, TL + 1:TL + 2], 0.0, "pool")
        c1 = fw.sb("c1", [128, TL], F32)
        c2 = fw.sb("c2", [128, TL], F32)
        if need_ctx:
            gTc = fw.sb("gTc", [128, 22, TC], BF16)
            asc = fw.sb("asc", [128, TC + 2], F32)
            fw.memset(asc.v[:, 0:1], 0.0, "pool")
            fw.memset(asc.v[:, TC + 1:TC + 2], 0.0, "pool")
            c1c = fw.sb("c1c", [128, TC], F32)
            c2c = fw.sb("c2c", [128, TC], F32)
        wus = [fw.sb(f"wu{i}", [128, 2, 8, 128], BF16) for i in range(2)]
        for j in range(22):
            wu = wus[j % 2]
            fw.dma("sp" if j % 2 else "act", wu.v, S.w_up_bf[l, j])
            for i in range(4):
                for k in range(8):
                    fw.mm(P[i].v, wu.v[:, 0, k, :], hwl.v[:, k, 1 + i * 512:1 + (i + 1) * 512], start=(k == 0), stop=(k == 7))
            for i in range(4):
                for k in range(8):
                    fw.mm(P[4 + i].v, wu.v[:, 1, k, :], hwl.v[:, k, 1 + i * 512:1 + (i + 1) * 512], start=(k == 0), stop=(k == 7))
            for i in range(4):
                fw.copy(asb.v[:, 1 + i * 512:1 + (i + 1) * 512], P[i].v, "act")
            fw.act(c1.v, asb.v[:, 0:TL], AF.Identity, scale=cw.v[:, 0, j:j + 1])
            fw.stt(c2.v, asb.v[:, 1:TL + 1], cw.v[:, 1, j:j + 1], c1.v, ALU.mult, ALU.add)
            fw.stt(c1.v, asb.v[:, 2:TL + 2], cw.v[:, 2, j:j + 1], c2.v, ALU.mult, ALU.add)
            fw.act(c2.v, c1.v, AF.Silu, bias=cb.v[:, j:j + 1])
            for i in range(4):
                fw.tt(gT.v[:, j, i * 512:(i + 1) * 512], c2.v[:, i * 512:(i + 1) * 512], P[4 + i].v, ALU.mult)
            if need_ctx:
                pc = P[0]
                for k in range(8):
                    fw.mm(pc.v[:, 0:TC], wu.v[:, 0, k, :], hwc.v[:, k, 1:TC + 1], start=(k == 0), stop=(k == 7))
                for k in range(8):
                    fw.mm(pc.v[:, TC:2 * TC], wu.v[:, 1, k, :], hwc.v[:, k, 1:TC + 1], start=(k == 0), stop=(k == 7))
                fw.copy(asc.v[:, 1:TC + 1], pc.v[:, 0:TC], "act")
                fw.act(c1c.v, asc.v[:, 0:TC], AF.Identity, scale=cw.v[:, 0, j:j + 1])
                fw.stt(c2c.v, asc.v[:, 1:TC + 1], cw.v[:, 1, j:j + 1], c1c.v, ALU.mult, ALU.add)
                fw.stt(c1c.v, asc.v[:, 2:TC + 2], cw.v[:, 2, j:j + 1], c2c.v, ALU.mult, ALU.add)
                fw.act(c2c.v, c1c.v, AF.Silu, bias=cb.v[:, j:j + 1])
                fw.tt(gTc.v[:, j, :], c2c.v, pc.v[:, TC:2 * TC], ALU.mult)
        wds = [fw.sb(f"wd{i}", [128, 22, 128], BF16) for i in range(2)]
        xt = fw.sb("xtm", [128, 8, 512], F32)
        it = 0
        for (t0, n) in tiles:
            lat = t0 >= TC
            vec = b if lat else S.nb
            fw.dma("act", xt.v[:, :, 0:n], S.XT[b, :, :, t0:t0 + n])
            for m in range(8):
                wd = wds[it % 2]
                it += 1
                fw.dma("sp", wd.v, S.w_down_bf[l, m])
                ps = nps(S)
                for j in range(22):
                    rhs = gT.v[:, j, t0 - TC:t0 - TC + n] if lat else gTc.v[:, j, 0:n]
                    fw.mm(ps.v[:, 0:n], wd.v[:, j, :], rhs, start=(j == 0), stop=(j == 21))
                fw.stt(xt.v[:, m, 0:n], ps.v[:, 0:n], G.mods.v[:, l, 40 + m, vec:vec + 1], xt.v[:, m, 0:n],
                       ALU.mult, ALU.add)
            fw.dma("sp", S.XT[b, :, :, t0:t0 + n], xt.v[:, :, 0:n])


def phase_final(S):
    fw, I, G = S.fw, S.I, S.G
    with fw.phase():
        gf = fw.sb("gf", [128, 8], F32)
        fw.dma("sp", gf.v, I.gfT.v)
        xts = [fw.sb(f"xf{i}", [128, 8, 128], F32) for i in range(2)]
        sq = fw.sb("sqf", [128, 8, 128], BF16)
        xn = fw.sb("xnf", [128, 8, 128], F32)
        rstd = fw.sb("rstdf", [128, 128], F32)
        lnt = fw.sb("lntf", [128, 128], F32)
        ots = [fw.sb(f"of{i}", [128, D], F32) for i in range(2)]
        it = 0
        for b in range(S.nb):
            for tt_ in range(TL // 128):
                xt, ot = xts[it % 2], ots[it % 2]
                it += 1
                t0 = TC + tt_ * 128
                fw.dma("sp", xt.v, S.XT[b, :, :, t0:t0 + 128])
                fw.act(sq.v, xt.v, AF.Square)
                ps = nps(S)
                for k in range(8):
                    fw.mm(ps.v[:, 0:128], G.ones_bf.v, sq.v[:, k, :], start=(k == 0), stop=(k == 7))
                rstd_from_sum(S, rstd.v, ps.v[:, 0:128], 1024.0, EPS, lnt.v)
                fw.tt(xn.v, xt.v, rstd.v.bc(1, 8), ALU.mult)
                fw.tt(xn.v, xn.v, gf.v.bc(2, 128), ALU.mult)
                for half in range(2):
                    p = nps(S)
                    for kk in range(4):
                        k = half * 4 + kk
                        fw.op("pe", lambda h: h.transpose(p.v[:, kk * 128:(kk + 1) * 128].ap, xn.v[:, k, :].ap,
                                                           G.ident.v.ap), [xn.v, G.ident.v], [p.v])
                    fw.copy(ot.v[:, half * 512:(half + 1) * 512], p.v, "act" if half else "dve")
                fw.dma("act", S.out[b, tt_ * 128:(tt_ + 1) * 128, :], ot.v)


RD = BF16
LDK = 0.6065306597126334
RK_EPS = 64e-5


def phase_rk(S, b, l, need_ctx):
    fw, I, G, P = S.fw, S.I, S.G, S.P
    SEG = 256
    NSEG = T // SEG
    with fw.phase():
        mu = fw.sb("mu", [128, 2, 7], F32)
        fw.dma("sp", mu.v, I.rk_mu[l])
        c0 = fw.sb("c0", [128, 7], F32)
        fw.tt(c0.v, mu.v[:, 0, :], mu.v[:, 1, :], ALU.add)
        fw.ts(c0.v, c0.v, -1.0, 1.0, ALU.mult, ALU.add)
        w0 = fw.sb("w0", [128, 2, 2], F32)
        a0 = fw.sb("a0", [128, 2, 2], F32)
        fw.dma("sp", w0.v, I.rk_w0[l])
        fw.dma("sp", a0.v, I.rk_a0[l])
        wup = fw.sb("wup", [32, 2, 256], F32)
        fw.dma("act", wup.v, I.rk_wup[l].r("d r c -> r d c"))
        aup = fw.sb("aup", [64, 2, 256], F32)
        fw.dma("act", aup.v[32:64], I.rk_aup[l].r("d r c -> r d c"))
        gup = fw.sb("gup", [128, 256], F32)
        fw.dma("act", gup.v[64:128], I.rk_gup[l])
        kkp = fw.sb("kkp", [128, 2], F32)
        ka = fw.sb("ka", [128, 2], F32)
        omka = fw.sb("omka", [128, 2], F32)
        rkp = fw.sb("rkp", [128, 2], F32)
        fw.dma("sp", kkp.v, I.rk_kk[l])
        fw.dma("sp", ka.v, I.rk_ka[l])
        fw.dma("sp", rkp.v, I.rk_rk[l])
        fw.ts(omka.v, ka.v, -1.0, 1.0, ALU.mult, ALU.add)
        lnw = fw.sb("lnw", [128, 2, 64], F32)
        lnb = fw.sb("lnb", [128, 2, 64], F32)
        for hh in range(2):
            fw.dma("sp", lnw.v[hh * 64:(hh + 1) * 64],
                   I.rk_lnw[l:l + 1, :].r("o (hp hh i) -> o hp hh i", hp=2, hh=2)[:, :, hh, :].f(
                       lambda a: a.broadcast_to([64, 2, 64])))
            fw.dma("act", lnb.v[hh * 64:(hh + 1) * 64],
                   I.rk_lnb[l:l + 1, :].r("o (hp hh i) -> o hp hh i", hp=2, hh=2)[:, :, hh, :].f(
                       lambda a: a.broadcast_to([64, 2, 64])))
        eye = fw.sb("eye", [128, 64], F32)
        fw.dma("sp", eye.v, I.c["rk_eye"].v)
        mkk = fw.sb("mkk", [128, 2, 2, 64], F32)
        fw.dma("sp", mkk.v, I.c["rk_mask"].v)
        bm = fw.sb("bm", [128, 2, 64], F32)
        fw.memset(bm.v, 0.0)
        fw.memset(bm.v[0:64, 0, :], 1.0)
        fw.memset(bm.v[64:128, 1, :], 1.0)
        mbd = fw.sb("mbd", [128, 2, 2, 2, 64], F32)
        for d in range(2):
            fw.tt(mbd.v[:, d], mkk.v[:, d].bc(2, 2), bm.v.bc(1, 2), ALU.mult)
        rm = fw.sb("rm", [128, 2, 512], F32)
        fw.dma("act", rm.v, I.c["rk_rm"].v)
        bg = fw.sb("bg", [128, 2, 2, T], BF16)
        Yacc = fw.sb("Yacc", [128, 36, 2, 64], F32)
        fw.memset(Yacc.v, 0.0, "pool")
        ST = fw.sb("ST", [128, 2, 2, 64], F32)
        fw.memset(ST.v, 0.0)
        if RD == F32:
            STb, identR = ST, G.ident
        else:
            STb = fw.sb("STb", [128, 2, 2, 64], RD)
            fw.memset(STb.v, 0.0)
            identR = fw.sb("identR", [128, 128], RD)
            fw.copy(identR.v, G.ident.v)

        with fw.phase():
            zw = fw.sb("zw", [128, 7, SEG + 2], F32)
            zs = fw.sb("zs", [128, 7, SEG], F32)
            zt = fw.sb("zt", [128, 7, SEG], F32)
            kk = fw.sb("kk", [128, 2, SEG], F32)
            kq = fw.sb("kq", [128, 2, SEG], F32)
            kkn = fw.sb("kkn", [128, 2, SEG], F32)
            lnk = fw.sb("lnk", [128, 2, SEG], F32)
            twd = fw.sb("twd", [32, SEG], F32)
            sgd = fw.sb("sgd", [128, SEG], F32)
            vbd = fw.sb("vbd", [128, 2, 4, 2, 64], F32)
            VT = fw.sb("VT", [128, 2, 4, 64], RD)
            X = NS()
            for nm in ("sig", "asig", "Ls", "ta", "tb", "Bt", "Kt", "Bh", "Kh"):
                setattr(X, nm, fw.sb(nm, [128, 2, SEG], F32))
            X.Btr = X.Bt if RD == F32 else fw.sb("Btr", [128, 2, SEG], RD)
            X.Ktr = X.Kt if RD == F32 else fw.sb("Ktr", [128, 2, SEG], RD)
            X.AR = fw.sb("AR", [128, 2, 4, 2, 64], RD)
            X.ARbd = fw.sb("ARbd", [128, 2, 4, 2, 2, 64], RD)
            X.Btbd = fw.sb("Btbd", [128, 2, 4, 2, 64], RD)
            X.Ktbd = fw.sb("Ktbd", [128, 2, 4, 2, 64], RD)
            X.Bhbd = fw.sb("Bhbd", [128, 2, 4, 2, 64], RD)
            X.Khbd = fw.sb("Khbd", [128, 2, 4, 2, 64], RD)
            X.BhT = fw.sb("BhT", [128, 2, 4, 128], RD)
            X.KhT = fw.sb("KhT", [128, 2, 4, 128], RD)
            X.MB = fw.sb("MB", [128, 2, 4, 2, 128], RD)
            X.MK = fw.sb("MK", [128, 2, 4, 2, 128], RD)
            X.Nn = [fw.sb(f"Nn{i}", [128, 4, 128], RD) for i in range(2)]
            X.NT = [fw.sb(f"NT{i}", [128, 4, 128], RD) for i in range(2)]
            X.Xx = [fw.sb(f"Xx{i}", [128, 4, 128], RD) for i in range(2)]
            X.TT = fw.sb("TT", [128, 2, 4, 128], RD)
            X.PC = fw.sb("PC", [128, 2, 4], F32)
            X.LC = fw.sb("LC", [128, 2, 4], F32)
            RHSsb = fw.sb("RHSsb", [128, 2, 64], RD)
            Usb = fw.sb("Usb", [128, 2, 64], RD)

            def c4(v):
                return v.r("p a (c s) -> p a c s", s=64)

            def prep_shared(seg):
                t0 = seg * SEG
                s0, s1 = (0, TC) if seg == 0 else (TC, T)
                w0_, w1_ = max(t0 - 1, s0), min(t0 + SEG + 1, s1)
                cc, n = w0_ - (t0 - 1), w1_ - w0_
                if n < SEG + 2:
                    fw.memset(zw.v, 0.0, "pool")
                fw.dma("sp", zw.v[:, :, cc:cc + n], S.ZT_rk[:, :, w0_:w1_])
                fw.tt(zs.v, zw.v[:, :, 1:SEG + 1], c0.v.bc(2, SEG), ALU.mult)
                fw.tt(zt.v, zw.v[:, :, 0:SEG], mu.v[:, 0, :].bc(2, SEG), ALU.mult, eng="pool")
                fw.tt(zs.v, zs.v, zt.v, ALU.add)
                fw.tt(zt.v, zw.v[:, :, 2:SEG + 2], mu.v[:, 1, :].bc(2, SEG), ALU.mult, eng="pool")
                fw.tt(zs.v, zs.v, zt.v, ALU.add)
                r_, k_, v_ = zs.v[:, 0:2, :], zs.v[:, 2:4, :], zs.v[:, 4:6, :]
                fw.act(twd.v, zs.v[0:32, 6, :], AF.Tanh)
                fw.act(sgd.v[64:128], zs.v[64:128, 6, :], AF.Sigmoid)
                fw.tt(kk.v, k_, kkp.v.bc(2, SEG), ALU.mult)
                fw.tt(kq.v, kk.v, kk.v, ALU.mult, eng="pool")
                ps = P[0]
                fw.mm(ps.v[:, 0:2 * SEG], G.blockones.v, kq.v.r("p a t -> p (a t)"))
                fw.ts(lnk.v.r("p a t -> p (a t)"), ps.v[:, 0:2 * SEG], 1e-24, None, ALU.max)
                fw.act(lnk.v, lnk.v, AF.Ln)
                fw.act(lnk.v, lnk.v, AF.Exp, scale=-0.5)
                fw.tt(kkn.v, kk.v, lnk.v, ALU.mult)
                fw.tt(kq.v, r_, k_, ALU.mult, eng="pool")
                fw.tt(kq.v, kq.v, rkp.v.bc(2, SEG), ALU.mult, eng="pool")
                ps = P[1]
                fw.mm(ps.v[:, 0:2 * SEG], G.blockones.v, kq.v.r("p a t -> p (a t)"))
                fw.tt(bg.v[:, :, 0, t0:t0 + SEG], ps.v[:, 0:2 * SEG].r("p (a t) -> p a t", a=2), v_, ALU.mult)
                ps = P[2]
                for hp in range(2):
                    fw.mm(ps.v[:, hp * SEG:(hp + 1) * SEG], gup.v[64:128, hp * 128:(hp + 1) * 128], sgd.v[64:128, :])
                fw.copy(bg.v[:, :, 1, t0:t0 + SEG], ps.v[:, 0:2 * SEG].r("p (a t) -> p a t", a=2), "act")
                for hp in range(2):
                    fw.tt(vbd.v[:, hp], c4(v_)[:, hp].bc(2, 2), bm.v.bc(1, 4), ALU.mult, eng="pool")
                ps = P[3]
                for hp in range(2):
                    for c in range(4):
                        fw.mm(ps.v[:, (hp * 4 + c) * 64:(hp * 4 + c + 1) * 64],
                              vbd.v[:, hp, c].r("p a s -> p (a s)"), eye.v)
                fw.copy(VT.v.r("p a c i -> p (a c i)"), ps.v[:, 0:512], "act")
                return r_, k_, v_

            def prep_dir(d, r_, k_):
                ps = P[4]
                for hp in range(2):
                    fw.mm(ps.v[:, hp * SEG:(hp + 1) * SEG], wup.v[0:32, d, hp * 128:(hp + 1) * 128], twd.v[0:32, :])
                for hp in range(2):
                    fw.act(X.sig.v[:, hp, :], ps.v[:, hp * SEG:(hp + 1) * SEG], AF.Sigmoid, bias=w0.v[:, d, hp:hp + 1])
                ps = P[5]
                for hp in range(2):
                    fw.mm(ps.v[:, hp * SEG:(hp + 1) * SEG], aup.v[32:64, d, hp * 128:(hp + 1) * 128], zs.v[32:64, 6, :])
                for hp in range(2):
                    fw.act(X.asig.v[:, hp, :], ps.v[:, hp * SEG:(hp + 1) * SEG], AF.Sigmoid, bias=a0.v[:, d, hp:hp + 1])
                for hp in range(2):
                    fw.ts(X.tb.v[:, hp, :], X.asig.v[:, hp, :], ka.v[:, hp:hp + 1], omka.v[:, hp:hp + 1], ALU.mult, ALU.add)
                fw.tt(X.tb.v, X.tb.v, k_, ALU.mult)
                fw.tt(X.ta.v, kkn.v, X.asig.v, ALU.mult, eng="pool")
                sf = X.sig.v.r("p a t -> p (a t)")
                lf = X.Ls.v.r("p a t -> p (a t)")
                if d == 0:
                    fw.scan(lf, rm.v[:, 0, :], sf)
                else:
                    fw.scan(lf[:, ::-1], rm.v[:, 1, ::-1], sf[:, ::-1])
                L4 = c4(X.Ls.v)
                lc = L4[:, :, :, 63:64] if d == 0 else L4[:, :, :, 0:1]
                fw.copy(X.LC.v, lc.r("p a c o -> p a (c o)"), "pool")
                fw.act(X.PC.v, X.LC.v, AF.Exp, scale=-LDK)
                eN, eC, eP, ePm = X.Bh, X.Kh, X.Bt, X.Kt
                fw.act(eN.v, X.Ls.v, AF.Exp, scale=LDK)
                fw.act(eP.v, X.Ls.v, AF.Exp, scale=-LDK)
                fw.tt(X.asig.v, X.Ls.v, X.sig.v, ALU.subtract)
                fw.act(ePm.v, X.asig.v, AF.Exp, scale=-LDK)
                fw.tt(c4(X.asig.v), X.LC.v.bc(3, 64), L4, ALU.subtract, eng="pool")
                AR = X.AR.v
                fw.stt(AR[:, :, :, 0, :], c4(kkn.v), -1.0, c4(ePm.v), ALU.mult, ALU.mult)
                fw.tt(AR[:, :, :, 1, :], c4(r_), c4(eP.v), ALU.mult)
                fw.tt(X.Btr.v, X.ta.v, eN.v, ALU.mult)
                fw.tt(X.Ktr.v, X.tb.v, eN.v, ALU.mult)
                fw.act(eC.v, X.asig.v, AF.Exp, scale=-LDK)
                fw.tt(X.Bh.v, X.ta.v, eC.v, ALU.mult)
                fw.tt(X.Kh.v, X.tb.v, eC.v, ALU.mult, eng="pool")
                for hp in range(2):
                    bmc = bm.v.bc(1, 4)
                    fw.tt(X.Btbd.v[:, hp], c4(X.Btr.v)[:, hp].bc(2, 2), bmc, ALU.mult, eng="pool")
                    fw.tt(X.Ktbd.v[:, hp], c4(X.Ktr.v)[:, hp].bc(2, 2), bmc, ALU.mult)
                    fw.tt(X.Bhbd.v[:, hp], c4(X.Bh.v)[:, hp].bc(2, 2), bmc, ALU.mult, eng="pool")
                    fw.tt(X.Khbd.v[:, hp], c4(X.Kh.v)[:, hp].bc(2, 2), bmc, ALU.mult)
                    for ar in range(2):
                        fw.tt(X.ARbd.v[:, hp, :, ar], AR[:, hp, :, ar, :].bc(2, 2), bmc, ALU.mult,
                              eng="pool" if ar else "dve")
                for hp in range(2):
                    pB, pK, pN = P[0], P[1], P[2]
                    for c in range(4):
                        arc = AR[:, hp, c].r("p a t -> p (a t)")
                        fw.mm(pB.v[:, c * 128:(c + 1) * 128], X.Btbd.v[:, hp, c].r("p a s -> p (a s)"), arc)
                        fw.mm(pK.v[:, c * 128:(c + 1) * 128], X.Ktbd.v[:, hp, c].r("p a s -> p (a s)"), arc)
                        fw.mm(pN.v[:, c * 64:(c + 1) * 64], X.ARbd.v[:, hp, c, 0].r("p a t -> p (a t)"),
                              X.Btr.v[:, hp, c * 64:(c + 1) * 64])
                    for ar in range(2):
                        mk_ = mbd.v[:, d, ar].bc(1, 4)
                        fw.tt(X.MB.v[:, hp, :, ar].r("p c (a t) -> p c a t", a=2),
                              pB.v.r("p (c a t) -> p c a t", c=4, a=2)[:, :, ar, :].bc(2, 2), mk_, ALU.mult)
                        fw.tt(X.MK.v[:, hp, :, ar].r("p c (a t) -> p c a t", a=2),
                              pK.v.r("p (c a t) -> p c a t", c=4, a=2)[:, :, ar, :].bc(2, 2), mk_, ALU.mult)
                    fw.tt(X.NT[0].v.r("p c (a t) -> p c a t", a=2), pN.v[:, 0:256].r("p (c t) -> p c t", c=4).bc(2, 2),
                          mbd.v[:, 1 - d, 0].bc(1, 4), ALU.mult)
                    Ncur = X.MB.v[:, hp, :, 0]
                    NTcur = X.NT[0].v
                    fw.tt(X.Xx[0].v, Ncur, G.ident.v.bc(1, 4), ALU.add, eng="pool")
                    Xcur = X.Xx[0].v
                    for lev in range(1, 6):
                        p1, p2, p3 = P[3], P[0], P[1]
                        NTn = X.NT[lev % 2].v
                        for c in range(4):
                            cs = slice(c * 128, (c + 1) * 128)
                            if lev < 5:
                                fw.mm(p1.v[:, cs], NTcur[:, c, :], Ncur[:, c, :])
                            fw.mm(p2.v[:, cs], Ncur[:, c, :], NTcur[:, c, :])
                        if lev < 5:
                            Nn = X.Nn[lev % 2].v
                            fw.copy(Nn, p1.v.r("p (c t) -> p c t", c=4), "act")
                        fw.copy(NTn, p2.v.r("p (c t) -> p c t", c=4), "dve")
                        for c in range(4):
                            fw.mm(p3.v[:, c * 128:(c + 1) * 128], NTn[:, c, :], Xcur[:, c, :])
                        Xn = X.Xx[lev % 2].v if lev < 5 else X.TT.v[:, hp]
                        fw.tt(Xn, p3.v.r("p (c t) -> p c t", c=4), Xcur, ALU.add)
                        Xcur = Xn
                        if lev < 5:
                            Ncur, NTcur = Nn, NTn
                for hp in range(2):
                    pB, pK = P[2], P[3]
                    for c in range(4):
                        cs = slice(c * 128, (c + 1) * 128)
                        fw.mm(pB.v[:, cs], X.Bhbd.v[:, hp, c].r("p a s -> p (a s)"), identR.v)
                        fw.mm(pK.v[:, cs], X.Khbd.v[:, hp, c].r("p a s -> p (a s)"), identR.v)
                    fw.copy(X.BhT.v[:, hp], pB.v.r("p (c t) -> p c t", c=4), "act")
                    fw.copy(X.KhT.v[:, hp], pK.v.r("p (c t) -> p c t", c=4), "dve")

            def chunk_step(d, c, q):
                pR, pU, pY, pS = P[4], P[5], P[6], P[7]
                for hp in range(2):
                    oc = slice(hp * 64, (hp + 1) * 64)
                    fw.mm(pR.v[:, oc], X.ARbd.v[:, hp, c, 0].r("p a t -> p (a t)"), STb.v[:, d, hp, :], start=True, stop=False)
                    fw.mm(pR.v[:, oc], X.MK.v[:, hp, c, 0], VT.v[:, hp, c, :], start=False, stop=True)
                fw.copy(RHSsb.v, pR.v[:, 0:128].r("p (a i) -> p a i", a=2), "act")
                for hp in range(2):
                    oc = slice(hp * 64, (hp + 1) * 64)
                    fw.mm(pU.v[:, oc], X.TT.v[:, hp, c, :], RHSsb.v[:, hp, :])
                fw.copy(Usb.v, pU.v[:, 0:128].r("p (a i) -> p a i", a=2), "dve")
                for hp in range(2):
                    oc = slice(hp * 64, (hp + 1) * 64)
                    fw.mm(pY.v[:, oc], X.ARbd.v[:, hp, c, 1].r("p a t -> p (a t)"), STb.v[:, d, hp, :], start=True, stop=False)
                    fw.mm(pY.v[:, oc], X.MB.v[:, hp, c, 1], Usb.v[:, hp, :], start=False, stop=False)
                    fw.mm(pY.v[:, oc], X.MK.v[:, hp, c, 1], VT.v[:, hp, c, :], start=False, stop=True)
                    fw.mm(pS.v[:, oc], X.BhT.v[:, hp, c, :], Usb.v[:, hp, :], start=True, stop=False)
                    fw.mm(pS.v[:, oc], X.KhT.v[:, hp, c, :], VT.v[:, hp, c, :], start=False, stop=True)
                fw.tt(Yacc.v[:, q], pY.v[:, 0:128].r("p (a i) -> p a i", a=2), Yacc.v[:, q], ALU.add)
                for hp in range(2):
                    oc = slice(hp * 64, (hp + 1) * 64)
                    fw.stt(ST.v[:, d, hp, :], ST.v[:, d, hp, :], X.PC.v[:, hp, c:c + 1], pS.v[:, oc], ALU.mult, ALU.add)
                if RD != F32:
                    fw.copy(STb.v[:, d], ST.v[:, d], "act")

            order = [list(range(NSEG)), [0] + list(range(NSEG - 1, 0, -1))]
            for d in range(2):
                for seg in order[d]:
                    r_, k_, v_ = prep_shared(seg)
                    prep_dir(d, r_, k_)
                    cl = range(4) if d == 0 else range(3, -1, -1)
                    for c in cl:
                        chunk_step(d, c, seg * 4 + c)

        with fw.phase():
            Yf = Yacc.v.r("p q a i -> p (q a) i")
            sm = fw.sb("sm", [128, 72], F32)
            sq = fw.sb("sqy", [128, 72, 64], F32)
            s2 = fw.sb("s2", [128, 72], F32)
            var = fw.sb("var", [128, 72], F32)
            fw.reduce(sm.v, Yf)
            fw.ts(sm.v, sm.v, 1.0 / 64, None, ALU.mult)
            fw.tt(sq.v, Yf, sm.v.bc(2, 64), ALU.subtract)
            yn = fw.sb("yn", [128, 72, 64], F32)
            fw.tt(yn.v, sq.v, sq.v, ALU.mult, eng="pool")
            fw.reduce(s2.v, yn.v)
            fw.act(var.v, s2.v, AF.Ln, scale=1.0 / 64, bias=RK_EPS)
            fw.act(var.v, var.v, AF.Exp, scale=-0.5)
            fw.tt(yn.v, sq.v, var.v.bc(2, 64), ALU.mult)
            y4 = yn.v.r("p (q a) i -> p q a i", a=2)
            fw.tt(y4, y4, lnw.v.bc(1, 36), ALU.mult)
            fw.tt(y4, y4, lnb.v.bc(1, 36), ALU.add, eng="pool")
            ybd = [fw.sb(f"ybd{i}", [128, 8, 2, 64], F32) for i in range(2)]
            osts = [fw.sb(f"ork{i}", [128, 512], F32) for i in range(2)]
            ostb = [fw.sb(f"orkb{i}", [128, 512], BF16) for i in range(2)]
            it = 0
            nq = 36
            for hp in range(2):
                for q0 in range(0, nq, 8):
                    nqq = min(8, nq - q0)
                    ps = P[it % 4]
                    o1, o2, yb = osts[it % 2], ostb[it % 2], ybd[it % 2]
                    it += 1
                    fw.tt(yb.v[:, 0:nqq], y4[:, q0:q0 + nqq, hp, :].bc(2, 2), bm.v.bc(1, nqq), ALU.mult)
                    for qq in range(nqq):
                        fw.mm(ps.v[:, qq * 64:(qq + 1) * 64], yb.v[:, qq].r("p a i -> p (a i)"), eye.v)
                    n = nqq * 64
                    t0 = q0 * 64
                    fw.tt(o1.v[:, 0:n], ps.v[:, 0:n], bg.v[:, hp, 0, t0:t0 + n], ALU.add)
                    fw.tt(o2.v[:, 0:n], o1.v[:, 0:n], bg.v[:, hp, 1, t0:t0 + n], ALU.mult, eng="pool")
                    fw.dma("sp", S.MIXT[:, 4 + hp, t0:t0 + n], o2.v[:, 0:n])


NCORES = 8
_CACHE = {}


def kernel(**inputs):
    inp = {k: np.asarray(v) for k, v in inputs.items()}
    B = inp["x"].shape[0]
    nb = B // NCORES
    if "prog" not in _CACHE:
        _CACHE["prog"] = build_program(nb)
    nc, fw = _CACHE["prog"]
    in_maps = [_layout_inputs(inp, nb, c) for c in range(NCORES)]
    res = run_bass_kernel_spmd(nc, in_maps, core_ids=list(range(NCORES)))
    out = np.concatenate([np.asarray(r["out"]) for r in res.results], axis=0)
    return out.astype(np.float32)
```

```python
import contextlib
import numpy as np
import concourse.bass as bass
import concourse.mybir as mybir
from concourse.bass_utils import run_bass_kernel_spmd

F32 = mybir.dt.float32
BF16 = mybir.dt.bfloat16
AF = mybir.ActivationFunctionType
ALU = mybir.AluOpType
AX = mybir.AxisListType

D = 1024
TC = 256
TL = 2048
T = TC + TL
NLAY = 4
DFF = 2816
INC = 2848
C_NA, C_MLA, C_RK, C_DF = 0, 768, 1184, 2080


class V:
    def __init__(self, ap, buf):
        self.ap, self.buf = ap, buf

    def __getitem__(self, idx):
        return V(self.ap[idx], self.buf)

    def f(self, fn):
        return V(fn(self.ap), self.buf)

    def r(self, pat, **kw):
        return V(self.ap.rearrange(pat, **kw), self.buf)

    def bc(self, axis, n):
        a = self.ap.unsqueeze(axis)
        sh = list(a.shape)
        sh[axis] = n
        return V(a.broadcast_to(sh), self.buf)


class Buf:
    def __init__(self, name, t):
        self.name, self.t = name, t
        self.lw = None
        self.rd = {}

    def __getitem__(self, idx):
        return V(self.t[idx], self)

    @property
    def v(self):
        return V(self.t[:], self)


def _ap(x):
    return x.ap if isinstance(x, V) else x


class FW:
    def __init__(self, nc, n_dma_sems=20):
        self.nc = nc
        self.engs = {}
        for name, h in (("pe", nc.tensor), ("act", nc.scalar), ("dve", nc.vector), ("pool", nc.gpsimd),
                        ("sp", nc.sync)):
            self.engs[name] = dict(name=name, h=h, sem=nc.alloc_semaphore("s_" + name), cnt=0, waited={})
        self.dq = {}
        for q in ("sp", "act", "pool"):
            sems = [nc.alloc_semaphore(f"d_{q}_{i}") for i in range(n_dma_sems)]
            self.dq[q] = dict(sems=sems, vals=[0] * n_dma_sems, nxt=0)
        self.ninst = 0
        self.nwait = 0
        self.stack = None

    def sb(self, name, shape, dt):
        self.nsb = getattr(self, "nsb", 0) + 1
        name = f"{name}_{self.nsb}"
        if self.stack is not None:
            t = self.stack.enter_context(self.nc.sbuf_tensor(name, list(shape), dt))
        else:
            t = self.nc.alloc_sbuf_tensor(name, list(shape), dt)
        return Buf(name, t)

    def ps(self, name, shape, dt=F32):
        return Buf(name, self.nc.alloc_psum_tensor(name, list(shape), dt))

    def dram(self, name, shape, dt, kind="Internal"):
        return Buf(name, self.nc.dram_tensor(name, list(shape), dt, kind=kind))

    @contextlib.contextmanager
    def phase(self):
        old = self.stack
        with contextlib.ExitStack() as st:
            self.stack = st
            yield
            self.barrier()
        self.stack = old

    def _wait(self, e, key, sem, val):
        if e["waited"].get(key, 0) >= val:
            return
        e["h"].wait_ge(sem, val)
        e["waited"][key] = val
        self.nwait += 1

    def _deps(self, e, reads, writes, mykey):
        deps = {}

        def add(k, s, v):
            if k not in deps or deps[k][1] < v:
                deps[k] = (s, v)
        for b in reads:
            if b.lw is not None:
                add(*b.lw)
        for b in writes:
            if b.lw is not None:
                add(*b.lw)
            for k, (s, v) in b.rd.items():
                add(k, s, v)
        need = []
        for k, (s, v) in deps.items():
            if k == mykey and e["name"] == "pe":
                continue
            if e["waited"].get(k, 0) >= v:
                continue
            need.append((k, s, v))
        for (k, s, v) in need[:-1]:
            self._wait(e, k, s, v)
        return need[-1] if need else None

    def _fold(self, e, ins, last):
        if last is None:
            return
        k, s, v = last
        ins.wait_op(s, v, "sem-ge")
        e["waited"][k] = v

    def _mark(self, reads, writes, key, sem, val):
        for b in reads:
            b.rd[key] = (sem, val)
        for b in writes:
            b.lw = (key, sem, val)
            b.rd = {}

    def op(self, eng, fn, reads, writes):
        e = self.engs[eng]
        reads = [x.buf for x in reads if isinstance(x, V)]
        writes = [x.buf for x in writes if isinstance(x, V)]
        last = self._deps(e, reads, writes, eng)
        ins = fn(e["h"])
        self._fold(e, ins, last)
        e["cnt"] += 1
        ins.then_inc(e["sem"], 1)
        self._mark(reads, writes, eng, e["sem"], e["cnt"])
        self.ninst += 1
        return ins

    def dma(self, q, out, in_, **kw):
        e = self.engs[q]
        d = self.dq[q]
        i = d["nxt"]
        d["nxt"] = (i + 1) % len(d["sems"])
        sem = d["sems"][i]
        key = f"dma_{q}_{i}"
        if d["vals"][i] > 0:
            self._wait(e, key, sem, d["vals"][i])
        reads, writes = [in_.buf], [out.buf]
        last = self._deps(e, reads, writes, key)
        ins = e["h"].dma_start(out=out.ap, in_=in_.ap, **kw)
        self._fold(e, ins, last)
        d["vals"][i] += 16
        ins.then_inc(sem, 16)
        self._mark(reads, writes, key, sem, d["vals"][i])
        self.ninst += 1
        return ins

    def barrier(self):
        sp = self.engs["sp"]
        for q, d in self.dq.items():
            for i, sem in enumerate(d["sems"]):
                if d["vals"][i] > 0:
                    self._wait(sp, f"dma_{q}_{i}", sem, d["vals"][i])
        for n in ("pe", "act", "dve", "pool"):
            o = self.engs[n]
            if o["cnt"] > 0:
                self._wait(sp, n, o["sem"], o["cnt"])
        sp["cnt"] += 1
        sp["h"].sem_inc(sp["sem"], 1)
        for n in ("pe", "act", "dve", "pool"):
            self._wait(self.engs[n], "sp", sp["sem"], sp["cnt"])
            for m in ("pe", "act", "dve", "pool"):
                self.engs[n]["waited"][m] = max(self.engs[n]["waited"].get(m, 0), self.engs[m]["cnt"])
            for q, d in self.dq.items():
                for i in range(len(d["sems"])):
                    k = f"dma_{q}_{i}"
                    self.engs[n]["waited"][k] = max(self.engs[n]["waited"].get(k, 0), d["vals"][i])

    def mm(self, out, lhsT, rhs, start=True, stop=True):
        return self.op("pe", lambda h: h.matmul(out.ap, lhsT=lhsT.ap, rhs=rhs.ap, start=start, stop=stop),
                       [lhsT, rhs], [out])

    def act(self, out, in_, func, scale=None, bias=None, accum_out=None, eng="act"):
        kw = {}
        if scale is not None:
            kw["scale"] = _ap(scale)
        if bias is not None:
            kw["bias"] = _ap(bias)
        if accum_out is not None:
            kw["accum_out"] = _ap(accum_out)
        return self.op("act", lambda h: h.activation(out=out.ap, in_=in_.ap, func=func, **kw),
                       [in_, scale, bias], [out, accum_out])

    def tt(self, out, in0, in1, op, eng="dve"):
        return self.op(eng, lambda h: h.tensor_tensor(out=out.ap, in0=in0.ap, in1=in1.ap, op=op), [in0, in1], [out])

    def ts(self, out, in0, s1, s2, op0, op1=None, eng="dve"):
        kw = dict(op1=op1) if op1 is not None else {}
        return self.op(eng, lambda h: h.tensor_scalar(out=out.ap, in0=in0.ap, scalar1=_ap(s1), scalar2=_ap(s2),
                                                      op0=op0, **kw), [in0, s1, s2], [out])

    def stt(self, out, in0, scalar, in1, op0, op1):
        return self.op("dve", lambda h: h.scalar_tensor_tensor(out=out.ap, in0=in0.ap, scalar=_ap(scalar),
                                                               in1=in1.ap, op0=op0, op1=op1),
                       [in0, scalar, in1], [out])

    def copy(self, out, in_, eng="dve"):
        if eng == "act":
            return self.act(out, in_, AF.Copy)
        return self.op(eng, lambda h: h.tensor_copy(out=out.ap, in_=in_.ap), [in_], [out])

    def memset(self, out, val, eng="dve"):
        return self.op(eng, lambda h: h.memset(out.ap, val), [], [out])

    def scan(self, out, d0, d1, op0=ALU.mult, op1=ALU.add):
        return self.op("dve", lambda h: h.tensor_tensor_scan(out=out.ap, data0=d0.ap, data1=d1.ap, initial=0.0,
                                                             op0=op0, op1=op1), [d0, d1], [out])

    def recip(self, out, in_):
        return self.op("dve", lambda h: h.reciprocal(out=out.ap, in_=in_.ap), [in_], [out])

    def reduce(self, out, in_, op=ALU.add, axis=AX.X):
        return self.op("dve", lambda h: h.tensor_reduce(out=out.ap, in_=in_.ap, axis=axis, op=op), [in_], [out])


def _rope_tables():
    t = np.arange(TL)
    row = (t // 64).astype(np.float32)
    col = (t % 64).astype(np.float32)
    freqs = (np.float32(10000.0) ** (-np.arange(0, 16, 2, dtype=np.float32) / np.float32(16))).astype(np.float32)
    ar = row[:, None] * freqs[None, :]
    ac = col[:, None] * freqs[None, :]
    ang = np.concatenate([ar, ar, ac, ac], axis=-1).astype(np.float32)
    return np.cos(ang).astype(np.float32).T.copy(), np.sin(ang).astype(np.float32).T.copy()


def _rot32():
    R = np.zeros((32, 32), np.float32)
    for m in range(8):
        R[8 + m, m] = -1.0
        R[m, 8 + m] = 1.0
        R[24 + m, 16 + m] = -1.0
        R[16 + m, 24 + m] = 1.0
    return R


def _consts():
    cos, sin = _rope_tables()
    c = {}
    c["cos128"] = np.tile(cos, (4, 1))
    c["sin128"] = np.tile(sin, (4, 1))
    c["cos96"] = np.concatenate([np.ones((64, TL), np.float32), cos], 0)
    c["sin96"] = np.concatenate([np.zeros((64, TL), np.float32), sin], 0)
    R = _rot32()
    R128 = np.zeros((128, 128), np.float32)
    for g in range(4):
        R128[g * 32:(g + 1) * 32, g * 32:(g + 1) * 32] = R
    R96 = np.zeros((96, 96), np.float32)
    R96[64:, 64:] = R
    c["R128"] = R128
    c["R96"] = R96
    c["ident"] = np.eye(128, dtype=np.float32)
    bo = np.zeros((128, 128), np.float32)
    bo[:64, :64] = 1.0
    bo[64:, 64:] = 1.0
    c["blockones"] = bo
    es = np.zeros((65, 64), np.float32)
    es[64, :] = 1.0
    c["esel"] = es
    cpos = np.arange(64)
    cstart = np.clip(cpos - 8, 0, 48)
    m = (cpos[None, :] >= cstart[:, None]) & (cpos[None, :] < cstart[:, None] + 16)
    c["na_mask"] = np.ascontiguousarray(m.T.astype(np.float32))
    s = np.arange(64)[:, None]
    t = np.arange(64)[None, :]
    mk = np.zeros((128, 2, 2, 64), np.float32)
    for hh in range(2):
        mk[hh * 64:(hh + 1) * 64, 0, 0] = (s < t)
        mk[hh * 64:(hh + 1) * 64, 0, 1] = (s <= t)
        mk[hh * 64:(hh + 1) * 64, 1, 0] = (s > t)
        mk[hh * 64:(hh + 1) * 64, 1, 1] = (s >= t)
    c["rk_mask"] = mk
    gmm = np.zeros((128, 4), np.float32)
    for g in range(4):
        gmm[g * 32:(g + 1) * 32, g] = 1.0
    c["df_gm"] = gmm
    c["rk_eye"] = np.concatenate([np.eye(64, dtype=np.float32)] * 2, 0)
    rm = np.ones((128, 2, 512), np.float32)
    rm[:, 0, 0::64] = 0.0
    rm[:, 1, 63::64] = 0.0
    c["rk_rm"] = rm
    return c


CONST_SHAPES = dict(cos128=(128, TL), sin128=(128, TL), cos96=(96, TL), sin96=(96, TL), R128=(128, 128),
                    R96=(96, 96), ident=(128, 128), blockones=(128, 128), esel=(65, 64), na_mask=(64, 64),
                    rk_mask=(128, 2, 2, 64), rk_eye=(128, 64), rk_rm=(128, 2, 512), df_gm=(128, 4))


def _fm(a, k):
    sh = a.shape[:-1]
    return np.ascontiguousarray(np.swapaxes(a.reshape(sh + (k, 128)), -1, -2))


def _layout_inputs(inp, nb, core):
    b0 = core * nb
    m = {}
    m["x"] = np.ascontiguousarray(inp["x"][b0:b0 + nb])
    m["ctx"] = np.ascontiguousarray(inp["ctx"][b0:b0 + nb])
    cv = np.concatenate([inp["c"][b0:b0 + nb], inp["c_ctx"][None]], 0)
    m["cvec"] = np.ascontiguousarray(cv.reshape(nb + 1, 8, 128).transpose(2, 1, 0))
    m["ada_w"] = inp["ada_w"]
    m["ada_bT"] = _fm(inp["ada_b"], 48)
    m["g1T"] = _fm(inp["norm1_g"], 8)
    m["g2T"] = _fm(inp["norm2_g"], 8)
    m["gfT"] = _fm(inp["final_norm_g"], 8)
    m["w_in"] = inp["w_in"]
    m["w_out"] = inp["w_out"]
    m["w_up"] = inp["mlp_w_up"]
    m["w_down"] = inp["mlp_w_down"]
    m["mla_qn"] = _fm(inp["mla_q_norm"], 2)
    m["mla_kvn"] = _fm(inp["mla_kv_norm"], 1)
    m["w_uq"] = inp["mla_w_uq"]
    m["w_ukv"] = inp["mla_w_ukv"]
    m["rk_mu"] = np.ascontiguousarray(inp["rwkv_mu"].reshape(NLAY, 2, 7, 128).transpose(0, 3, 1, 2))
    m["rk_w0"] = np.ascontiguousarray(inp["rwkv_w0"].reshape(NLAY, 2, 2, 128).transpose(0, 3, 1, 2))
    m["rk_a0"] = np.ascontiguousarray(inp["rwkv_a0"].reshape(NLAY, 2, 2, 128).transpose(0, 3, 1, 2))
    m["rk_wup"] = inp["rwkv_w_up"]
    m["rk_aup"] = inp["rwkv_a_up"]
    m["rk_gup"] = inp["rwkv_g_up"]
    m["rk_kk"] = _fm(inp["rwkv_k_k"], 2)
    m["rk_ka"] = _fm(inp["rwkv_k_a"], 2)
    m["rk_rk"] = _fm(inp["rwkv_r_k"].reshape(NLAY, 256), 2)
    m["rk_lnw"] = inp["rwkv_ln_w"]
    m["rk_lnb"] = inp["rwkv_ln_b"]
    m["df_lam"] = inp["diff_lambda"].reshape(NLAY, 1, 128)
    m["df_sub"] = np.ascontiguousarray(inp["diff_subln"].reshape(NLAY, 64, 1))
    m["conv_w"] = np.ascontiguousarray(inp["mlp_conv_w"].reshape(NLAY, 3, 22, 128).transpose(0, 3, 1, 2))
    m["conv_b"] = _fm(inp["mlp_conv_b"], 22)
    rpb = inp["na_rpb"]
    kc = np.arange(64)[:, None]
    qc = np.arange(64)[None, :]
    cidx = np.clip(kc - qc + 15, 0, 30)
    g = rpb[:, :, ::-1, :][:, :, :, cidx]
    m["na_bias"] = np.ascontiguousarray(g.transpose(0, 3, 1, 2, 4))
    for k, v in _consts().items():
        m["c_" + k] = v
    return {k: np.ascontiguousarray(v) for k, v in m.items()}


class NS:
    pass


IN_SHAPES = None


def build_program(nb, nlay=NLAY, dbg=(), parts=("na", "mla", "rk", "df", "mlp"), upto="end"):
    nc = bass.Bass("TRN2", target_bir_lowering=False)
    fw = FW(nc)
    S = NS()
    S.nc, S.fw, S.nb, S.nlay, S.dbg, S.parts = nc, fw, nb, nlay, set(dbg), parts
    NV = nb + 1
    S.NV = NV

    def din(name, shape, dt=F32):
        return fw.dram(name, shape, dt, kind="ExternalInput")

    I = NS()
    S.I = I
    I.x = din("x", [nb, TL, D])
    I.ctx = din("ctx", [nb, TC, D])
    I.cvec = din("cvec", [128, 8, NV])
    I.ada_w = din("ada_w", [NLAY, D, 6 * D])
    I.ada_bT = din("ada_bT", [NLAY, 128, 48])
    I.g1T = din("g1T", [NLAY, 128, 8])
    I.g2T = din("g2T", [NLAY, 128, 8])
    I.gfT = din("gfT", [128, 8])
    I.w_in = din("w_in", [NLAY, D, INC])
    I.w_out = din("w_out", [NLAY, D, D])
    I.w_up = din("w_up", [NLAY, D, 2 * DFF])
    I.w_down = din("w_down", [NLAY, DFF, D])
    I.mla_qn = din("mla_qn", [NLAY, 128, 2])
    I.mla_kvn = din("mla_kvn", [NLAY, 128, 1])
    I.w_uq = din("w_uq", [NLAY, 256, 384])
    I.w_ukv = din("w_ukv", [NLAY, 128, 512])
    I.rk_mu = din("rk_mu", [NLAY, 128, 2, 7])
    I.rk_w0 = din("rk_w0", [NLAY, 128, 2, 2])
    I.rk_a0 = din("rk_a0", [NLAY, 128, 2, 2])
    I.rk_wup = din("rk_wup", [NLAY, 2, 32, 256])
    I.rk_aup = din("rk_aup", [NLAY, 2, 32, 256])
    I.rk_gup = din("rk_gup", [NLAY, 64, 256])
    I.rk_kk = din("rk_kk", [NLAY, 128, 2])
    I.rk_ka = din("rk_ka", [NLAY, 128, 2])
    I.rk_rk = din("rk_rk", [NLAY, 128, 2])
    I.rk_lnw = din("rk_lnw", [NLAY, 256])
    I.rk_lnb = din("rk_lnb", [NLAY, 256])
    I.df_lam = din("df_lam", [NLAY, 1, 128])
    I.df_sub = din("df_sub", [NLAY, 64, 1])
    I.conv_w = din("conv_w", [NLAY, 128, 3, 22])
    I.conv_b = din("conv_b", [NLAY, 128, 22])
    I.na_bias = din("na_bias", [NLAY, 64, 4, 15, 64])
    I.c = {}
    for k, sh in CONST_SHAPES.items():
        I.c[k] = din("c_" + k, list(sh))
    S.out = fw.dram("out", [nb, TL, D], F32, kind="ExternalOutput")

    def scratch(name, shape, dt):
        return fw.dram(name, shape, dt, kind="ExternalOutput" if name in S.dbg else "Internal")
    S.scratch = scratch
    S.w_in_bf = scratch("w_in_bf", [NLAY, D, INC], BF16)
    S.w_out_bf = scratch("w_out_bf", [NLAY, D, D], BF16)
    S.w_up_bf = scratch("w_up_bf", [NLAY, 22, 128, 2, 8, 128], BF16)
    S.w_down_bf = scratch("w_down_bf", [NLAY, 8, 128, 22, 128], BF16)
    S.w_uq_bf = scratch("w_uq_bf", [NLAY, 256, 384], BF16)
    S.w_ukv_bf = scratch("w_ukv_bf", [NLAY, 128, 512], BF16)
    S.XT = scratch("XT", [nb, 128, 8, T], F32)
    S.QK_na = scratch("QK_na", [128, 4, T], BF16)
    S.V_na = scratch("V_na", [18, 128, 4, 128], BF16)
    S.QT_mla = scratch("QT_mla", [96, 4, T], BF16)
    S.KT_mla = scratch("KT_mla", [96, 4, T], BF16)
    S.V_mla = scratch("V_mla", [18, 128, 4, 128], BF16)
    S.QK_df = scratch("QK_df", [128, 4, T], BF16)
    S.V_df = scratch("V_df", [18, 128, 4, 128], BF16)
    S.ZT_rk = scratch("ZT_rk", [128, 7, T], F32)
    S.MIXT = scratch("MIXT", [128, 8, T], BF16)

    S.P = [fw.ps(f"P{i}", [128, 512]) for i in range(8)]

    G = NS()
    S.G = G
    G.ident = fw.sb("ident", [128, 128], F32)
    G.ones_bf = fw.sb("ones_bf", [128, 128], BF16)
    G.blockones = fw.sb("blockones", [128, 128], F32)
    G.esel = fw.sb("esel", [65, 64], F32)
    G.mods = fw.sb("mods", [128, nlay, 48, NV], F32)
    G.A = fw.sb("Amod", [128, nlay, 2, 8, NV], F32)
    fw.dma("sp", G.ident.v, I.c["ident"].v)
    fw.dma("sp", G.blockones.v, I.c["blockones"].v)
    fw.dma("sp", G.esel.v, I.c["esel"].v)
    fw.memset(G.ones_bf.v, 1.0)

    phase_prep(S)
    phase_mod(S)
    for b in range(nb):
        for l in range(nlay):
            need_ctx = l < NLAY - 1
            phase_A(S, b, l)
            if upto == "A":
                continue
            if "mla" in parts:
                phase_mla(S, b, l, need_ctx)
            if "df" in parts:
                phase_df(S, b, l, need_ctx)
            if "na" in parts:
                phase_na(S, b, l, need_ctx)
            if "rk" in parts:
                phase_rk(S, b, l, need_ctx)
            if upto == "att":
                continue
            phase_O(S, b, l, need_ctx)
            if upto == "O":
                continue
            if "mlp" in parts:
                phase_mlp(S, b, l, need_ctx)
    if upto == "end":
        phase_final(S)
    fw.barrier()
    return nc, fw


def phase_prep(S):
    fw, I, nb = S.fw, S.I, S.nb
    for l in range(S.nlay):
        for j in range(22):
            for ab in range(2):
                fw.dma("pool", S.w_up_bf[l, j, :, ab, :, :],
                       I.w_up[l, :, ab * DFF + j * 128:ab * DFF + (j + 1) * 128].r("(k p) n -> p k n", p=128))
        for m in range(8):
            fw.dma("pool", S.w_down_bf[l, m], I.w_down[l, :, m * 128:(m + 1) * 128].r("(j p) n -> p j n", p=128))
        for src, dst, rows, cols in ((I.w_in, S.w_in_bf, D, INC), (I.w_out, S.w_out_bf, D, D),
                                     (I.w_uq, S.w_uq_bf, 256, 384), (I.w_ukv, S.w_ukv_bf, 128, 512)):
            r0 = 0
            while r0 < rows:
                nr = min(512, rows - r0)
                fw.dma("pool", dst[l, r0:r0 + nr, :], src[l, r0:r0 + nr, :], max_dma_last_dim=4096)
                r0 += nr
    with fw.phase():
        xin = [fw.sb(f"xin{i}", [128, D], F32) for i in range(2)]
        xo = [fw.sb(f"xo{i}", [128, 8, 128], F32) for i in range(2)]
        it = 0
        for b in range(nb):
            for tt_ in range(T // 128):
                xi, xoo = xin[it % 2], xo[it % 2]
                if tt_ < 2:
                    src = I.ctx[b, tt_ * 128:(tt_ + 1) * 128, :]
                else:
                    src = I.x[b, (tt_ - 2) * 128:(tt_ - 1) * 128, :]
                fw.dma("sp" if it % 2 == 0 else "act", xi.v, src)
                for half in range(2):
                    ps = S.P[(it * 2 + half) % 4]
                    for kk in range(4):
                        k = half * 4 + kk
                        fw.op("pe", lambda h: h.transpose(ps.v[:, kk * 128:(kk + 1) * 128].ap,
                                                           xi.v[:, k * 128:(k + 1) * 128].ap, S.G.ident.v.ap),
                              [xi.v, S.G.ident.v], [ps.v])
                    dst = xoo.v[:, half * 4:(half + 1) * 4, :]
                    srcp = ps.v.r("p (k t) -> p k t", k=4)
                    if half == 0:
                        fw.copy(dst, srcp, "dve")
                    else:
                        fw.copy(dst, srcp, "act")
                fw.dma("sp", S.XT[b, :, :, tt_ * 128:(tt_ + 1) * 128], xoo.v)
                it += 1


def phase_mod(S):
    fw, I, G, NV = S.fw, S.I, S.G, S.NV
    with fw.phase():
        cv = fw.sb("cv", [128, 8, NV], F32)
        sv = fw.sb("sv", [128, 8, NV], F32)
        fw.dma("sp", cv.v, I.cvec.v)
        fw.act(sv.v, cv.v, AF.Silu)
        wb = [fw.sb(f"adaw{i}", [128, 8, 768], F32) for i in range(2)]
        for l in range(S.nlay):
            abT = fw.sb(f"abT{l}", [128, 48], F32)
            fw.dma("act", abT.v, I.ada_bT[l])
            gT = fw.sb(f"gT{l}", [128, 2, 8], F32)
            fw.dma("act", gT.v[:, 0, :], I.g1T[l])
            fw.dma("act", gT.v[:, 1, :], I.g2T[l])
            ps = S.P[4 + (l % 2)]
            for blk in range(8):
                w = wb[(l * 8 + blk) % 2]
                fw.dma("sp" if blk % 2 == 0 else "act", w.v,
                       I.ada_w[l, :, blk * 768:(blk + 1) * 768].r("(k p) n -> p k n", p=128))
                for cc in range(6):
                    ch = blk * 6 + cc
                    for k in range(8):
                        fw.mm(ps.v[:, ch * NV:(ch + 1) * NV], w.v[:, k, cc * 128:(cc + 1) * 128], sv.v[:, k, :],
                              start=(k == 0), stop=(k == 7))
            fw.tt(G.mods.v[:, l], ps.v[:, 0:48 * NV].r("p (c v) -> p c v", v=NV), abT.v.bc(2, NV), ALU.add)
            for n, c0 in ((0, 8), (1, 32)):
                tmp = fw.sb(f"modtmp{l}{n}", [128, 8, NV], F32)
                fw.ts(tmp.v, G.mods.v[:, l, c0:c0 + 8, :], 1.0, None, ALU.add)
                fw.tt(G.A.v[:, l, n], tmp.v, gT.v[:, n, :].bc(2, NV), ALU.mult)


TOK_TILES = [(0, 256), (256, 512), (768, 512), (1280, 512), (1792, 512)]
EPS = 1e-6


def nps(S):
    S.pi = getattr(S, "pi", 0) + 1
    return S.P[S.pi % 8]


def rstd_from_sum(S, out, ssum, n, eps, tmp):
    fw = S.fw
    fw.act(tmp, ssum, AF.Ln, scale=1.0 / n, bias=float(eps))
    fw.act(out, tmp, AF.Exp, scale=-0.5)


def phase_A(S, b, l):
    fw, I, G = S.fw, S.I, S.G
    with fw.phase():
        win = fw.sb("win", [128, 8, INC], BF16)
        for k in range(8):
            fw.dma("sp" if k % 2 == 0 else "act", win.v[:, k, :], S.w_in_bf[l, k * 128:(k + 1) * 128, :])
        wuq = fw.sb("wuq", [128, 2, 384], BF16)
        fw.dma("sp", wuq.v, S.w_uq_bf[l].r("(k p) n -> p k n", p=128))
        wukv = fw.sb("wukv", [128, 512], BF16)
        fw.dma("act", wukv.v, S.w_ukv_bf[l])
        qn = fw.sb("qn", [128, 2], F32)
        kvn = fw.sb("kvn", [128, 1], F32)
        fw.dma("sp", qn.v, I.mla_qn[l])
        fw.dma("sp", kvn.v, I.mla_kvn[l])
        tabs = {}
        for nm, rows in (("cos128", 128), ("sin128", 128), ("cos96", 96), ("sin96", 96)):
            tabs[nm] = fw.sb(nm, [rows, TL], F32)
            fw.dma("act", tabs[nm].v, I.c[nm].v)
        R128 = fw.sb("R128", [128, 128], F32)
        R96 = fw.sb("R96", [96, 96], F32)
        fw.dma("sp", R128.v, I.c["R128"].v)
        fw.dma("sp", R96.v, I.c["R96"].v)

        xt = fw.sb("xt", [128, 8, 512], F32)
        sq = fw.sb("sq", [128, 8, 512], BF16)
        xn = fw.sb("xn", [128, 8, 512], F32)
        hT = fw.sb("hT", [128, 8, 512], BF16)
        rstd = fw.sb("rstd", [128, 512], F32)
        lnt = fw.sb("lnt", [128, 512], F32)
        st_na = fw.sb("st_na", [128, 4, 512], BF16)
        st_df = fw.sb("st_df", [128, 4, 512], BF16)
        st_rk = fw.sb("st_rk", [128, 7, 512], F32)
        vts = [fw.sb(f"vt{i}", [128, 4, 128], BF16) for i in range(6)]
        for vt in vts:
            fw.memset(vt.v, 1.0, "pool")
        cq = fw.sb("cq", [128, 3, 512], F32)
        sqm = fw.sb("sqm", [128, 3, 512], BF16)
        rq = fw.sb("rq", [128, 2, 512], F32)
        cqn = fw.sb("cqn", [128, 3, 512], BF16)
        qst = fw.sb("qst", [96, 4, 512], BF16)
        kst = fw.sb("kst", [96, 4, 512], BF16)
        xsb = [fw.sb(f"xsb{i}", [128, 512], F32) for i in range(2)]
        t1s = [fw.sb(f"t1s{i}", [128, 512], F32) for i in range(2)]
        t2s = [fw.sb(f"t2s{i}", [128, 512], F32) for i in range(2)]
        krt = fw.sb("krt", [96, 512], BF16)
        S.rc = 0
        S.vc = 0

        def rope(psv, M, N, tl0, Rm, cosn, sinn, outv):
            i = S.rc % 2
            S.rc += 1
            x, t1, t2 = xsb[i].v[0:M, 0:N], t1s[i].v[0:M, 0:N], t2s[i].v[0:M, 0:N]
            fw.copy(x, psv, "act")
            rp = nps(S).v[0:M, 0:N]
            fw.mm(rp, Rm.v[0:M, 0:M], x)
            fw.tt(t1, x, tabs[cosn].v[0:M, tl0:tl0 + N], ALU.mult)
            fw.tt(t2, rp, tabs[sinn].v[0:M, tl0:tl0 + N], ALU.mult)
            fw.tt(outv, t1, t2, ALU.add, eng="pool")

        for (t0, N) in TOK_TILES:
            lat = t0 >= TC
            vec = b if lat else S.nb
            tl0 = t0 - TC
            fw.dma("sp", xt.v[:, :, 0:N], S.XT[b, :, :, t0:t0 + N])
            fw.act(sq.v[:, :, 0:N], xt.v[:, :, 0:N], AF.Square)
            ps = nps(S)
            for k in range(8):
                fw.mm(ps.v[:, 0:N], G.ones_bf.v, sq.v[:, k, 0:N], start=(k == 0), stop=(k == 7))
            rstd_from_sum(S, rstd.v[:, 0:N], ps.v[:, 0:N], 1024.0, EPS, lnt.v[:, 0:N])
            fw.tt(xn.v[:, :, 0:N], xt.v[:, :, 0:N], rstd.v[:, 0:N].bc(1, 8), ALU.mult)
            for k in range(8):
                fw.act(hT.v[:, k, 0:N], xn.v[:, k, 0:N], AF.Identity, scale=G.A.v[:, l, 0, k, vec:vec + 1],
                       bias=G.mods.v[:, l, 0 + k, vec:vec + 1])

            def zchunk(c0, M):
                p = nps(S)
                for k in range(8):
                    fw.mm(p.v[0:M, 0:N], win.v[:, k, c0:c0 + M], hT.v[:, k, 0:N], start=(k == 0), stop=(k == 7))
                return p.v[0:M, 0:N]

            def vproj(c0, dst):
                for sub in range(N // 128):
                    p = nps(S)
                    for k in range(8):
                        fw.mm(p.v[:, 0:256], hT.v[:, k, sub * 128:(sub + 1) * 128], win.v[:, k, c0:c0 + 256],
                              start=(k == 0), stop=(k == 7))
                    vt = vts[S.vc % 6]
                    S.vc += 1
                    fw.copy(vt.v[:, :, 0:64], p.v[:, 0:256].r("p (h c) -> p h c", h=4), "dve")
                    fw.dma("act", dst[(t0 + sub * 128) // 128], vt.v)

            for c in range(4):
                p = zchunk(C_NA + c * 128, 128)
                fw.copy(st_na.v[:, c, 0:N], p, "act" if c % 2 else "dve")
            fw.dma("sp", S.QK_na[:, :, t0:t0 + N], st_na.v[:, :, 0:N])
            vproj(C_NA + 512, S.V_na)
            for c in range(4):
                p = zchunk(C_DF + c * 128, 128)
                if lat:
                    rope(p, 128, N, tl0, R128, "cos128", "sin128", st_df.v[:, c, 0:N])
                else:
                    fw.copy(st_df.v[:, c, 0:N], p, "act" if c % 2 else "dve")
            fw.dma("sp", S.QK_df[:, :, t0:t0 + N], st_df.v[:, :, 0:N])
            vproj(C_DF + 512, S.V_df)
            for c in range(7):
                p = zchunk(C_RK + c * 128, 128)
                fw.copy(st_rk.v[:, c, 0:N], p, "act" if c % 2 else "dve")
            fw.dma("sp", S.ZT_rk[:, :, t0:t0 + N], st_rk.v[:, :, 0:N])
            for c in range(3):
                p = zchunk(C_MLA + c * 128, 128)
                fw.copy(cq.v[:, c, 0:N], p, "act" if c % 2 else "dve")
            fw.act(sqm.v[:, :, 0:N], cq.v[:, :, 0:N], AF.Square)
            pq, pk = nps(S), nps(S)
            fw.mm(pq.v[:, 0:N], G.ones_bf.v, sqm.v[:, 0, 0:N], start=True, stop=False)
            fw.mm(pq.v[:, 0:N], G.ones_bf.v, sqm.v[:, 1, 0:N], start=False, stop=True)
            fw.mm(pk.v[:, 0:N], G.ones_bf.v, sqm.v[:, 2, 0:N])
            rstd_from_sum(S, rq.v[:, 0, 0:N], pq.v[:, 0:N], 256.0, EPS, lnt.v[:, 0:N])
            rstd_from_sum(S, rq.v[:, 1, 0:N], pk.v[:, 0:N], 128.0, EPS, lnt.v[:, 0:N])
            for j in range(3):
                sc = qn.v[:, j:j + 1] if j < 2 else kvn.v[:, 0:1]
                rr = rq.v[:, 0, 0:N] if j < 2 else rq.v[:, 1, 0:N]
                fw.stt(cqn.v[:, j, 0:N], cq.v[:, j, 0:N], sc, rr, ALU.mult, ALU.mult)
            for h in range(4):
                p = nps(S).v[0:96, 0:N]
                for j in range(2):
                    fw.mm(p, wuq.v[:, j, h * 96:(h + 1) * 96], cqn.v[:, j, 0:N], start=(j == 0), stop=(j == 1))
                if lat:
                    rope(p, 96, N, tl0, R96, "cos96", "sin96", qst.v[:, h, 0:N])
                else:
                    fw.copy(qst.v[:, h, 0:N], p, "act")
                p2 = nps(S).v[0:64, 0:N]
                fw.mm(p2, wukv.v[:, h * 128:h * 128 + 64], cqn.v[:, 2, 0:N])
                fw.copy(kst.v[0:64, h, 0:N], p2, "dve")
            p = zchunk(C_MLA + 320, 96)
            if lat:
                rope(p, 96, N, tl0, R96, "cos96", "sin96", krt.v[:, 0:N])
            else:
                fw.copy(krt.v[:, 0:N], p, "act")
            for h in range(4):
                fw.copy(kst.v[64:96, h, 0:N], krt.v[64:96, 0:N], "pool")
            fw.dma("sp", S.QT_mla[:, :, t0:t0 + N], qst.v[:, :, 0:N])
            fw.dma("sp", S.KT_mla[:, :, t0:t0 + N], kst.v[:, :, 0:N])
            for sub in range(N // 128):
                pv = nps(S)
                fw.mm(pv.v[:, 0:256].r("p (h c) -> p h c", h=4), cqn.v[:, 2, sub * 128:(sub + 1) * 128],
                      wukv.v.r("p (h c) -> p h c", h=4)[:, :, 64:128])
                vt = vts[S.vc % 6]
                S.vc += 1
                fw.copy(vt.v[:, :, 0:64], pv.v[:, 0:256].r("p (h c) -> p h c", h=4), "dve")
                fw.dma("act", S.V_mla[(t0 + sub * 128) // 128], vt.v)


def _qtiles(need_ctx):
    qt = [(TC + i * 512, 512, 18) for i in range(4)]
    if need_ctx:
        qt.append((0, 256, 2))
    return qt


def _normalize(S, fw, O, NQ, osb, lnr, rec, outv, bcp, eng="pool"):
    G = S.G
    fw.copy(osb.v[0:65, 0:NQ], O.v[0:65, 0:NQ], "dve")
    fw.mm(bcp.v[0:64, 0:NQ], G.esel.v[0:65, 0:64], osb.v[0:65, 0:NQ])
    fw.act(lnr.v[0:64, 0:NQ], bcp.v[0:64, 0:NQ], AF.Ln)
    fw.act(rec.v[0:64, 0:NQ], lnr.v[0:64, 0:NQ], AF.Exp, scale=-1.0)
    fw.tt(outv, osb.v[0:64, 0:NQ], rec.v[0:64, 0:NQ], ALU.mult, eng=eng)


def phase_mla(S, b, l, need_ctx):
    fw, I, G, P = S.fw, S.I, S.G, S.P
    scale = 96.0 ** -0.5
    with fw.phase():
        KT = fw.sb("KT", [128, 4, T], BF16)
        QT = fw.sb("QT", [128, 4, T], BF16)
        Vv = fw.sb("Vv", [128, 18, 4, 128], BF16)
        fw.memset(KT.v, 0.0)
        fw.memset(QT.v, 0.0, "pool")
        fw.dma("sp", KT.v[0:96], S.KT_mla.v)
        fw.dma("act", QT.v[0:96], S.QT_mla.v)
        fw.dma("sp", Vv.v, S.V_mla.v.r("j p h c -> p j h c"))
        pts = [fw.sb(f"pt{i}", [128, 512], BF16) for i in range(3)]
        osb = fw.sb("osb", [65, 512], F32)
        lnr = fw.sb("lnr", [64, 512], F32)
        rec = fw.sb("rec", [64, 512], F32)
        osts = [fw.sb(f"ost{i}", [64, 4, 512], BF16) for i in range(2)]
        cnt = 0
        for qi, (q0, NQ, nk) in enumerate(_qtiles(need_ctx)):
            ost = osts[qi % 2]
            for h in range(4):
                O = P[4 + (h % 2)]
                for j in range(nk):
                    sp = P[cnt % 4]
                    pt = pts[cnt % 3]
                    cnt += 1
                    fw.mm(sp.v[:, 0:NQ], KT.v[:, h, j * 128:(j + 1) * 128], QT.v[:, h, q0:q0 + NQ])
                    fw.act(pt.v[:, 0:NQ], sp.v[:, 0:NQ], AF.Exp, scale=scale)
                    fw.mm(O.v[:, 0:NQ], Vv.v[:, j, h, :], pt.v[:, 0:NQ], start=(j == 0), stop=(j == nk - 1))
                _normalize(S, fw, O, NQ, osb, lnr, rec, ost.v[:, h, 0:NQ], P[6 + (h % 2)])
            for h in range(4):
                hb = (h % 2) * 64
                fw.dma("sp" if h % 2 else "act", S.MIXT[hb:hb + 64, 2 + h // 2, q0:q0 + NQ], ost.v[:, h, 0:NQ])


def phase_df(S, b, l, need_ctx):
    fw, I, G, P = S.fw, S.I, S.G, S.P
    import math
    scale = 32.0 ** -0.5
    lam_init = 0.8 - 0.6 * math.exp(-0.3 * l)
    with fw.phase():
        Kd = fw.sb("Kd", [128, 2, T], BF16)
        Qd = fw.sb("Qd", [128, 2, T], BF16)
        fw.dma("sp", Qd.v, S.QK_df[:, 0:2, :])
        fw.dma("act", Kd.v, S.QK_df[:, 2:4, :])
        gm = fw.sb("gm", [128, 4], F32)
        fw.dma("sp", gm.v, I.c["df_gm"].v)
        Qm = fw.sb("Qm", [128, 2, 4, T], BF16)
        for c in range(2):
            for g in range(4):
                if g % 2:
                    fw.ts(Qm.v[:, c, g, :], Qd.v[:, c, :], gm.v[:, g:g + 1], None, ALU.mult)
                else:
                    fw.act(Qm.v[:, c, g, :], Qd.v[:, c, :], AF.Identity, scale=gm.v[:, g:g + 1])
        Vv = fw.sb("Vv", [128, 18, 4, 128], BF16)
        fw.dma("sp", Vv.v, S.V_df.v.r("j p h c -> p j h c"))
        dl = fw.sb("dl", [64, 128], F32)
        fw.dma("act", dl.v, I.df_lam[l].f(lambda a: a.broadcast_to([64, 128])))
        pr = fw.sb("pr", [64, 2, 32], F32)
        fw.tt(pr.v, dl.v.r("p (a b d) -> p a b d", a=2, b=2)[:, :, 0, :], dl.v.r("p (a b d) -> p a b d", a=2, b=2)[:, :, 1, :],
              ALU.mult)
        ss = fw.sb("ss", [64, 2], F32)
        fw.reduce(ss.v, pr.v)
        ee = fw.sb("ee", [64, 2], F32)
        fw.act(ee.v, ss.v, AF.Exp)
        nlam = fw.sb("nlam", [64, 1], F32)
        fw.tt(nlam.v, ee.v[:, 1:2], ee.v[:, 0:1], ALU.subtract)
        fw.ts(nlam.v, nlam.v, -lam_init, None, ALU.add)
        sub = fw.sb("sub", [64, 1], F32)
        fw.dma("act", sub.v, I.df_sub[l])
        fw.ts(sub.v, sub.v, 1.0 - lam_init, None, ALU.mult)
        pts = [fw.sb(f"pt{i}", [128, 512], BF16) for i in range(3)]
        osb = [fw.sb(f"osb{i}", [65, 512], F32) for i in range(2)]
        lnr = fw.sb("lnr", [64, 512], F32)
        rec = fw.sb("rec", [64, 512], F32)
        o01 = [fw.sb(f"o01{i}", [64, 512], F32) for i in range(2)]
        oo = fw.sb("oo", [64, 512], F32)
        sqo = fw.sb("sqo", [64, 512], F32)
        osts = [fw.sb(f"ost{i}", [64, 4, 512], BF16) for i in range(2)]
        cnt = 0
        for qi, (q0, NQ, nk) in enumerate(_qtiles(need_ctx)):
            ost = osts[qi % 2]
            for h in range(4):
                c = h // 2
                for j in range(nk):
                    for n in range(2):
                        g = (h % 2) * 2 + n
                        sp = P[cnt % 4]
                        pt = pts[cnt % 3]
                        cnt += 1
                        fw.mm(sp.v[:, 0:NQ], Kd.v[:, c, j * 128:(j + 1) * 128], Qm.v[:, c, g, q0:q0 + NQ])
                        fw.act(pt.v[:, 0:NQ], sp.v[:, 0:NQ], AF.Exp, scale=scale)
                        fw.mm(P[4 + n].v[:, 0:NQ], Vv.v[:, j, h, :], pt.v[:, 0:NQ], start=(j == 0), stop=(j == nk - 1))
                for n in range(2):
                    _normalize(S, fw, P[4 + n], NQ, osb[n], lnr, rec, o01[n].v[:, 0:NQ], P[6 + n], eng="dve")
                fw.stt(oo.v[:, 0:NQ], o01[1].v[:, 0:NQ], nlam.v[:, 0:1], o01[0].v[:, 0:NQ], ALU.mult, ALU.add)
                fw.act(sqo.v[:, 0:NQ], oo.v[:, 0:NQ], AF.Square)
                fw.mm(P[6].v[0:64, 0:NQ], G.blockones.v[0:64, 0:64], sqo.v[:, 0:NQ])
                rstd_from_sum(S, rec.v[:, 0:NQ], P[6].v[0:64, 0:NQ], 64.0, 1e-5, lnr.v[:, 0:NQ])
                fw.stt(ost.v[:, h, 0:NQ], oo.v[:, 0:NQ], sub.v[:, 0:1], rec.v[:, 0:NQ], ALU.mult, ALU.mult)
            for h in range(4):
                hb = (h % 2) * 64
                fw.dma("sp" if h % 2 else "act", S.MIXT[hb:hb + 64, 6 + h // 2, q0:q0 + NQ], ost.v[:, h, 0:NQ])


def _na_rs(r):
    return min(max(r - 4, 0), 24)


def phase_na(S, b, l, need_ctx):
    fw, I, G, P = S.fw, S.I, S.G, S.P
    scale = 0.125
    with fw.phase():
        QK = fw.sb("QKn", [128, 4, T], BF16)
        fw.dma("sp", QK.v, S.QK_na.v)
        Vv = fw.sb("Vv", [128, 18, 4, 128], BF16)
        fw.dma("act", Vv.v, S.V_na.v.r("j p h c -> p j h c"))
        E = fw.sb("E", [128, 4, 15, 64], F32)
        msk = fw.sb("msk", [128, 64], F32)
        for hh in range(2):
            fw.dma("sp", E.v[hh * 64:(hh + 1) * 64], I.na_bias[l])
            fw.dma("act", msk.v[hh * 64:(hh + 1) * 64], I.c["na_mask"].v)
        Ef = E.v.r("p h d q -> p (h d) q")
        fw.act(Ef, Ef, AF.Exp)
        fw.tt(Ef, Ef, msk.v.bc(1, 60), ALU.mult)
        pts = [fw.sb(f"pt{i}", [128, 512], BF16) for i in range(3)]
        ptf = [fw.sb(f"ptf{i}", [128, 512], F32) for i in range(2)]
        osb = fw.sb("osb", [65, 512], F32)
        osm = fw.sb("osm", [65, 512], F32)
        lnr = fw.sb("lnr", [64, 512], F32)
        rec = fw.sb("rec", [64, 512], F32)
        osts = [fw.sb(f"ost{i}", [64, 4, 512], BF16) for i in range(2)]
        cnt = 0
        qtiles = [(i * 8, TC + i * 512, 512) for i in range(4)]
        if need_ctx:
            qtiles.append((None, 0, 256))
        for qi, (r0, q0, NQ) in enumerate(qtiles):
            ost = osts[qi % 2]
            work = []
            for i in range(4):
                work.append(((i % 2) * 64, i // 2, i * 64, 0, NQ, None))
            if r0 is not None:
                for kr in range(_na_rs(r0), _na_rs(r0 + 7) + 8):
                    rr = [r for r in range(r0, r0 + 8) if _na_rs(r) <= kr < _na_rs(r) + 8]
                    if not rr:
                        continue
                    ra, rb = rr[0], rr[-1]
                    work.append(((kr % 2) * 64, 2 + kr // 2, TC + kr * 64, (ra - r0) * 64, (rb - r0 + 1) * 64,
                                 (ra - kr + 7, rb - kr + 8)))
            last = {}
            for wi, w in enumerate(work):
                last[w[0]] = wi
            for h in range(4):
                hb = (h % 2) * 64
                cq, ck = h // 2, 2 + h // 2
                started = {0: False, 64: False}
                for wi, (kb, vtile, ktok, ca, cb, eidx) in enumerate(work):
                    sp = P[cnt % 4]
                    pt = pts[cnt % 3]
                    pf = ptf[cnt % 2]
                    cnt += 1
                    acc = P[4] if kb == 0 else P[5]
                    fw.mm(sp.v[kb:kb + 64, ca:cb], QK.v[hb:hb + 64, ck, ktok:ktok + 64],
                          QK.v[hb:hb + 64, cq, q0 + ca:q0 + cb])
                    if eidx is None:
                        fw.act(pt.v[kb:kb + 64, ca:cb], sp.v[kb:kb + 64, ca:cb], AF.Exp, scale=scale)
                    else:
                        fw.act(pf.v[kb:kb + 64, ca:cb], sp.v[kb:kb + 64, ca:cb], AF.Exp, scale=scale)
                        ev = E.v[kb:kb + 64, h, eidx[0]:eidx[1], :]
                        fw.tt(pt.v[kb:kb + 64, ca:cb].r("p (r q) -> p r q", q=64), pf.v[kb:kb + 64, ca:cb].r("p (r q) -> p r q", q=64),
                              ev, ALU.mult)
                    fw.mm(acc.v[:, ca:cb], Vv.v[kb:kb + 64, vtile, h, :], pt.v[kb:kb + 64, ca:cb],
                          start=(not started[kb]), stop=(last[kb] == wi))
                    started[kb] = True
                fw.copy(osb.v[0:65, 0:NQ], P[4].v[0:65, 0:NQ], "dve")
                fw.tt(osm.v[0:65, 0:NQ], osb.v[0:65, 0:NQ], P[5].v[0:65, 0:NQ], ALU.add)
                fw.mm(P[6].v[0:64, 0:NQ], G.esel.v[0:65, 0:64], osm.v[0:65, 0:NQ])
                fw.act(lnr.v[0:64, 0:NQ], P[6].v[0:64, 0:NQ], AF.Ln)
                fw.act(rec.v[0:64, 0:NQ], lnr.v[0:64, 0:NQ], AF.Exp, scale=-1.0)
                fw.tt(ost.v[:, h, 0:NQ], osm.v[0:64, 0:NQ], rec.v[0:64, 0:NQ], ALU.mult, eng="pool")
            for h in range(4):
                hb = (h % 2) * 64
                fw.dma("sp" if h % 2 else "act", S.MIXT[hb:hb + 64, 0 + h // 2, q0:q0 + NQ], ost.v[:, h, 0:NQ])


def phase_O(S, b, l, need_ctx):
    fw, I, G = S.fw, S.I, S.G
    with fw.phase():
        wo = fw.sb("wo", [128, 8, D], BF16)
        fw.dma("sp", wo.v, S.w_out_bf[l].r("(k p) n -> p k n", p=128))
        mxs = [fw.sb(f"mx{i}", [128, 8, 512], BF16) for i in range(2)]
        xts = [fw.sb(f"xto{i}", [128, 8, 512], F32) for i in range(2)]
        tiles = TOK_TILES[1:] + ([TOK_TILES[0]] if need_ctx else [])
        for ti, (t0, N) in enumerate(tiles):
            vec = b if t0 >= TC else S.nb
            mx, xt = mxs[ti % 2], xts[ti % 2]
            fw.dma("act", mx.v[:, :, 0:N], S.MIXT[:, :, t0:t0 + N])
            fw.dma("sp", xt.v[:, :, 0:N], S.XT[b, :, :, t0:t0 + N])
            for m in range(8):
                ps = nps(S)
                for k in range(8):
                    fw.mm(ps.v[:, 0:N], wo.v[:, k, m * 128:(m + 1) * 128], mx.v[:, k, 0:N], start=(k == 0), stop=(k == 7))
                fw.stt(xt.v[:, m, 0:N], ps.v[:, 0:N], G.mods.v[:, l, 16 + m, vec:vec + 1], xt.v[:, m, 0:N],
                       ALU.mult, ALU.add)
            fw.dma("sp", S.XT[b, :, :, t0:t0 + N], xt.v[:, :, 0:N])


def phase_mlp(S, b, l, need_ctx):
    fw, I, G, P = S.fw, S.I, S.G, S.P
    with fw.phase():
        cw = fw.sb("cw", [128, 3, 22], F32)
        cb = fw.sb("cb", [128, 22], F32)
        fw.dma("act", cw.v, I.conv_w[l])
        fw.dma("act", cb.v, I.conv_b[l])
        hwl = fw.sb("hwl", [128, 8, TL + 2], BF16)
        fw.memset(hwl.v[:, :, 0:1], 0.0, "pool")
        fw.memset(hwl.v[:, :, TL + 1:TL + 2], 0.0, "pool")
        if need_ctx:
            hwc = fw.sb("hwc", [128, 8, TC + 2], BF16)
            fw.memset(hwc.v[:, :, 0:1], 0.0, "pool")
            fw.memset(hwc.v[:, :, TC + 1:TC + 2], 0.0, "pool")
        tiles = TOK_TILES[1:] + ([TOK_TILES[0]] if need_ctx else [])
        with fw.phase():
            xw = fw.sb("xw", [128, 8, 512], F32)
            sq = fw.sb("sqw", [128, 8, 512], BF16)
            xn = fw.sb("xnw", [128, 8, 512], F32)
            rstd = fw.sb("rstdw", [128, 512], F32)
            lnt = fw.sb("lntw", [128, 512], F32)
            for (t0, n) in tiles:
                lat = t0 >= TC
                vec = b if lat else S.nb
                fw.dma("sp", xw.v[:, :, 0:n], S.XT[b, :, :, t0:t0 + n])
                fw.act(sq.v[:, :, 0:n], xw.v[:, :, 0:n], AF.Square)
                ps = nps(S)
                for k in range(8):
                    fw.mm(ps.v[:, 0:n], G.ones_bf.v, sq.v[:, k, 0:n], start=(k == 0), stop=(k == 7))
                rstd_from_sum(S, rstd.v[:, 0:n], ps.v[:, 0:n], 1024.0, EPS, lnt.v[:, 0:n])
                fw.tt(xn.v[:, :, 0:n], xw.v[:, :, 0:n], rstd.v[:, 0:n].bc(1, 8), ALU.mult)
                for k in range(8):
                    dst = hwl.v[:, k, 1 + t0 - TC:1 + t0 - TC + n] if lat else hwc.v[:, k, 1:1 + n]
                    fw.act(dst, xn.v[:, k, 0:n], AF.Identity, scale=G.A.v[:, l, 1, k, vec:vec + 1],
                           bias=G.mods.v[:, l, 24 + k, vec:vec + 1])
        gT = fw.sb("gT", [128, 22, TL], BF16)
        asb = fw.sb("asb", [128, TL + 2], F32)
        fw.memset(asb.v[:, 0:1], 0.0, "pool")
        fw.memset(asb.v[:, TL + 1:TL + 2], 0.0, "pool")
        c1 = fw.sb("c1", [128, TL], F32)
        c2 = fw.sb("c2", [128, TL], F32)
        if need_ctx:
            gTc = fw.sb("gTc", [128, 22, TC], BF16)
            asc = fw.sb("asc", [128, TC + 2], F32)
            fw.memset(asc.v[:, 0:1], 0.0, "pool")
            fw.memset(asc.v[:, TC + 1:TC + 2], 0.0, "pool")
            c1c = fw.sb("c1c", [128, TC], F32)
            c2c = fw.sb("c2c", [128, TC], F32)
        wus = [fw.sb(f"wu{i}", [128, 2, 8, 128], BF16) for i in range(2)]
        for j in range(22):
            wu = wus[j % 2]
            fw.dma("sp" if j % 2 else "act", wu.v, S.w_up_bf[l, j])
            for i in range(4):
                for k in range(8):
                    fw.mm(P[i].v, wu.v[:, 0, k, :], hwl.v[:, k, 1 + i * 512:1 + (i + 1) * 512], start=(k == 0), stop=(k == 7))
            for i in range(4):
                for k in range(8):
                    fw.mm(P[4 + i].v, wu.v[:, 1, k, :], hwl.v[:, k, 1 + i * 512:1 + (i + 1) * 512], start=(k == 0), stop=(k == 7))
            for i in range(4):
                fw.copy(asb.v[:, 1 + i * 512:1 + (i + 1) * 512], P[i].v, "act")
            fw.act(c1.v, asb.v[:, 0:TL], AF.Identity, scale=cw.v[:, 0, j:j + 1])
            fw.stt(c2.v, asb.v[:, 1:TL + 1], cw.v[:, 1, j:j + 1], c1.v, ALU.mult, ALU.add)
            fw.stt(c1.v, asb.v[:, 2:TL + 2], cw.v[:, 2, j:j + 1], c2.v, ALU.mult, ALU.add)
            fw.act(c2.v, c1.v, AF.Silu, bias=cb.v[:, j:j + 1])
            for i in range(4):
                fw.tt(gT.v[:, j, i * 512:(i + 1) * 512], c2.v[:, i * 512:(i + 1) * 512], P[4 + i].v, ALU.mult)
            if need_ctx:
                pc = P[0]
                for k in range(8):
                    fw.mm(pc.v[:, 0:TC], wu.v[:, 0, k, :], hwc.v[:, k, 1:TC + 1], start=(k == 0), stop=(k == 7))
                for k in range(8):
                    fw.mm(pc.v[:, TC:2 * TC], wu.v[:, 1, k, :], hwc.v[:, k, 1:TC + 1], start=(k == 0), stop=(k == 7))
                fw.copy(asc.v[:, 1:TC + 1], pc.v[:, 0:TC], "act")
                fw.act(c1c.v, asc.v[:, 0:TC], AF.Identity, scale=cw.v[:, 0, j:j + 1])
                fw.stt(c2c.v, asc.v[:, 1:TC + 1], cw.v[:, 1, j:j + 1], c1c.v, ALU.mult, ALU.add)
                fw.stt(c1c.v, asc.v[:, 2:TC + 2], cw.v[:, 2, j:j + 1], c2c.v, ALU.mult, ALU.add)
                fw.act(c2c.v, c1c.v, AF.Silu, bias=cb.v[:, j:j + 1])
                fw.tt(gTc.v[:, j, :], c2c.v, pc.v[:, TC:2 * TC], ALU.mult)
        wds = [fw.sb(f"wd{i}", [128, 22, 128], BF16) for i in range(2)]
        xt = fw.sb("xtm", [128, 8, 512], F32)
        it = 0
        for (t0, n) in tiles:
            lat = t0 >= TC
            vec = b if lat else S.nb
            fw.dma("act", xt.v[:, :, 0:n], S.XT[b, :, :, t0:t0 + n])
            for m in range(8):
                wd = wds[it % 2]
                it += 1
                fw.dma("sp", wd.v, S.w_down_bf[l, m])
                ps = nps(S)
                for j in range(22):
                    rhs = gT.v[:, j, t0 - TC:t0 - TC + n] if lat else gTc.v[:, j, 0:n]
                    fw.mm(ps.v[:, 0:n], wd.v[:, j, :], rhs, start=(j == 0), stop=(j == 21))
                fw.stt(xt.v[:, m, 0:n], ps.v[:, 0:n], G.mods.v[:, l, 40 + m, vec:vec + 1], xt.v[:, m, 0:n],
                       ALU.mult, ALU.add)
            fw.dma("sp", S.XT[b, :, :, t0:t0 + n], xt.v[:, :, 0:n])


def phase_final(S):
    fw, I, G = S.fw, S.I, S.G
    with fw.phase():
        gf = fw.sb("gf", [128, 8], F32)
        fw.dma("sp", gf.v, I.gfT.v)
        xts = [fw.sb(f"xf{i}", [128, 8, 128], F32) for i in range(2)]
        sq = fw.sb("sqf", [128, 8, 128], BF16)
        xn = fw.sb("xnf", [128, 8, 128], F32)
        rstd = fw.sb("rstdf", [128, 128], F32)
        lnt = fw.sb("lntf", [128, 128], F32)
        ots = [fw.sb(f"of{i}", [128, D], F32) for i in range(2)]
        it = 0
        for b in range(S.nb):
            for tt_ in range(TL // 128):
                xt, ot = xts[it % 2], ots[it % 2]
                it += 1
                t0 = TC + tt_ * 128
                fw.dma("sp", xt.v, S.XT[b, :, :, t0:t0 + 128])
                fw.act(sq.v, xt.v, AF.Square)
                ps = nps(S)
                for k in range(8):
                    fw.mm(ps.v[:, 0:128], G.ones_bf.v, sq.v[:, k, :], start=(k == 0), stop=(k == 7))
                rstd_from_sum(S, rstd.v, ps.v[:, 0:128], 1024.0, EPS, lnt.v)
                fw.tt(xn.v, xt.v, rstd.v.bc(1, 8), ALU.mult)
                fw.tt(xn.v, xn.v, gf.v.bc(2, 128), ALU.mult)
                for half in range(2):
                    p = nps(S)
                    for kk in range(4):
                        k = half * 4 + kk
                        fw.op("pe", lambda h: h.transpose(p.v[:, kk * 128:(kk + 1) * 128].ap, xn.v[:, k, :].ap,
                                                           G.ident.v.ap), [xn.v, G.ident.v], [p.v])
                    fw.copy(ot.v[:, half * 512:(half + 1) * 512], p.v, "act" if half else "dve")
                fw.dma("act", S.out[b, tt_ * 128:(tt_ + 1) * 128, :], ot.v)


RD = BF16
LDK = 0.6065306597126334
RK_EPS = 64e-5


def phase_rk(S, b, l, need_ctx):
    fw, I, G, P = S.fw, S.I, S.G, S.P
    SEG = 256
    NSEG = T // SEG
    with fw.phase():
        mu = fw.sb("mu", [128, 2, 7], F32)
        fw.dma("sp", mu.v, I.rk_mu[l])
        c0 = fw.sb("c0", [128, 7], F32)
        fw.tt(c0.v, mu.v[:, 0, :], mu.v[:, 1, :], ALU.add)
        fw.ts(c0.v, c0.v, -1.0, 1.0, ALU.mult, ALU.add)
        w0 = fw.sb("w0", [128, 2, 2], F32)
        a0 = fw.sb("a0", [128, 2, 2], F32)
        fw.dma("sp", w0.v, I.rk_w0[l])
        fw.dma("sp", a0.v, I.rk_a0[l])
        wup = fw.sb("wup", [32, 2, 256], F32)
        fw.dma("act", wup.v, I.rk_wup[l].r("d r c -> r d c"))
        aup = fw.sb("aup", [64, 2, 256], F32)
        fw.dma("act", aup.v[32:64], I.rk_aup[l].r("d r c -> r d c"))
        gup = fw.sb("gup", [128, 256], F32)
        fw.dma("act", gup.v[64:128], I.rk_gup[l])
        kkp = fw.sb("kkp", [128, 2], F32)
        ka = fw.sb("ka", [128, 2], F32)
        omka = fw.sb("omka", [128, 2], F32)
        rkp = fw.sb("rkp", [128, 2], F32)
        fw.dma("sp", kkp.v, I.rk_kk[l])
        fw.dma("sp", ka.v, I.rk_ka[l])
        fw.dma("sp", rkp.v, I.rk_rk[l])
        fw.ts(omka.v, ka.v, -1.0, 1.0, ALU.mult, ALU.add)
        lnw = fw.sb("lnw", [128, 2, 64], F32)
        lnb = fw.sb("lnb", [128, 2, 64], F32)
        for hh in range(2):
            fw.dma("sp", lnw.v[hh * 64:(hh + 1) * 64],
                   I.rk_lnw[l:l + 1, :].r("o (hp hh i) -> o hp hh i", hp=2, hh=2)[:, :, hh, :].f(
                       lambda a: a.broadcast_to([64, 2, 64])))
            fw.dma("act", lnb.v[hh * 64:(hh + 1) * 64],
                   I.rk_lnb[l:l + 1, :].r("o (hp hh i) -> o hp hh i", hp=2, hh=2)[:, :, hh, :].f(
                       lambda a: a.broadcast_to([64, 2, 64])))
        eye = fw.sb("eye", [128, 64], F32)
        fw.dma("sp", eye.v, I.c["rk_eye"].v)
        mkk = fw.sb("mkk", [128, 2, 2, 64], F32)
        fw.dma("sp", mkk.v, I.c["rk_mask"].v)
        bm = fw.sb("bm", [128, 2, 64], F32)
        fw.memset(bm.v, 0.0)
        fw.memset(bm.v[0:64, 0, :], 1.0)
        fw.memset(bm.v[64:128, 1, :], 1.0)
        mbd = fw.sb("mbd", [128, 2, 2, 2, 64], F32)
        for d in range(2):
            fw.tt(mbd.v[:, d], mkk.v[:, d].bc(2, 2), bm.v.bc(1, 2), ALU.mult)
        rm = fw.sb("rm", [128, 2, 512], F32)
        fw.dma("act", rm.v, I.c["rk_rm"].v)
        bg = fw.sb("bg", [128, 2, 2, T], BF16)
        Yacc = fw.sb("Yacc", [128, 36, 2, 64], F32)
        fw.memset(Yacc.v, 0.0, "pool")
        ST = fw.sb("ST", [128, 2, 2, 64], F32)
        fw.memset(ST.v, 0.0)
        if RD == F32:
            STb, identR = ST, G.ident
        else:
            STb = fw.sb("STb", [128, 2, 2, 64], RD)
            fw.memset(STb.v, 0.0)
            identR = fw.sb("identR", [128, 128], RD)
            fw.copy(identR.v, G.ident.v)

        with fw.phase():
            zw = fw.sb("zw", [128, 7, SEG + 2], F32)
            zs = fw.sb("zs", [128, 7, SEG], F32)
            zt = fw.sb("zt", [128, 7, SEG], F32)
            kk = fw.sb("kk", [128, 2, SEG], F32)
            kq = fw.sb("kq", [128, 2, SEG], F32)
            kkn = fw.sb("kkn", [128, 2, SEG], F32)
            lnk = fw.sb("lnk", [128, 2, SEG], F32)
            twd = fw.sb("twd", [32, SEG], F32)
            sgd = fw.sb("sgd", [128, SEG], F32)
            vbd = fw.sb("vbd", [128, 2, 4, 2, 64], F32)
            X = NS()
            for nm in ("sig", "asig", "Ls", "ta", "tb", "Bt", "Kt", "Bh", "Kh"):
                setattr(X, nm, fw.sb(nm, [128, 2, SEG], F32))
            X.Btr = X.Bt if RD == F32 else fw.sb("Btr", [128, 2, SEG], RD)
            X.Ktr = X.Kt if RD == F32 else fw.sb("Ktr", [128, 2, SEG], RD)
            X.AR = fw.sb("AR", [128, 2, 4, 2, 64], RD)
            X.Btbd = fw.sb("Btbd", [128, 2, 4, 2, 64], RD)
            X.Ktbd = fw.sb("Ktbd", [128, 2, 4, 2, 64], RD)
            X.Bhbd = fw.sb("Bhbd", [128, 2, 4, 2, 64], RD)
            X.Khbd = fw.sb("Khbd", [128, 2, 4, 2, 64], RD)
            X.Nn = [[fw.sb(f"Nn{h}{i}", [128, 4, 128], RD) for i in range(2)] for h in range(2)]
            X.NT = [[fw.sb(f"NT{h}{i}", [128, 4, 128], RD) for i in range(2)] for h in range(2)]
            X.Xx = [[fw.sb(f"Xx{h}{i}", [128, 4, 128], RD) for i in range(2)] for h in range(2)]
            X.LC = fw.sb("LC", [128, 2, 4], F32)
            SX = []
            for i in range(2):
                Y = NS()
                Y.ARbd = fw.sb(f"ARbd{i}", [128, 2, 4, 2, 2, 64], RD)
                Y.BhT = fw.sb(f"BhT{i}", [128, 2, 4, 128], RD)
                Y.KhT = fw.sb(f"KhT{i}", [128, 2, 4, 128], RD)
                Y.MB = fw.sb(f"MB{i}", [128, 2, 4, 2, 128], RD)
                Y.MK = fw.sb(f"MK{i}", [128, 2, 4, 2, 128], RD)
                Y.TT = fw.sb(f"TT{i}", [128, 2, 4, 128], RD)
                Y.PC = fw.sb(f"PC{i}", [128, 2, 4], F32)
                Y.VT = fw.sb(f"VT{i}", [128, 2, 4, 64], RD)
                SX.append(Y)
            RHSsb = fw.sb("RHSsb", [128, 2, 64], RD)
            Usb = fw.sb("Usb", [128, 2, 64], RD)

            def c4(v):
                return v.r("p a (c s) -> p a c s", s=64)

            def prep(seg, d, Y):
                t0 = seg * SEG
                s0, s1 = (0, TC) if seg == 0 else (TC, T)
                w0_, w1_ = max(t0 - 1, s0), min(t0 + SEG + 1, s1)
                cc, n = w0_ - (t0 - 1), w1_ - w0_
                if n < SEG + 2:
                    fw.memset(zw.v, 0.0, "pool")
                fw.dma("sp", zw.v[:, :, cc:cc + n], S.ZT_rk[:, :, w0_:w1_])
                fw.tt(zs.v, zw.v[:, :, 1:SEG + 1], c0.v.bc(2, SEG), ALU.mult)
                fw.tt(zt.v, zw.v[:, :, 0:SEG], mu.v[:, 0, :].bc(2, SEG), ALU.mult, eng="pool")
                fw.tt(zs.v, zs.v, zt.v, ALU.add)
                fw.tt(zt.v, zw.v[:, :, 2:SEG + 2], mu.v[:, 1, :].bc(2, SEG), ALU.mult, eng="pool")
                fw.tt(zs.v, zs.v, zt.v, ALU.add)
                yield
                r_, k_, v_ = zs.v[:, 0:2, :], zs.v[:, 2:4, :], zs.v[:, 4:6, :]
                fw.act(twd.v, zs.v[0:32, 6, :], AF.Tanh)
                fw.act(sgd.v[64:128], zs.v[64:128, 6, :], AF.Sigmoid)
                fw.tt(kk.v, k_, kkp.v.bc(2, SEG), ALU.mult)
                fw.tt(kq.v, kk.v, kk.v, ALU.mult, eng="pool")
                ps = P[0]
                fw.mm(ps.v[:, 0:2 * SEG], G.blockones.v, kq.v.r("p a t -> p (a t)"))
                fw.ts(lnk.v.r("p a t -> p (a t)"), ps.v[:, 0:2 * SEG], 1e-24, None, ALU.max)
                fw.act(lnk.v, lnk.v, AF.Ln)
                fw.act(lnk.v, lnk.v, AF.Exp, scale=-0.5)
                fw.tt(kkn.v, kk.v, lnk.v, ALU.mult)
                yield
                fw.tt(kq.v, r_, k_, ALU.mult, eng="pool")
                fw.tt(kq.v, kq.v, rkp.v.bc(2, SEG), ALU.mult, eng="pool")
                ps = P[1]
                fw.mm(ps.v[:, 0:2 * SEG], G.blockones.v, kq.v.r("p a t -> p (a t)"))
                fw.tt(bg.v[:, :, 0, t0:t0 + SEG], ps.v[:, 0:2 * SEG].r("p (a t) -> p a t", a=2), v_, ALU.mult)
                ps = P[2]
                for hp in range(2):
                    fw.mm(ps.v[:, hp * SEG:(hp + 1) * SEG], gup.v[64:128, hp * 128:(hp + 1) * 128], sgd.v[64:128, :])
                fw.copy(bg.v[:, :, 1, t0:t0 + SEG], ps.v[:, 0:2 * SEG].r("p (a t) -> p a t", a=2), "act")
                for hp in range(2):
                    fw.tt(vbd.v[:, hp], c4(v_)[:, hp].bc(2, 2), bm.v.bc(1, 4), ALU.mult, eng="pool")
                ps = P[3]
                for hp in range(2):
                    for c in range(4):
                        fw.mm(ps.v[:, (hp * 4 + c) * 64:(hp * 4 + c + 1) * 64],
                              vbd.v[:, hp, c].r("p a s -> p (a s)"), eye.v)
                fw.copy(Y.VT.v.r("p a c i -> p (a c i)"), ps.v[:, 0:512], "act")
                yield
                ps = P[0]
                for hp in range(2):
                    fw.mm(ps.v[:, hp * SEG:(hp + 1) * SEG], wup.v[0:32, d, hp * 128:(hp + 1) * 128], twd.v[0:32, :])
                for hp in range(2):
                    fw.act(X.sig.v[:, hp, :], ps.v[:, hp * SEG:(hp + 1) * SEG], AF.Sigmoid, bias=w0.v[:, d, hp:hp + 1])
                ps = P[1]
                for hp in range(2):
                    fw.mm(ps.v[:, hp * SEG:(hp + 1) * SEG], aup.v[32:64, d, hp * 128:(hp + 1) * 128], zs.v[32:64, 6, :])
                for hp in range(2):
                    fw.act(X.asig.v[:, hp, :], ps.v[:, hp * SEG:(hp + 1) * SEG], AF.Sigmoid, bias=a0.v[:, d, hp:hp + 1])
                for hp in range(2):
                    fw.ts(X.tb.v[:, hp, :], X.asig.v[:, hp, :], ka.v[:, hp:hp + 1], omka.v[:, hp:hp + 1], ALU.mult, ALU.add)
                fw.tt(X.tb.v, X.tb.v, k_, ALU.mult)
                fw.tt(X.ta.v, kkn.v, X.asig.v, ALU.mult, eng="pool")
                sf = X.sig.v.r("p a t -> p (a t)")
                lf = X.Ls.v.r("p a t -> p (a t)")
                if d == 0:
                    fw.scan(lf, rm.v[:, 0, :], sf)
                else:
                    fw.scan(lf[:, ::-1], rm.v[:, 1, ::-1], sf[:, ::-1])
                yield
                L4 = c4(X.Ls.v)
                lc = L4[:, :, :, 63:64] if d == 0 else L4[:, :, :, 0:1]
                fw.copy(X.LC.v, lc.r("p a c o -> p a (c o)"), "pool")
                fw.act(Y.PC.v, X.LC.v, AF.Exp, scale=-LDK)
                eN, eC, eP, ePm = X.Bh, X.Kh, X.Bt, X.Kt
                fw.act(eN.v, X.Ls.v, AF.Exp, scale=LDK)
                fw.act(eP.v, X.Ls.v, AF.Exp, scale=-LDK)
                fw.tt(X.asig.v, X.Ls.v, X.sig.v, ALU.subtract)
                fw.act(ePm.v, X.asig.v, AF.Exp, scale=-LDK)
                fw.tt(c4(X.asig.v), X.LC.v.bc(3, 64), L4, ALU.subtract, eng="pool")
                AR = X.AR.v
                fw.stt(AR[:, :, :, 0, :], c4(kkn.v), -1.0, c4(ePm.v), ALU.mult, ALU.mult)
                fw.tt(AR[:, :, :, 1, :], c4(r_), c4(eP.v), ALU.mult)
                yield
                fw.tt(X.Btr.v, X.ta.v, eN.v, ALU.mult)
                fw.tt(X.Ktr.v, X.tb.v, eN.v, ALU.mult)
                fw.act(eC.v, X.asig.v, AF.Exp, scale=-LDK)
                fw.tt(X.Bh.v, X.ta.v, eC.v, ALU.mult)
                fw.tt(X.Kh.v, X.tb.v, eC.v, ALU.mult, eng="pool")
                yield
                for hp in range(2):
                    bmc = bm.v.bc(1, 4)
                    fw.tt(X.Btbd.v[:, hp], c4(X.Btr.v)[:, hp].bc(2, 2), bmc, ALU.mult, eng="pool")
                    fw.tt(X.Ktbd.v[:, hp], c4(X.Ktr.v)[:, hp].bc(2, 2), bmc, ALU.mult)
                    for ar in range(2):
                        fw.tt(Y.ARbd.v[:, hp, :, ar], AR[:, hp, :, ar, :].bc(2, 2), bmc, ALU.mult,
                              eng="pool" if ar else "dve")
                yield
                banks = [(P[3], P[0], P[1]), (P[2], P[4], P[5])]
                for hp in range(2):
                    pB, pK, pN = banks[hp]
                    for c in range(4):
                        arc = AR[:, hp, c].r("p a t -> p (a t)")
                        fw.mm(pB.v[:, c * 128:(c + 1) * 128], X.Btbd.v[:, hp, c].r("p a s -> p (a s)"), arc)
                        fw.mm(pK.v[:, c * 128:(c + 1) * 128], X.Ktbd.v[:, hp, c].r("p a s -> p (a s)"), arc)
                        fw.mm(pN.v[:, c * 64:(c + 1) * 64], Y.ARbd.v[:, hp, c, 0].r("p a t -> p (a t)"),
                              X.Btr.v[:, hp, c * 64:(c + 1) * 64])
                    for ar in range(2):
                        mk_ = mbd.v[:, d, ar].bc(1, 4)
                        fw.tt(Y.MB.v[:, hp, :, ar].r("p c (a t) -> p c a t", a=2),
                              pB.v.r("p (c a t) -> p c a t", c=4, a=2)[:, :, ar, :].bc(2, 2), mk_, ALU.mult)
                        fw.tt(Y.MK.v[:, hp, :, ar].r("p c (a t) -> p c a t", a=2),
                              pK.v.r("p (c a t) -> p c a t", c=4, a=2)[:, :, ar, :].bc(2, 2), mk_, ALU.mult,
                              eng="dve")
                    fw.tt(X.NT[hp][0].v.r("p c (a t) -> p c a t", a=2), pN.v[:, 0:256].r("p (c t) -> p c t", c=4).bc(2, 2),
                          mbd.v[:, 1 - d, 0].bc(1, 4), ALU.mult)
                    fw.tt(X.Xx[hp][0].v, Y.MB.v[:, hp, :, 0], G.ident.v.bc(1, 4), ALU.add, eng="pool")
                yield
                Ncur = [Y.MB.v[:, hp, :, 0] for hp in range(2)]
                NTcur = [X.NT[hp][0].v for hp in range(2)]
                Xcur = [X.Xx[hp][0].v for hp in range(2)]
                for lev in range(1, 6):
                    NTn = [X.NT[hp][lev % 2].v for hp in range(2)]
                    Nn = [X.Nn[hp][lev % 2].v for hp in range(2)]
                    for hp in range(2):
                        p1, p2, p3 = banks[hp]
                        for c in range(4):
                            cs = slice(c * 128, (c + 1) * 128)
                            if lev < 5:
                                fw.mm(p1.v[:, cs], NTcur[hp][:, c, :], Ncur[hp][:, c, :])
                            fw.mm(p2.v[:, cs], Ncur[hp][:, c, :], NTcur[hp][:, c, :])
                    for hp in range(2):
                        p1, p2, p3 = banks[hp]
                        if lev < 5:
                            fw.copy(Nn[hp], p1.v.r("p (c t) -> p c t", c=4), "act")
                        fw.copy(NTn[hp], p2.v.r("p (c t) -> p c t", c=4), "dve")
                    for hp in range(2):
                        p1, p2, p3 = banks[hp]
                        for c in range(4):
                            fw.mm(p3.v[:, c * 128:(c + 1) * 128], NTn[hp][:, c, :], Xcur[hp][:, c, :])
                    for hp in range(2):
                        p1, p2, p3 = banks[hp]
                        Xn = X.Xx[hp][lev % 2].v if lev < 5 else Y.TT.v[:, hp]
                        fw.tt(Xn, p3.v.r("p (c t) -> p c t", c=4), Xcur[hp], ALU.add)
                        Xcur[hp] = Xn
                        if lev < 5:
                            Ncur[hp], NTcur[hp] = Nn[hp], NTn[hp]
                    yield
                for hp in range(2):
                    bmc = bm.v.bc(1, 4)
                    fw.tt(X.Bhbd.v[:, hp], c4(X.Bh.v)[:, hp].bc(2, 2), bmc, ALU.mult, eng="pool")
                    fw.tt(X.Khbd.v[:, hp], c4(X.Kh.v)[:, hp].bc(2, 2), bmc, ALU.mult)
                for hp in range(2):
                    pB, pK = banks[hp][0], banks[hp][1]
                    for c in range(4):
                        cs = slice(c * 128, (c + 1) * 128)
                        fw.mm(pB.v[:, cs], X.Bhbd.v[:, hp, c].r("p a s -> p (a s)"), identR.v)
                        fw.mm(pK.v[:, cs], X.Khbd.v[:, hp, c].r("p a s -> p (a s)"), identR.v)
                    fw.copy(Y.BhT.v[:, hp], pB.v.r("p (c t) -> p c t", c=4), "act")
                    fw.copy(Y.KhT.v[:, hp], pK.v.r("p (c t) -> p c t", c=4), "dve")
                yield

            def chunk_step(d, c, q, Y):
                pRU, pYS = P[6], P[7]
                pR, pU = pRU.v[:, 0:128], pRU.v[:, 128:256]
                pY, pS = pYS.v[:, 0:128], pYS.v[:, 128:256]
                for hp in range(2):
                    oc = slice(hp * 64, (hp + 1) * 64)
                    fw.mm(pR[:, oc], Y.ARbd.v[:, hp, c, 0].r("p a t -> p (a t)"), STb.v[:, d, hp, :], start=True, stop=False)
                    fw.mm(pR[:, oc], Y.MK.v[:, hp, c, 0], Y.VT.v[:, hp, c, :], start=False, stop=True)
                fw.copy(RHSsb.v, pR.r("p (a i) -> p a i", a=2), "act")
                for hp in range(2):
                    oc = slice(hp * 64, (hp + 1) * 64)
                    fw.mm(pU[:, oc], Y.TT.v[:, hp, c, :], RHSsb.v[:, hp, :])
                fw.copy(Usb.v, pU.r("p (a i) -> p a i", a=2), "dve")
                for hp in range(2):
                    oc = slice(hp * 64, (hp + 1) * 64)
                    fw.mm(pY[:, oc], Y.ARbd.v[:, hp, c, 1].r("p a t -> p (a t)"), STb.v[:, d, hp, :], start=True, stop=False)
                    fw.mm(pY[:, oc], Y.MB.v[:, hp, c, 1], Usb.v[:, hp, :], start=False, stop=False)
                    fw.mm(pY[:, oc], Y.MK.v[:, hp, c, 1], Y.VT.v[:, hp, c, :], start=False, stop=True)
                    fw.mm(pS[:, oc], Y.BhT.v[:, hp, c, :], Usb.v[:, hp, :], start=True, stop=False)
                    fw.mm(pS[:, oc], Y.KhT.v[:, hp, c, :], Y.VT.v[:, hp, c, :], start=False, stop=True)
                fw.tt(Yacc.v[:, q], pY.r("p (a i) -> p a i", a=2), Yacc.v[:, q], ALU.add)
                for hp in range(2):
                    oc = slice(hp * 64, (hp + 1) * 64)
                    fw.stt(ST.v[:, d, hp, :], ST.v[:, d, hp, :], Y.PC.v[:, hp, c:c + 1], pS[:, oc], ALU.mult, ALU.add)
                if RD != F32:
                    fw.copy(STb.v[:, d], ST.v[:, d], "act")

            work = [(0, seg) for seg in range(NSEG)] + [(1, seg) for seg in [0] + list(range(NSEG - 1, 0, -1))]

            def drain(g, k=None):
                n = 0
                for _ in g:
                    n += 1
                    if k is not None and n >= k:
                        return
            cur = prep(work[0][1], work[0][0], SX[0])
            drain(cur)
            for wi, (d, seg) in enumerate(work):
                Y = SX[wi % 2]
                nxt = prep(work[wi + 1][1], work[wi + 1][0], SX[(wi + 1) % 2]) if wi + 1 < len(work) else None
                cl = range(4) if d == 0 else range(3, -1, -1)
                for c in cl:
                    chunk_step(d, c, seg * 4 + c, Y)
                    if nxt is not None:
                        drain(nxt, 4)
                if nxt is not None:
                    drain(nxt)

        with fw.phase():
            Yf = Yacc.v.r("p q a i -> p (q a) i")
            sm = fw.sb("sm", [128, 72], F32)
            sq = fw.sb("sqy", [128, 72, 64], F32)
            s2 = fw.sb("s2", [128, 72], F32)
            var = fw.sb("var", [128, 72], F32)
            fw.reduce(sm.v, Yf)
            fw.ts(sm.v, sm.v, 1.0 / 64, None, ALU.mult)
            fw.tt(sq.v, Yf, sm.v.bc(2, 64), ALU.subtract)
            yn = fw.sb("yn", [128, 72, 64], F32)
            fw.tt(yn.v, sq.v, sq.v, ALU.mult, eng="pool")
            fw.reduce(s2.v, yn.v)
            fw.act(var.v, s2.v, AF.Ln, scale=1.0 / 64, bias=RK_EPS)
            fw.act(var.v, var.v, AF.Exp, scale=-0.5)
            fw.tt(yn.v, sq.v, var.v.bc(2, 64), ALU.mult)
            y4 = yn.v.r("p (q a) i -> p q a i", a=2)
            fw.tt(y4, y4, lnw.v.bc(1, 36), ALU.mult)
            fw.tt(y4, y4, lnb.v.bc(1, 36), ALU.add, eng="pool")
            ybd = [fw.sb(f"ybd{i}", [128, 8, 2, 64], F32) for i in range(2)]
            osts = [fw.sb(f"ork{i}", [128, 512], F32) for i in range(2)]
            ostb = [fw.sb(f"orkb{i}", [128, 512], BF16) for i in range(2)]
            it = 0
            nq = 36
            for hp in range(2):
                for q0 in range(0, nq, 8):
                    nqq = min(8, nq - q0)
                    ps = P[it % 4]
                    o1, o2, yb = osts[it % 2], ostb[it % 2], ybd[it % 2]
                    it += 1
                    fw.tt(yb.v[:, 0:nqq], y4[:, q0:q0 + nqq, hp, :].bc(2, 2), bm.v.bc(1, nqq), ALU.mult)
                    for qq in range(nqq):
                        fw.mm(ps.v[:, qq * 64:(qq + 1) * 64], yb.v[:, qq].r("p a i -> p (a i)"), eye.v)
                    n = nqq * 64
                    t0 = q0 * 64
                    fw.tt(o1.v[:, 0:n], ps.v[:, 0:n], bg.v[:, hp, 0, t0:t0 + n], ALU.add)
                    fw.tt(o2.v[:, 0:n], o1.v[:, 0:n], bg.v[:, hp, 1, t0:t0 + n], ALU.mult, eng="pool")
                    fw.dma("sp", S.MIXT[:, 4 + hp, t0:t0 + n], o2.v[:, 0:n])


NCORES = 8
_CACHE = {}


def kernel(**inputs):
    inp = {k: np.asarray(v) for k, v in inputs.items()}
    B = inp["x"].shape[0]
    nb = B // NCORES
    if "prog" not in _CACHE:
        _CACHE["prog"] = build_program(nb)
    nc, fw = _CACHE["prog"]
    in_maps = [_layout_inputs(inp, nb, c) for c in range(NCORES)]
    res = run_bass_kernel_spmd(nc, in_maps, core_ids=list(range(NCORES)))
    out = np.concatenate([np.asarray(r["out"]) for r in res.results], axis=0)
    return out.astype(np.float32)
```

```python
import contextlib
import numpy as np
import concourse.bass as bass
import concourse.mybir as mybir
from concourse.bass_utils import run_bass_kernel_spmd

F32 = mybir.dt.float32
BF16 = mybir.dt.bfloat16
AF = mybir.ActivationFunctionType
ALU = mybir.AluOpType
AX = mybir.AxisListType

D = 1024
TC = 256
TL = 2048
T = TC + TL
NLAY = 4
DFF = 2816
INC = 2848
C_NA, C_MLA, C_RK, C_DF = 0, 768, 1184, 2080


class V:
    def __init__(self, ap, buf):
        self.ap, self.buf = ap, buf

    def __getitem__(self, idx):
        return V(self.ap[idx], self.buf)

    def f(self, fn):
        return V(fn(self.ap), self.buf)

    def r(self, pat, **kw):
        return V(self.ap.rearrange(pat, **kw), self.buf)

    def bc(self, axis, n):
        a = self.ap.unsqueeze(axis)
        sh = list(a.shape)
        sh[axis] = n
        return V(a.broadcast_to(sh), self.buf)


class Buf:
    def __init__(self, name, t):
        self.name, self.t = name, t
        self.lw = None
        self.rd = {}

    def __getitem__(self, idx):
        return V(self.t[idx], self)

    @property
    def v(self):
        return V(self.t[:], self)


def _ap(x):
    return x.ap if isinstance(x, V) else x


class FW:
    def __init__(self, nc, n_dma_sems=20):
        self.nc = nc
        self.engs = {}
        for name, h in (("pe", nc.tensor), ("act", nc.scalar), ("dve", nc.vector), ("pool", nc.gpsimd),
                        ("sp", nc.sync)):
            self.engs[name] = dict(name=name, h=h, sem=nc.alloc_semaphore("s_" + name), cnt=0, waited={})
        self.dq = {}
        for q in ("sp", "act", "pool"):
            sems = [nc.alloc_semaphore(f"d_{q}_{i}") for i in range(n_dma_sems)]
            self.dq[q] = dict(sems=sems, vals=[0] * n_dma_sems, nxt=0)
        self.ninst = 0
        self.nwait = 0
        self.stack = None

    def sb(self, name, shape, dt):
        self.nsb = getattr(self, "nsb", 0) + 1
        name = f"{name}_{self.nsb}"
        if self.stack is not None:
            t = self.stack.enter_context(self.nc.sbuf_tensor(name, list(shape), dt))
        else:
            t = self.nc.alloc_sbuf_tensor(name, list(shape), dt)
        return Buf(name, t)

    def ps(self, name, shape, dt=F32):
        return Buf(name, self.nc.alloc_psum_tensor(name, list(shape), dt))

    def dram(self, name, shape, dt, kind="Internal"):
        return Buf(name, self.nc.dram_tensor(name, list(shape), dt, kind=kind))

    @contextlib.contextmanager
    def phase(self):
        old = self.stack
        with contextlib.ExitStack() as st:
            self.stack = st
            yield
            self.barrier()
        self.stack = old

    def _wait(self, e, key, sem, val):
        if e["waited"].get(key, 0) >= val:
            return
        e["h"].wait_ge(sem, val)
        e["waited"][key] = val
        self.nwait += 1

    def _deps(self, e, reads, writes, mykey):
        deps = {}

        def add(k, s, v):
            if k not in deps or deps[k][1] < v:
                deps[k] = (s, v)
        for b in reads:
            if b.lw is not None:
                add(*b.lw)
        for b in writes:
            if b.lw is not None:
                add(*b.lw)
            for k, (s, v) in b.rd.items():
                add(k, s, v)
        need = []
        for k, (s, v) in deps.items():
            if k == mykey and e["name"] == "pe":
                continue
            if e["waited"].get(k, 0) >= v:
                continue
            need.append((k, s, v))
        for (k, s, v) in need[:-1]:
            self._wait(e, k, s, v)
        return need[-1] if need else None

    def _fold(self, e, ins, last):
        if last is None:
            return
        k, s, v = last
        ins.wait_op(s, v, "sem-ge")
        e["waited"][k] = v

    def _mark(self, reads, writes, key, sem, val):
        for b in reads:
            b.rd[key] = (sem, val)
        for b in writes:
            b.lw = (key, sem, val)
            b.rd = {}

    def op(self, eng, fn, reads, writes):
        e = self.engs[eng]
        reads = [x.buf for x in reads if isinstance(x, V)]
        writes = [x.buf for x in writes if isinstance(x, V)]
        last = self._deps(e, reads, writes, eng)
        ins = fn(e["h"])
        self._fold(e, ins, last)
        e["cnt"] += 1
        ins.then_inc(e["sem"], 1)
        self._mark(reads, writes, eng, e["sem"], e["cnt"])
        self.ninst += 1
        return ins

    def dma(self, q, out, in_, **kw):
        e = self.engs[q]
        d = self.dq[q]
        i = d["nxt"]
        d["nxt"] = (i + 1) % len(d["sems"])
        sem = d["sems"][i]
        key = f"dma_{q}_{i}"
        if d["vals"][i] > 0:
            self._wait(e, key, sem, d["vals"][i])
        reads, writes = [in_.buf], [out.buf]
        last = self._deps(e, reads, writes, key)
        ins = e["h"].dma_start(out=out.ap, in_=in_.ap, **kw)
        self._fold(e, ins, last)
        d["vals"][i] += 16
        ins.then_inc(sem, 16)
        self._mark(reads, writes, key, sem, d["vals"][i])
        self.ninst += 1
        return ins

    def barrier(self):
        sp = self.engs["sp"]
        for q, d in self.dq.items():
            for i, sem in enumerate(d["sems"]):
                if d["vals"][i] > 0:
                    self._wait(sp, f"dma_{q}_{i}", sem, d["vals"][i])
        for n in ("pe", "act", "dve", "pool"):
            o = self.engs[n]
            if o["cnt"] > 0:
                self._wait(sp, n, o["sem"], o["cnt"])
        sp["cnt"] += 1
        sp["h"].sem_inc(sp["sem"], 1)
        for n in ("pe", "act", "dve", "pool"):
            self._wait(self.engs[n], "sp", sp["sem"], sp["cnt"])
            for m in ("pe", "act", "dve", "pool"):
                self.engs[n]["waited"][m] = max(self.engs[n]["waited"].get(m, 0), self.engs[m]["cnt"])
            for q, d in self.dq.items():
                for i in range(len(d["sems"])):
                    k = f"dma_{q}_{i}"
                    self.engs[n]["waited"][k] = max(self.engs[n]["waited"].get(k, 0), d["vals"][i])

    def mm(self, out, lhsT, rhs, start=True, stop=True):
        return self.op("pe", lambda h: h.matmul(out.ap, lhsT=lhsT.ap, rhs=rhs.ap, start=start, stop=stop),
                       [lhsT, rhs], [out])

    def act(self, out, in_, func, scale=None, bias=None, accum_out=None, eng="act"):
        kw = {}
        if scale is not None:
            kw["scale"] = _ap(scale)
        if bias is not None:
            kw["bias"] = _ap(bias)
        if accum_out is not None:
            kw["accum_out"] = _ap(accum_out)
        return self.op("act", lambda h: h.activation(out=out.ap, in_=in_.ap, func=func, **kw),
                       [in_, scale, bias], [out, accum_out])

    def tt(self, out, in0, in1, op, eng="dve"):
        return self.op(eng, lambda h: h.tensor_tensor(out=out.ap, in0=in0.ap, in1=in1.ap, op=op), [in0, in1], [out])

    def ts(self, out, in0, s1, s2, op0, op1=None, eng="dve"):
        kw = dict(op1=op1) if op1 is not None else {}
        return self.op(eng, lambda h: h.tensor_scalar(out=out.ap, in0=in0.ap, scalar1=_ap(s1), scalar2=_ap(s2),
                                                      op0=op0, **kw), [in0, s1, s2], [out])

    def stt(self, out, in0, scalar, in1, op0, op1):
        return self.op("dve", lambda h: h.scalar_tensor_tensor(out=out.ap, in0=in0.ap, scalar=_ap(scalar),
                                                               in1=in1.ap, op0=op0, op1=op1),
                       [in0, scalar, in1], [out])

    def copy(self, out, in_, eng="dve"):
        if eng == "act":
            return self.act(out, in_, AF.Copy)
        return self.op(eng, lambda h: h.tensor_copy(out=out.ap, in_=in_.ap), [in_], [out])

    def memset(self, out, val, eng="dve"):
        return self.op(eng, lambda h: h.memset(out.ap, val), [], [out])

    def scan(self, out, d0, d1, op0=ALU.mult, op1=ALU.add):
        return self.op("dve", lambda h: h.tensor_tensor_scan(out=out.ap, data0=d0.ap, data1=d1.ap, initial=0.0,
                                                             op0=op0, op1=op1), [d0, d1], [out])

    def recip(self, out, in_):
        return self.op("dve", lambda h: h.reciprocal(out=out.ap, in_=in_.ap), [in_], [out])

    def reduce(self, out, in_, op=ALU.add, axis=AX.X):
        return self.op("dve", lambda h: h.tensor_reduce(out=out.ap, in_=in_.ap, axis=axis, op=op), [in_], [out])


def _rope_tables():
    t = np.arange(TL)
    row = (t // 64).astype(np.float32)
    col = (t % 64).astype(np.float32)
    freqs = (np.float32(10000.0) ** (-np.arange(0, 16, 2, dtype=np.float32) / np.float32(16))).astype(np.float32)
    ar = row[:, None] * freqs[None, :]
    ac = col[:, None] * freqs[None, :]
    ang = np.concatenate([ar, ar, ac, ac], axis=-1).astype(np.float32)
    return np.cos(ang).astype(np.float32).T.copy(), np.sin(ang).astype(np.float32).T.copy()


def _rot32():
    R = np.zeros((32, 32), np.float32)
    for m in range(8):
        R[8 + m, m] = -1.0
        R[m, 8 + m] = 1.0
        R[24 + m, 16 + m] = -1.0
        R[16 + m, 24 + m] = 1.0
    return R


def _consts():
    cos, sin = _rope_tables()
    c = {}
    c["cos128"] = np.tile(cos, (4, 1))
    c["sin128"] = np.tile(sin, (4, 1))
    c["cos96"] = np.concatenate([np.ones((64, TL), np.float32), cos], 0)
    c["sin96"] = np.concatenate([np.zeros((64, TL), np.float32), sin], 0)
    R = _rot32()
    R128 = np.zeros((128, 128), np.float32)
    for g in range(4):
        R128[g * 32:(g + 1) * 32, g * 32:(g + 1) * 32] = R
    R96 = np.zeros((96, 96), np.float32)
    R96[64:, 64:] = R
    c["R128"] = R128
    c["R96"] = R96
    c["ident"] = np.eye(128, dtype=np.float32)
    bo = np.zeros((128, 128), np.float32)
    bo[:64, :64] = 1.0
    bo[64:, 64:] = 1.0
    c["blockones"] = bo
    es = np.zeros((65, 64), np.float32)
    es[64, :] = 1.0
    c["esel"] = es
    cpos = np.arange(64)
    cstart = np.clip(cpos - 8, 0, 48)
    m = (cpos[None, :] >= cstart[:, None]) & (cpos[None, :] < cstart[:, None] + 16)
    c["na_mask"] = np.ascontiguousarray(m.T.astype(np.float32))
    s = np.arange(64)[:, None]
    t = np.arange(64)[None, :]
    mk = np.zeros((128, 2, 2, 64), np.float32)
    for hh in range(2):
        mk[hh * 64:(hh + 1) * 64, 0, 0] = (s < t)
        mk[hh * 64:(hh + 1) * 64, 0, 1] = (s <= t)
        mk[hh * 64:(hh + 1) * 64, 1, 0] = (s > t)
        mk[hh * 64:(hh + 1) * 64, 1, 1] = (s >= t)
    c["rk_mask"] = mk
    gmm = np.zeros((128, 4), np.float32)
    for g in range(4):
        gmm[g * 32:(g + 1) * 32, g] = 1.0
    c["df_gm"] = gmm
    c["rk_eye"] = np.concatenate([np.eye(64, dtype=np.float32)] * 2, 0)
    rm = np.ones((128, 2, 512), np.float32)
    rm[:, 0, 0::64] = 0.0
    rm[:, 1, 63::64] = 0.0
    c["rk_rm"] = rm
    return c


CONST_SHAPES = dict(cos128=(128, TL), sin128=(128, TL), cos96=(96, TL), sin96=(96, TL), R128=(128, 128),
                    R96=(96, 96), ident=(128, 128), blockones=(128, 128), esel=(65, 64), na_mask=(64, 64),
                    rk_mask=(128, 2, 2, 64), rk_eye=(128, 64), rk_rm=(128, 2, 512), df_gm=(128, 4))


def _fm(a, k):
    sh = a.shape[:-1]
    return np.ascontiguousarray(np.swapaxes(a.reshape(sh + (k, 128)), -1, -2))


def _layout_inputs(inp, nb, core):
    b0 = core * nb
    m = {}
    m["x"] = np.ascontiguousarray(inp["x"][b0:b0 + nb])
    m["ctx"] = np.ascontiguousarray(inp["ctx"][b0:b0 + nb])
    cv = np.concatenate([inp["c"][b0:b0 + nb], inp["c_ctx"][None]], 0)
    m["cvec"] = np.ascontiguousarray(cv.reshape(nb + 1, 8, 128).transpose(2, 1, 0))
    m["ada_w"] = inp["ada_w"]
    m["ada_bT"] = _fm(inp["ada_b"], 48)
    m["g1T"] = _fm(inp["norm1_g"], 8)
    m["g2T"] = _fm(inp["norm2_g"], 8)
    m["gfT"] = _fm(inp["final_norm_g"], 8)
    m["w_in"] = inp["w_in"]
    m["w_out"] = inp["w_out"]
    m["w_up"] = inp["mlp_w_up"]
    m["w_down"] = inp["mlp_w_down"]
    m["mla_qn"] = _fm(inp["mla_q_norm"], 2)
    m["mla_kvn"] = _fm(inp["mla_kv_norm"], 1)
    m["w_uq"] = inp["mla_w_uq"]
    m["w_ukv"] = inp["mla_w_ukv"]
    m["rk_mu"] = np.ascontiguousarray(inp["rwkv_mu"].reshape(NLAY, 2, 7, 128).transpose(0, 3, 1, 2))
    m["rk_w0"] = np.ascontiguousarray(inp["rwkv_w0"].reshape(NLAY, 2, 2, 128).transpose(0, 3, 1, 2))
    m["rk_a0"] = np.ascontiguousarray(inp["rwkv_a0"].reshape(NLAY, 2, 2, 128).transpose(0, 3, 1, 2))
    m["rk_wup"] = inp["rwkv_w_up"]
    m["rk_aup"] = inp["rwkv_a_up"]
    m["rk_gup"] = inp["rwkv_g_up"]
    m["rk_kk"] = _fm(inp["rwkv_k_k"], 2)
    m["rk_ka"] = _fm(inp["rwkv_k_a"], 2)
    m["rk_rk"] = _fm(inp["rwkv_r_k"].reshape(NLAY, 256), 2)
    m["rk_lnw"] = inp["rwkv_ln_w"]
    m["rk_lnb"] = inp["rwkv_ln_b"]
    m["df_lam"] = inp["diff_lambda"].reshape(NLAY, 1, 128)
    m["df_sub"] = np.ascontiguousarray(inp["diff_subln"].reshape(NLAY, 64, 1))
    m["conv_w"] = np.ascontiguousarray(inp["mlp_conv_w"].reshape(NLAY, 3, 22, 128).transpose(0, 3, 1, 2))
    m["conv_b"] = _fm(inp["mlp_conv_b"], 22)
    rpb = inp["na_rpb"]
    kc = np.arange(64)[:, None]
    qc = np.arange(64)[None, :]
    cidx = np.clip(kc - qc + 15, 0, 30)
    g = rpb[:, :, ::-1, :][:, :, :, cidx]
    m["na_bias"] = np.ascontiguousarray(g.transpose(0, 3, 1, 2, 4))
    for k, v in _consts().items():
        m["c_" + k] = v
    return {k: np.ascontiguousarray(v) for k, v in m.items()}


class NS:
    pass


IN_SHAPES = None


def build_program(nb, nlay=NLAY, dbg=(), parts=("na", "mla", "rk", "df", "mlp"), upto="end"):
    nc = bass.Bass("TRN2", target_bir_lowering=False)
    fw = FW(nc)
    S = NS()
    S.nc, S.fw, S.nb, S.nlay, S.dbg, S.parts = nc, fw, nb, nlay, set(dbg), parts
    NV = nb + 1
    S.NV = NV

    def din(name, shape, dt=F32):
        return fw.dram(name, shape, dt, kind="ExternalInput")

    I = NS()
    S.I = I
    I.x = din("x", [nb, TL, D])
    I.ctx = din("ctx", [nb, TC, D])
    I.cvec = din("cvec", [128, 8, NV])
    I.ada_w = din("ada_w", [NLAY, D, 6 * D])
    I.ada_bT = din("ada_bT", [NLAY, 128, 48])
    I.g1T = din("g1T", [NLAY, 128, 8])
    I.g2T = din("g2T", [NLAY, 128, 8])
    I.gfT = din("gfT", [128, 8])
    I.w_in = din("w_in", [NLAY, D, INC])
    I.w_out = din("w_out", [NLAY, D, D])
    I.w_up = din("w_up", [NLAY, D, 2 * DFF])
    I.w_down = din("w_down", [NLAY, DFF, D])
    I.mla_qn = din("mla_qn", [NLAY, 128, 2])
    I.mla_kvn = din("mla_kvn", [NLAY, 128, 1])
    I.w_uq = din("w_uq", [NLAY, 256, 384])
    I.w_ukv = din("w_ukv", [NLAY, 128, 512])
    I.rk_mu = din("rk_mu", [NLAY, 128, 2, 7])
    I.rk_w0 = din("rk_w0", [NLAY, 128, 2, 2])
    I.rk_a0 = din("rk_a0", [NLAY, 128, 2, 2])
    I.rk_wup = din("rk_wup", [NLAY, 2, 32, 256])
    I.rk_aup = din("rk_aup", [NLAY, 2, 32, 256])
    I.rk_gup = din("rk_gup", [NLAY, 64, 256])
    I.rk_kk = din("rk_kk", [NLAY, 128, 2])
    I.rk_ka = din("rk_ka", [NLAY, 128, 2])
    I.rk_rk = din("rk_rk", [NLAY, 128, 2])
    I.rk_lnw = din("rk_lnw", [NLAY, 256])
    I.rk_lnb = din("rk_lnb", [NLAY, 256])
    I.df_lam = din("df_lam", [NLAY, 1, 128])
    I.df_sub = din("df_sub", [NLAY, 64, 1])
    I.conv_w = din("conv_w", [NLAY, 128, 3, 22])
    I.conv_b = din("conv_b", [NLAY, 128, 22])
    I.na_bias = din("na_bias", [NLAY, 64, 4, 15, 64])
    I.c = {}
    for k, sh in CONST_SHAPES.items():
        I.c[k] = din("c_" + k, list(sh))
    S.out = fw.dram("out", [nb, TL, D], F32, kind="ExternalOutput")

    def scratch(name, shape, dt):
        return fw.dram(name, shape, dt, kind="ExternalOutput" if name in S.dbg else "Internal")
    S.scratch = scratch
    S.w_in_bf = scratch("w_in_bf", [NLAY, D, INC], BF16)
    S.w_out_bf = scratch("w_out_bf", [NLAY, D, D], BF16)
    S.w_up_bf = scratch("w_up_bf", [NLAY, 22, 128, 2, 8, 128], BF16)
    S.w_down_bf = scratch("w_down_bf", [NLAY, 8, 128, 22, 128], BF16)
    S.w_uq_bf = scratch("w_uq_bf", [NLAY, 256, 384], BF16)
    S.w_ukv_bf = scratch("w_ukv_bf", [NLAY, 128, 512], BF16)
    S.XT = scratch("XT", [nb, 128, 8, T], F32)
    S.QK_na = scratch("QK_na", [128, 4, T], BF16)
    S.V_na = scratch("V_na", [18, 128, 4, 128], BF16)
    S.QT_mla = scratch("QT_mla", [96, 4, T], BF16)
    S.KT_mla = scratch("KT_mla", [96, 4, T], BF16)
    S.V_mla = scratch("V_mla", [18, 128, 4, 128], BF16)
    S.QK_df = scratch("QK_df", [128, 4, T], BF16)
    S.V_df = scratch("V_df", [18, 128, 4, 128], BF16)
    S.ZT_rk = scratch("ZT_rk", [128, 7, T], F32)
    S.MIXT = scratch("MIXT", [128, 8, T], BF16)

    S.P = [fw.ps(f"P{i}", [128, 512]) for i in range(8)]

    G = NS()
    S.G = G
    G.ident = fw.sb("ident", [128, 128], F32)
    G.ones_bf = fw.sb("ones_bf", [128, 128], BF16)
    G.blockones = fw.sb("blockones", [128, 128], F32)
    G.esel = fw.sb("esel", [65, 64], F32)
    G.mods = fw.sb("mods", [128, nlay, 48, NV], F32)
    G.A = fw.sb("Amod", [128, nlay, 2, 8, NV], F32)
    fw.dma("sp", G.ident.v, I.c["ident"].v)
    fw.dma("sp", G.blockones.v, I.c["blockones"].v)
    fw.dma("sp", G.esel.v, I.c["esel"].v)
    fw.memset(G.ones_bf.v, 1.0)

    phase_prep(S)
    phase_mod(S)
    for b in range(nb):
        for l in range(nlay):
            need_ctx = l < NLAY - 1
            phase_A(S, b, l)
            if upto == "A":
                continue
            if "mla" in parts:
                phase_mla(S, b, l, need_ctx)
            if "df" in parts:
                phase_df(S, b, l, need_ctx)
            if "na" in parts:
                phase_na(S, b, l, need_ctx)
            if "rk" in parts:
                phase_rk(S, b, l, need_ctx)
            if upto == "att":
                continue
            phase_O(S, b, l, need_ctx)
            if upto == "O":
                continue
            if "mlp" in parts:
                phase_mlp(S, b, l, need_ctx)
    if upto == "end":
        phase_final(S)
    fw.barrier()
    return nc, fw


def phase_prep(S):
    fw, I, nb = S.fw, S.I, S.nb
    for l in range(S.nlay):
        for j in range(22):
            for ab in range(2):
                fw.dma("pool", S.w_up_bf[l, j, :, ab, :, :],
                       I.w_up[l, :, ab * DFF + j * 128:ab * DFF + (j + 1) * 128].r("(k p) n -> p k n", p=128))
        for m in range(8):
            fw.dma("pool", S.w_down_bf[l, m], I.w_down[l, :, m * 128:(m + 1) * 128].r("(j p) n -> p j n", p=128))
        for src, dst, rows, cols in ((I.w_in, S.w_in_bf, D, INC), (I.w_out, S.w_out_bf, D, D),
                                     (I.w_uq, S.w_uq_bf, 256, 384), (I.w_ukv, S.w_ukv_bf, 128, 512)):
            r0 = 0
            while r0 < rows:
                nr = min(512, rows - r0)
                fw.dma("pool", dst[l, r0:r0 + nr, :], src[l, r0:r0 + nr, :], max_dma_last_dim=4096)
                r0 += nr
    with fw.phase():
        xin = [fw.sb(f"xin{i}", [128, D], F32) for i in range(2)]
        xo = [fw.sb(f"xo{i}", [128, 8, 128], F32) for i in range(2)]
        it = 0
        for b in range(nb):
            for tt_ in range(T // 128):
                xi, xoo = xin[it % 2], xo[it % 2]
                if tt_ < 2:
                    src = I.ctx[b, tt_ * 128:(tt_ + 1) * 128, :]
                else:
                    src = I.x[b, (tt_ - 2) * 128:(tt_ - 1) * 128, :]
                fw.dma("sp" if it % 2 == 0 else "act", xi.v, src)
                for half in range(2):
                    ps = S.P[(it * 2 + half) % 4]
                    for kk in range(4):
                        k = half * 4 + kk
                        fw.op("pe", lambda h: h.transpose(ps.v[:, kk * 128:(kk + 1) * 128].ap,
                                                           xi.v[:, k * 128:(k + 1) * 128].ap, S.G.ident.v.ap),
                              [xi.v, S.G.ident.v], [ps.v])
                    dst = xoo.v[:, half * 4:(half + 1) * 4, :]
                    srcp = ps.v.r("p (k t) -> p k t", k=4)
                    if half == 0:
                        fw.copy(dst, srcp, "dve")
                    else:
                        fw.copy(dst, srcp, "act")
                fw.dma("sp", S.XT[b, :, :, tt_ * 128:(tt_ + 1) * 128], xoo.v)
                it += 1


def phase_mod(S):
    fw, I, G, NV = S.fw, S.I, S.G, S.NV
    with fw.phase():
        cv = fw.sb("cv", [128, 8, NV], F32)
        sv = fw.sb("sv", [128, 8, NV], F32)
        fw.dma("sp", cv.v, I.cvec.v)
        fw.act(sv.v, cv.v, AF.Silu)
        wb = [fw.sb(f"adaw{i}", [128, 8, 768], F32) for i in range(2)]
        for l in range(S.nlay):
            abT = fw.sb(f"abT{l}", [128, 48], F32)
            fw.dma("act", abT.v, I.ada_bT[l])
            gT = fw.sb(f"gT{l}", [128, 2, 8], F32)
            fw.dma("act", gT.v[:, 0, :], I.g1T[l])
            fw.dma("act", gT.v[:, 1, :], I.g2T[l])
            ps = S.P[4 + (l % 2)]
            for blk in range(8):
                w = wb[(l * 8 + blk) % 2]
                fw.dma("sp" if blk % 2 == 0 else "act", w.v,
                       I.ada_w[l, :, blk * 768:(blk + 1) * 768].r("(k p) n -> p k n", p=128))
                for cc in range(6):
                    ch = blk * 6 + cc
                    for k in range(8):
                        fw.mm(ps.v[:, ch * NV:(ch + 1) * NV], w.v[:, k, cc * 128:(cc + 1) * 128], sv.v[:, k, :],
                              start=(k == 0), stop=(k == 7))
            fw.tt(G.mods.v[:, l], ps.v[:, 0:48 * NV].r("p (c v) -> p c v", v=NV), abT.v.bc(2, NV), ALU.add)
            for n, c0 in ((0, 8), (1, 32)):
                tmp = fw.sb(f"modtmp{l}{n}", [128, 8, NV], F32)
                fw.ts(tmp.v, G.mods.v[:, l, c0:c0 + 8, :], 1.0, None, ALU.add)
                fw.tt(G.A.v[:, l, n], tmp.v, gT.v[:, n, :].bc(2, NV), ALU.mult)


TOK_TILES = [(0, 256), (256, 512), (768, 512), (1280, 512), (1792, 512)]
EPS = 1e-6


def nps(S):
    S.pi = getattr(S, "pi", 0) + 1
    return S.P[S.pi % 8]


def rstd_from_sum(S, out, ssum, n, eps, tmp):
    fw = S.fw
    fw.act(tmp, ssum, AF.Ln, scale=1.0 / n, bias=float(eps))
    fw.act(out, tmp, AF.Exp, scale=-0.5)


def phase_A(S, b, l):
    fw, I, G = S.fw, S.I, S.G
    with fw.phase():
        win = fw.sb("win", [128, 8, INC], BF16)
        for k in range(8):
            fw.dma("sp" if k % 2 == 0 else "act", win.v[:, k, :], S.w_in_bf[l, k * 128:(k + 1) * 128, :])
        wuq = fw.sb("wuq", [128, 2, 384], BF16)
        fw.dma("sp", wuq.v, S.w_uq_bf[l].r("(k p) n -> p k n", p=128))
        wukv = fw.sb("wukv", [128, 512], BF16)
        fw.dma("act", wukv.v, S.w_ukv_bf[l])
        qn = fw.sb("qn", [128, 2], F32)
        kvn = fw.sb("kvn", [128, 1], F32)
        fw.dma("sp", qn.v, I.mla_qn[l])
        fw.dma("sp", kvn.v, I.mla_kvn[l])
        tabs = {}
        for nm, rows in (("cos128", 128), ("sin128", 128), ("cos96", 96), ("sin96", 96)):
            tabs[nm] = fw.sb(nm, [rows, TL], F32)
            fw.dma("act", tabs[nm].v, I.c[nm].v)
        R128 = fw.sb("R128", [128, 128], F32)
        R96 = fw.sb("R96", [96, 96], F32)
        fw.dma("sp", R128.v, I.c["R128"].v)
        fw.dma("sp", R96.v, I.c["R96"].v)

        xt = fw.sb("xt", [128, 8, 512], F32)
        sq = fw.sb("sq", [128, 8, 512], BF16)
        xn = fw.sb("xn", [128, 8, 512], F32)
        hT = fw.sb("hT", [128, 8, 512], BF16)
        rstd = fw.sb("rstd", [128, 512], F32)
        lnt = fw.sb("lnt", [128, 512], F32)
        st_na = fw.sb("st_na", [128, 4, 512], BF16)
        st_df = fw.sb("st_df", [128, 4, 512], BF16)
        st_rk = fw.sb("st_rk", [128, 7, 512], F32)
        vts = [fw.sb(f"vt{i}", [128, 4, 128], BF16) for i in range(6)]
        for vt in vts:
            fw.memset(vt.v, 1.0, "pool")
        cq = fw.sb("cq", [128, 3, 512], F32)
        sqm = fw.sb("sqm", [128, 3, 512], BF16)
        rq = fw.sb("rq", [128, 2, 512], F32)
        cqn = fw.sb("cqn", [128, 3, 512], BF16)
        qst = fw.sb("qst", [96, 4, 512], BF16)
        kst = fw.sb("kst", [96, 4, 512], BF16)
        xsb = [fw.sb(f"xsb{i}", [128, 512], F32) for i in range(2)]
        t1s = [fw.sb(f"t1s{i}", [128, 512], F32) for i in range(2)]
        t2s = [fw.sb(f"t2s{i}", [128, 512], F32) for i in range(2)]
        krt = fw.sb("krt", [96, 512], BF16)
        S.rc = 0
        S.vc = 0

        def rope(psv, M, N, tl0, Rm, cosn, sinn, outv):
            i = S.rc % 2
            S.rc += 1
            x, t1, t2 = xsb[i].v[0:M, 0:N], t1s[i].v[0:M, 0:N], t2s[i].v[0:M, 0:N]
            fw.copy(x, psv, "act")
            rp = nps(S).v[0:M, 0:N]
            fw.mm(rp, Rm.v[0:M, 0:M], x)
            fw.tt(t1, x, tabs[cosn].v[0:M, tl0:tl0 + N], ALU.mult)
            fw.tt(t2, rp, tabs[sinn].v[0:M, tl0:tl0 + N], ALU.mult)
            fw.tt(outv, t1, t2, ALU.add, eng="pool")

        for (t0, N) in TOK_TILES:
            lat = t0 >= TC
            vec = b if lat else S.nb
            tl0 = t0 - TC
            fw.dma("sp", xt.v[:, :, 0:N], S.XT[b, :, :, t0:t0 + N])
            fw.act(sq.v[:, :, 0:N], xt.v[:, :, 0:N], AF.Square)
            ps = nps(S)
            for k in range(8):
                fw.mm(ps.v[:, 0:N], G.ones_bf.v, sq.v[:, k, 0:N], start=(k == 0), stop=(k == 7))
            rstd_from_sum(S, rstd.v[:, 0:N], ps.v[:, 0:N], 1024.0, EPS, lnt.v[:, 0:N])
            fw.tt(xn.v[:, :, 0:N], xt.v[:, :, 0:N], rstd.v[:, 0:N].bc(1, 8), ALU.mult)
            for k in range(8):
                fw.act(hT.v[:, k, 0:N], xn.v[:, k, 0:N], AF.Identity, scale=G.A.v[:, l, 0, k, vec:vec + 1],
                       bias=G.mods.v[:, l, 0 + k, vec:vec + 1])

            def zchunk(c0, M):
                p = nps(S)
                for k in range(8):
                    fw.mm(p.v[0:M, 0:N], win.v[:, k, c0:c0 + M], hT.v[:, k, 0:N], start=(k == 0), stop=(k == 7))
                return p.v[0:M, 0:N]

            def vproj(c0, dst):
                for sub in range(N // 128):
                    p = nps(S)
                    for k in range(8):
                        fw.mm(p.v[:, 0:256], hT.v[:, k, sub * 128:(sub + 1) * 128], win.v[:, k, c0:c0 + 256],
                              start=(k == 0), stop=(k == 7))
                    vt = vts[S.vc % 6]
                    S.vc += 1
                    fw.copy(vt.v[:, :, 0:64], p.v[:, 0:256].r("p (h c) -> p h c", h=4), "dve")
                    fw.dma("act", dst[(t0 + sub * 128) // 128], vt.v)

            for c in range(4):
                p = zchunk(C_NA + c * 128, 128)
                fw.copy(st_na.v[:, c, 0:N], p, "act" if c % 2 else "dve")
            fw.dma("sp", S.QK_na[:, :, t0:t0 + N], st_na.v[:, :, 0:N])
            vproj(C_NA + 512, S.V_na)
            for c in range(4):
                p = zchunk(C_DF + c * 128, 128)
                if lat:
                    rope(p, 128, N, tl0, R128, "cos128", "sin128", st_df.v[:, c, 0:N])
                else:
                    fw.copy(st_df.v[:, c, 0:N], p, "act" if c % 2 else "dve")
            fw.dma("sp", S.QK_df[:, :, t0:t0 + N], st_df.v[:, :, 0:N])
            vproj(C_DF + 512, S.V_df)
            for c in range(7):
                p = zchunk(C_RK + c * 128, 128)
                fw.copy(st_rk.v[:, c, 0:N], p, "act" if c % 2 else "dve")
            fw.dma("sp", S.ZT_rk[:, :, t0:t0 + N], st_rk.v[:, :, 0:N])
            for c in range(3):
                p = zchunk(C_MLA + c * 128, 128)
                fw.copy(cq.v[:, c, 0:N], p, "act" if c % 2 else "dve")
            fw.act(sqm.v[:, :, 0:N], cq.v[:, :, 0:N], AF.Square)
            pq, pk = nps(S), nps(S)
            fw.mm(pq.v[:, 0:N], G.ones_bf.v, sqm.v[:, 0, 0:N], start=True, stop=False)
            fw.mm(pq.v[:, 0:N], G.ones_bf.v, sqm.v[:, 1, 0:N], start=False, stop=True)
            fw.mm(pk.v[:, 0:N], G.ones_bf.v, sqm.v[:, 2, 0:N])
            rstd_from_sum(S, rq.v[:, 0, 0:N], pq.v[:, 0:N], 256.0, EPS, lnt.v[:, 0:N])
            rstd_from_sum(S, rq.v[:, 1, 0:N], pk.v[:, 0:N], 128.0, EPS, lnt.v[:, 0:N])
            for j in range(3):
                sc = qn.v[:, j:j + 1] if j < 2 else kvn.v[:, 0:1]
                rr = rq.v[:, 0, 0:N] if j < 2 else rq.v[:, 1, 0:N]
                fw.stt(cqn.v[:, j, 0:N], cq.v[:, j, 0:N], sc, rr, ALU.mult, ALU.mult)
            for h in range(4):
                p = nps(S).v[0:96, 0:N]
                for j in range(2):
                    fw.mm(p, wuq.v[:, j, h * 96:(h + 1) * 96], cqn.v[:, j, 0:N], start=(j == 0), stop=(j == 1))
                if lat:
                    rope(p, 96, N, tl0, R96, "cos96", "sin96", qst.v[:, h, 0:N])
                else:
                    fw.copy(qst.v[:, h, 0:N], p, "act")
                p2 = nps(S).v[0:64, 0:N]
                fw.mm(p2, wukv.v[:, h * 128:h * 128 + 64], cqn.v[:, 2, 0:N])
                fw.copy(kst.v[0:64, h, 0:N], p2, "dve")
            p = zchunk(C_MLA + 320, 96)
            if lat:
                rope(p, 96, N, tl0, R96, "cos96", "sin96", krt.v[:, 0:N])
            else:
                fw.copy(krt.v[:, 0:N], p, "act")
            for h in range(4):
                fw.copy(kst.v[64:96, h, 0:N], krt.v[64:96, 0:N], "pool")
            fw.dma("sp", S.QT_mla[:, :, t0:t0 + N], qst.v[:, :, 0:N])
            fw.dma("sp", S.KT_mla[:, :, t0:t0 + N], kst.v[:, :, 0:N])
            for sub in range(N // 128):
                pv = nps(S)
                fw.mm(pv.v[:, 0:256].r("p (h c) -> p h c", h=4), cqn.v[:, 2, sub * 128:(sub + 1) * 128],
                      wukv.v.r("p (h c) -> p h c", h=4)[:, :, 64:128])
                vt = vts[S.vc % 6]
                S.vc += 1
                fw.copy(vt.v[:, :, 0:64], pv.v[:, 0:256].r("p (h c) -> p h c", h=4), "dve")
                fw.dma("act", S.V_mla[(t0 + sub * 128) // 128], vt.v)


class Skew:
    def __init__(self, L=2):
        self.L, self.q = L, []

    def push(self, fn):
        self.q.append(fn)
        while len(self.q) > self.L:
            self.q.pop(0)()

    def drain(self):
        while self.q:
            self.q.pop(0)()


def _qtiles(need_ctx):
    qt = [(TC + i * 512, 512, 18) for i in range(4)]
    if need_ctx:
        qt.append((0, 256, 2))
    return qt


def _normalize(S, fw, O, NQ, osb, lnr, rec, outv, bcp, eng="pool"):
    G = S.G
    fw.copy(osb.v[0:65, 0:NQ], O.v[0:65, 0:NQ], "dve")
    fw.mm(bcp.v[0:64, 0:NQ], G.esel.v[0:65, 0:64], osb.v[0:65, 0:NQ])
    fw.act(lnr.v[0:64, 0:NQ], bcp.v[0:64, 0:NQ], AF.Ln)
    fw.act(rec.v[0:64, 0:NQ], lnr.v[0:64, 0:NQ], AF.Exp, scale=-1.0)
    fw.tt(outv, osb.v[0:64, 0:NQ], rec.v[0:64, 0:NQ], ALU.mult, eng=eng)


def phase_mla(S, b, l, need_ctx):
    fw, I, G, P = S.fw, S.I, S.G, S.P
    scale = 96.0 ** -0.5
    with fw.phase():
        KT = fw.sb("KT", [128, 4, T], BF16)
        QT = fw.sb("QT", [128, 4, T], BF16)
        Vv = fw.sb("Vv", [128, 18, 4, 128], BF16)
        fw.memset(KT.v, 0.0)
        fw.memset(QT.v, 0.0, "pool")
        fw.dma("sp", KT.v[0:96], S.KT_mla.v)
        fw.dma("act", QT.v[0:96], S.QT_mla.v)
        fw.dma("sp", Vv.v, S.V_mla.v.r("j p h c -> p j h c"))
        pts = [fw.sb(f"pt{i}", [128, 512], BF16) for i in range(3)]
        osb = fw.sb("osb", [65, 512], F32)
        lnr = fw.sb("lnr", [64, 512], F32)
        rec = fw.sb("rec", [64, 512], F32)
        osts = [fw.sb(f"ost{i}", [64, 4, 512], BF16) for i in range(2)]
        cnt = 0
        pend = []
        sk = Skew(2)

        def flush():
            while pend:
                pend.pop(0)()
        for qi, (q0, NQ, nk) in enumerate(_qtiles(need_ctx)):
            ost = osts[qi % 2]
            for h in range(4):
                O = P[4 + (h % 2)]
                for j in range(nk):
                    sp = P[cnt % 4]
                    pt = pts[cnt % 3]
                    cnt += 1
                    fw.mm(sp.v[:, 0:NQ], KT.v[:, h, j * 128:(j + 1) * 128], QT.v[:, h, q0:q0 + NQ])
                    fw.act(pt.v[:, 0:NQ], sp.v[:, 0:NQ], AF.Exp, scale=scale)
                    sk.push(lambda O=O, j=j, h=h, pt=pt, NQ=NQ, nk=nk: fw.mm(
                        O.v[:, 0:NQ], Vv.v[:, j, h, :], pt.v[:, 0:NQ], start=(j == 0), stop=(j == nk - 1)))
                    if j == min(5, nk - 1):
                        flush()

                def norm(h=h, NQ=NQ, ost=ost, q0=q0):
                    bcp = P[6 + (h % 2)]
                    fw.mm(bcp.v[0:64, 0:NQ], G.esel.v[0:65, 0:64], osb.v[0:65, 0:NQ])
                    fw.act(lnr.v[0:64, 0:NQ], bcp.v[0:64, 0:NQ], AF.Ln)
                    fw.act(rec.v[0:64, 0:NQ], lnr.v[0:64, 0:NQ], AF.Exp, scale=-1.0)
                    fw.tt(ost.v[:, h, 0:NQ], osb.v[0:64, 0:NQ], rec.v[0:64, 0:NQ], ALU.mult, eng="pool")
                    if h == 3:
                        for hh in range(4):
                            hb = (hh % 2) * 64
                            fw.dma("sp" if hh % 2 else "act", S.MIXT[hb:hb + 64, 2 + hh // 2, q0:q0 + NQ], ost.v[:, hh, 0:NQ])

                def headend(O=O, NQ=NQ, norm=norm):
                    fw.copy(osb.v[0:65, 0:NQ], O.v[0:65, 0:NQ], "dve")
                    pend.append(norm)
                sk.push(headend)
        sk.drain()
        flush()


def phase_df(S, b, l, need_ctx):
    fw, I, G, P = S.fw, S.I, S.G, S.P
    import math
    scale = 32.0 ** -0.5
    lam_init = 0.8 - 0.6 * math.exp(-0.3 * l)
    with fw.phase():
        Kd = fw.sb("Kd", [128, 2, T], BF16)
        Qd = fw.sb("Qd", [128, 2, T], BF16)
        fw.dma("sp", Qd.v, S.QK_df[:, 0:2, :])
        fw.dma("act", Kd.v, S.QK_df[:, 2:4, :])
        gm = fw.sb("gm", [128, 4], F32)
        fw.dma("sp", gm.v, I.c["df_gm"].v)
        Qm = fw.sb("Qm", [128, 2, 4, T], BF16)
        for c in range(2):
            for g in range(4):
                if g % 2:
                    fw.ts(Qm.v[:, c, g, :], Qd.v[:, c, :], gm.v[:, g:g + 1], None, ALU.mult)
                else:
                    fw.act(Qm.v[:, c, g, :], Qd.v[:, c, :], AF.Identity, scale=gm.v[:, g:g + 1])
        Vv = fw.sb("Vv", [128, 18, 4, 128], BF16)
        fw.dma("sp", Vv.v, S.V_df.v.r("j p h c -> p j h c"))
        dl = fw.sb("dl", [64, 128], F32)
        fw.dma("act", dl.v, I.df_lam[l].f(lambda a: a.broadcast_to([64, 128])))
        pr = fw.sb("pr", [64, 2, 32], F32)
        fw.tt(pr.v, dl.v.r("p (a b d) -> p a b d", a=2, b=2)[:, :, 0, :], dl.v.r("p (a b d) -> p a b d", a=2, b=2)[:, :, 1, :],
              ALU.mult)
        ss = fw.sb("ss", [64, 2], F32)
        fw.reduce(ss.v, pr.v)
        ee = fw.sb("ee", [64, 2], F32)
        fw.act(ee.v, ss.v, AF.Exp)
        nlam = fw.sb("nlam", [64, 1], F32)
        fw.tt(nlam.v, ee.v[:, 1:2], ee.v[:, 0:1], ALU.subtract)
        fw.ts(nlam.v, nlam.v, -lam_init, None, ALU.add)
        sub = fw.sb("sub", [64, 1], F32)
        fw.dma("act", sub.v, I.df_sub[l])
        fw.ts(sub.v, sub.v, 1.0 - lam_init, None, ALU.mult)
        pts = [fw.sb(f"pt{i}", [128, 512], BF16) for i in range(3)]
        osb = [fw.sb(f"osb{i}", [65, 512], F32) for i in range(2)]
        lnr = fw.sb("lnr", [64, 512], F32)
        rec = fw.sb("rec", [64, 512], F32)
        o01 = [fw.sb(f"o01{i}", [64, 512], F32) for i in range(2)]
        oo = fw.sb("oo", [64, 512], F32)
        sqo = fw.sb("sqo", [64, 512], F32)
        osts = [fw.sb(f"ost{i}", [64, 4, 512], BF16) for i in range(2)]
        cnt = 0
        pend = []
        sk = Skew(2)

        def flush():
            while pend:
                pend.pop(0)()
        for qi, (q0, NQ, nk) in enumerate(_qtiles(need_ctx)):
            ost = osts[qi % 2]
            for h in range(4):
                c = h // 2
                for j in range(nk):
                    for n in range(2):
                        g = (h % 2) * 2 + n
                        sp = P[cnt % 4]
                        pt = pts[cnt % 3]
                        cnt += 1
                        fw.mm(sp.v[:, 0:NQ], Kd.v[:, c, j * 128:(j + 1) * 128], Qm.v[:, c, g, q0:q0 + NQ])
                        fw.act(pt.v[:, 0:NQ], sp.v[:, 0:NQ], AF.Exp, scale=scale)
                        sk.push(lambda n=n, j=j, h=h, pt=pt, NQ=NQ, nk=nk: fw.mm(
                            P[4 + n].v[:, 0:NQ], Vv.v[:, j, h, :], pt.v[:, 0:NQ], start=(j == 0), stop=(j == nk - 1)))
                    if j == min(3, nk - 1):
                        flush()

                def norm(h=h, NQ=NQ, ost=ost, q0=q0):
                    for n in range(2):
                        bcp = P[6 + n]
                        fw.mm(bcp.v[0:64, 0:NQ], G.esel.v[0:65, 0:64], osb[n].v[0:65, 0:NQ])
                        fw.act(lnr.v[0:64, 0:NQ], bcp.v[0:64, 0:NQ], AF.Ln)
                        fw.act(rec.v[0:64, 0:NQ], lnr.v[0:64, 0:NQ], AF.Exp, scale=-1.0)
                        fw.tt(o01[n].v[:, 0:NQ], osb[n].v[0:64, 0:NQ], rec.v[0:64, 0:NQ], ALU.mult, eng="pool")
                    fw.stt(oo.v[:, 0:NQ], o01[1].v[:, 0:NQ], nlam.v[:, 0:1], o01[0].v[:, 0:NQ], ALU.mult, ALU.add)
                    fw.tt(sqo.v[:, 0:NQ], oo.v[:, 0:NQ], oo.v[:, 0:NQ], ALU.mult, eng="pool")
                    fw.mm(P[6].v[0:64, 0:NQ], G.blockones.v[0:64, 0:64], sqo.v[:, 0:NQ])
                    rstd_from_sum(S, rec.v[:, 0:NQ], P[6].v[0:64, 0:NQ], 64.0, 1e-5, lnr.v[:, 0:NQ])
                    fw.stt(ost.v[:, h, 0:NQ], oo.v[:, 0:NQ], sub.v[:, 0:1], rec.v[:, 0:NQ], ALU.mult, ALU.mult)
                    if h == 3:
                        for hh in range(4):
                            hb = (hh % 2) * 64
                            fw.dma("sp" if hh % 2 else "act", S.MIXT[hb:hb + 64, 6 + hh // 2, q0:q0 + NQ], ost.v[:, hh, 0:NQ])

                def headend(NQ=NQ, norm=norm):
                    for n in range(2):
                        fw.copy(osb[n].v[0:65, 0:NQ], P[4 + n].v[0:65, 0:NQ], "dve")
                    pend.append(norm)
                sk.push(headend)
        sk.drain()
        flush()


def _na_rs(r):
    return min(max(r - 4, 0), 24)


def phase_na(S, b, l, need_ctx):
    fw, I, G, P = S.fw, S.I, S.G, S.P
    scale = 0.125
    with fw.phase():
        QK = fw.sb("QKn", [128, 4, T], BF16)
        fw.dma("sp", QK.v, S.QK_na.v)
        Vv = fw.sb("Vv", [128, 18, 4, 128], BF16)
        fw.dma("act", Vv.v, S.V_na.v.r("j p h c -> p j h c"))
        E = fw.sb("E", [128, 4, 15, 64], F32)
        msk = fw.sb("msk", [128, 64], F32)
        for hh in range(2):
            fw.dma("sp", E.v[hh * 64:(hh + 1) * 64], I.na_bias[l])
            fw.dma("act", msk.v[hh * 64:(hh + 1) * 64], I.c["na_mask"].v)
        Ef = E.v.r("p h d q -> p (h d) q")
        fw.act(Ef, Ef, AF.Exp)
        fw.tt(Ef, Ef, msk.v.bc(1, 60), ALU.mult)
        pts = [fw.sb(f"pt{i}", [128, 512], BF16) for i in range(3)]
        ptf = [fw.sb(f"ptf{i}", [128, 512], F32) for i in range(2)]
        osb = fw.sb("osb", [65, 512], F32)
        osm = fw.sb("osm", [65, 512], F32)
        lnr = fw.sb("lnr", [64, 512], F32)
        rec = fw.sb("rec", [64, 512], F32)
        osts = [fw.sb(f"ost{i}", [64, 4, 512], BF16) for i in range(2)]
        cnt = 0
        pend = []
        sk = Skew(2)

        def flush():
            while pend:
                pend.pop(0)()
        qtiles = [(i * 8, TC + i * 512, 512) for i in range(4)]
        if need_ctx:
            qtiles.append((None, 0, 256))
        for qi, (r0, q0, NQ) in enumerate(qtiles):
            ost = osts[qi % 2]
            work = []
            for i in range(4):
                work.append(((i % 2) * 64, i // 2, i * 64, 0, NQ, None))
            if r0 is not None:
                for kr in range(_na_rs(r0), _na_rs(r0 + 7) + 8):
                    rr = [r for r in range(r0, r0 + 8) if _na_rs(r) <= kr < _na_rs(r) + 8]
                    if not rr:
                        continue
                    ra, rb = rr[0], rr[-1]
                    work.append(((kr % 2) * 64, 2 + kr // 2, TC + kr * 64, (ra - r0) * 64, (rb - r0 + 1) * 64,
                                 (ra - kr + 7, rb - kr + 8)))
            last = {}
            for wi, w in enumerate(work):
                last[w[0]] = wi
            for h in range(4):
                hb = (h % 2) * 64
                cq, ck = h // 2, 2 + h // 2
                started = {0: False, 64: False}
                for wi, (kb, vtile, ktok, ca, cb, eidx) in enumerate(work):
                    sp = P[cnt % 4]
                    pt = pts[cnt % 3]
                    pf = ptf[cnt % 2]
                    cnt += 1
                    acc = P[4] if kb == 0 else P[5]
                    fw.mm(sp.v[kb:kb + 64, ca:cb], QK.v[hb:hb + 64, ck, ktok:ktok + 64],
                          QK.v[hb:hb + 64, cq, q0 + ca:q0 + cb])
                    if eidx is None:
                        fw.act(pt.v[kb:kb + 64, ca:cb], sp.v[kb:kb + 64, ca:cb], AF.Exp, scale=scale)
                    else:
                        fw.act(pf.v[kb:kb + 64, ca:cb], sp.v[kb:kb + 64, ca:cb], AF.Exp, scale=scale)
                        ev = E.v[kb:kb + 64, h, eidx[0]:eidx[1], :]
                        fw.tt(pt.v[kb:kb + 64, ca:cb].r("p (r q) -> p r q", q=64), pf.v[kb:kb + 64, ca:cb].r("p (r q) -> p r q", q=64),
                              ev, ALU.mult)
                    sk.push(lambda acc=acc, ca=ca, cb=cb, kb=kb, vtile=vtile, h=h, pt=pt, st=(not started[kb]), sp_=(last[kb] == wi):
                            fw.mm(acc.v[:, ca:cb], Vv.v[kb:kb + 64, vtile, h, :], pt.v[kb:kb + 64, ca:cb], start=st, stop=sp_))
                    started[kb] = True
                    if wi == min(5, len(work) - 1):
                        flush()

                def norm(h=h, NQ=NQ, ost=ost, q0=q0):
                    fw.mm(P[6].v[0:64, 0:NQ], G.esel.v[0:65, 0:64], osm.v[0:65, 0:NQ])
                    fw.act(lnr.v[0:64, 0:NQ], P[6].v[0:64, 0:NQ], AF.Ln)
                    fw.act(rec.v[0:64, 0:NQ], lnr.v[0:64, 0:NQ], AF.Exp, scale=-1.0)
                    fw.tt(ost.v[:, h, 0:NQ], osm.v[0:64, 0:NQ], rec.v[0:64, 0:NQ], ALU.mult, eng="pool")
                    if h == 3:
                        for hh in range(4):
                            hb = (hh % 2) * 64
                            fw.dma("sp" if hh % 2 else "act", S.MIXT[hb:hb + 64, 0 + hh // 2, q0:q0 + NQ], ost.v[:, hh, 0:NQ])

                def headend(NQ=NQ, norm=norm):
                    fw.copy(osb.v[0:65, 0:NQ], P[4].v[0:65, 0:NQ], "dve")
                    fw.tt(osm.v[0:65, 0:NQ], osb.v[0:65, 0:NQ], P[5].v[0:65, 0:NQ], ALU.add)
                    pend.append(norm)
                sk.push(headend)
        sk.drain()
        flush()


def phase_O(S, b, l, need_ctx):
    fw, I, G = S.fw, S.I, S.G
    with fw.phase():
        wo = fw.sb("wo", [128, 8, D], BF16)
        fw.dma("sp", wo.v, S.w_out_bf[l].r("(k p) n -> p k n", p=128))
        mxs = [fw.sb(f"mx{i}", [128, 8, 512], BF16) for i in range(2)]
        xts = [fw.sb(f"xto{i}", [128, 8, 512], F32) for i in range(2)]
        tiles = TOK_TILES[1:] + ([TOK_TILES[0]] if need_ctx else [])
        for ti, (t0, N) in enumerate(tiles):
            vec = b if t0 >= TC else S.nb
            mx, xt = mxs[ti % 2], xts[ti % 2]
            fw.dma("act", mx.v[:, :, 0:N], S.MIXT[:, :, t0:t0 + N])
            fw.dma("sp", xt.v[:, :, 0:N], S.XT[b, :, :, t0:t0 + N])
            for m in range(8):
                ps = nps(S)
                for k in range(8):
                    fw.mm(ps.v[:, 0:N], wo.v[:, k, m * 128:(m + 1) * 128], mx.v[:, k, 0:N], start=(k == 0), stop=(k == 7))
                fw.stt(xt.v[:, m, 0:N], ps.v[:, 0:N], G.mods.v[:, l, 16 + m, vec:vec + 1], xt.v[:, m, 0:N],
                       ALU.mult, ALU.add)
            fw.dma("sp", S.XT[b, :, :, t0:t0 + N], xt.v[:, :, 0:N])


def phase_mlp(S, b, l, need_ctx):
    fw, I, G, P = S.fw, S.I, S.G, S.P
    with fw.phase():
        cw = fw.sb("cw", [128, 3, 22], F32)
        cb = fw.sb("cb", [128, 22], F32)
        fw.dma("act", cw.v, I.conv_w[l])
        fw.dma("act", cb.v, I.conv_b[l])
        hwl = fw.sb("hwl", [128, 8, TL + 2], BF16)
        fw.memset(hwl.v[:, :, 0:1], 0.0, "pool")
        fw.memset(hwl.v[:, :, TL + 1:TL + 2], 0.0, "pool")
        if need_ctx:
            hwc = fw.sb("hwc", [128, 8, TC + 2], BF16)
            fw.memset(hwc.v[:, :, 0:1], 0.0, "pool")
            fw.memset(hwc.v[:, :, TC + 1:TC + 2], 0.0, "pool")
        tiles = TOK_TILES[1:] + ([TOK_TILES[0]] if need_ctx else [])
        with fw.phase():
            xw = fw.sb("xw", [128, 8, 512], F32)
            sq = fw.sb("sqw", [128, 8, 512], BF16)
            xn = fw.sb("xnw", [128, 8, 512], F32)
            rstd = fw.sb("rstdw", [128, 512], F32)
            lnt = fw.sb("lntw", [128, 512], F32)
            for (t0, n) in tiles:
                lat = t0 >= TC
                vec = b if lat else S.nb
                fw.dma("sp", xw.v[:, :, 0:n], S.XT[b, :, :, t0:t0 + n])
                fw.act(sq.v[:, :, 0:n], xw.v[:, :, 0:n], AF.Square)
                ps = nps(S)
                for k in range(8):
                    fw.mm(ps.v[:, 0:n], G.ones_bf.v, sq.v[:, k, 0:n], start=(k == 0), stop=(k == 7))
                rstd_from_sum(S, rstd.v[:, 0:n], ps.v[:, 0:n], 1024.0, EPS, lnt.v[:, 0:n])
                fw.tt(xn.v[:, :, 0:n], xw.v[:, :, 0:n], rstd.v[:, 0:n].bc(1, 8), ALU.mult)
                for k in range(8):
                    dst = hwl.v[:, k, 1 + t0 - TC:1 + t0 - TC + n] if lat else hwc.v[:, k, 1:1 + n]
                    fw.act(dst, xn.v[:, k, 0:n], AF.Identity, scale=G.A.v[:, l, 1, k, vec:vec + 1],
                           bias=G.mods.v[:, l, 24 + k, vec:vec + 1])
        gT = fw.sb("gT", [128, 22, TL], BF16)
        asb = fw.sb("asb", [128, TL + 2], F32)
        fw.memset(asb.v[:, 0:1], 0.0, "pool")
        fw.memset(asb.v[:, TL + 1:TL + 2], 0.0, "pool")
        c1 = fw.sb("c1", [128, TL], F32)
        c2 = fw.sb("c2", [128, TL], F32)
        if need_ctx:
            gTc = fw.sb("gTc", [128, 22, TC], BF16)
            asc = fw.sb("asc", [128, TC + 2], F32)
            fw.memset(asc.v[:, 0:1], 0.0, "pool")
            fw.memset(asc.v[:, TC + 1:TC + 2], 0.0, "pool")
            c1c = fw.sb("c1c", [128, TC], F32)
            c2c = fw.sb("c2c", [128, TC], F32)
        wus = [fw.sb(f"wu{i}", [128, 2, 8, 128], BF16) for i in range(2)]
        for j in range(22):
            wu = wus[j % 2]
            fw.dma("sp" if j % 2 else "act", wu.v, S.w_up_bf[l, j])
            for i in range(4):
                for k in range(8):
                    fw.mm(P[i].v, wu.v[:, 0, k, :], hwl.v[:, k, 1 + i * 512:1 + (i + 1) * 512], start=(k == 0), stop=(k == 7))
            for i in range(4):
                for k in range(8):
                    fw.mm(P[4 + i].v, wu.v[:, 1, k, :], hwl.v[:, k, 1 + i * 512:1 + (i + 1) * 512], start=(k == 0), stop=(k == 7))
            for i in range(4):
                fw.copy(asb.v[:, 1 + i * 512:1 + (i + 1) * 512], P[i].v, "act")
            fw.act(c1.v, asb.v[:, 0:TL], AF.Identity, scale=cw.v[:, 0, j:j + 1])
            fw.stt(c2.v, asb.v[:, 1:TL + 1], cw.v[:, 1, j:j + 1], c1.v, ALU.mult, ALU.add)
            fw.stt(c1.v, asb.v[:, 2:TL + 2], cw.v[:, 2, j:j + 1], c2.v, ALU.mult, ALU.add)
            fw.act(c2.v, c1.v, AF.Silu, bias=cb.v[:, j:j + 1])
            for i in range(4):
                fw.tt(gT.v[:, j, i * 512:(i + 1) * 512], c2.v[:, i * 512:(i + 1) * 512], P[4 + i].v, ALU.mult)
            if need_ctx:
                pc = P[0]
                for k in range(8):
                    fw.mm(pc.v[:, 0:TC], wu.v[:, 0, k, :], hwc.v[:, k, 1:TC + 1], start=(k == 0), stop=(k == 7))
                for k in range(8):
                    fw.mm(pc.v[:, TC:2 * TC], wu.v[:, 1, k, :], hwc.v[:, k, 1:TC + 1], start=(k == 0), stop=(k == 7))
                fw.copy(asc.v[:, 1:TC + 1], pc.v[:, 0:TC], "act")
                fw.act(c1c.v, asc.v[:, 0:TC], AF.Identity, scale=cw.v[:, 0, j:j + 1])
                fw.stt(c2c.v, asc.v[:, 1:TC + 1], cw.v[:, 1, j:j + 1], c1c.v, ALU.mult, ALU.add)
                fw.stt(c1c.v, asc.v[:, 2:TC + 2], cw.v[:, 2, j:j + 1], c2c.v, ALU.mult, ALU.add)
                fw.act(c2c.v, c1c.v, AF.Silu, bias=cb.v[:, j:j + 1])
                fw.tt(gTc.v[:, j, :], c2c.v, pc.v[:, TC:2 * TC], ALU.mult)
        wds = [fw.sb(f"wd{i}", [128, 22, 128], BF16) for i in range(2)]
        xt = fw.sb("xtm", [128, 8, 512], F32)
        it = 0
        for (t0, n) in tiles:
            lat = t0 >= TC
            vec = b if lat else S.nb
            fw.dma("act", xt.v[:, :, 0:n], S.XT[b, :, :, t0:t0 + n])
            for m in range(8):
                wd = wds[it % 2]
                it += 1
                fw.dma("sp", wd.v, S.w_down_bf[l, m])
                ps = nps(S)
                for j in range(22):
                    rhs = gT.v[:, j, t0 - TC:t0 - TC + n] if lat else gTc.v[:, j, 0:n]
                    fw.mm(ps.v[:, 0:n], wd.v[:, j, :], rhs, start=(j == 0), stop=(j == 21))
                fw.stt(xt.v[:, m, 0:n], ps.v[:, 0:n], G.mods.v[:, l, 40 + m, vec:vec + 1], xt.v[:, m, 0:n],
                       ALU.mult, ALU.add)
            fw.dma("sp", S.XT[b, :, :, t0:t0 + n], xt.v[:, :, 0:n])


def phase_final(S):
    fw, I, G = S.fw, S.I, S.G
    with fw.phase():
        gf = fw.sb("gf", [128, 8], F32)
        fw.dma("sp", gf.v, I.gfT.v)
        xts = [fw.sb(f"xf{i}", [128, 8, 128], F32) for i in range(2)]
        sq = fw.sb("sqf", [128, 8, 128], BF16)
        xn = fw.sb("xnf", [128, 8, 128], F32)
        rstd = fw.sb("rstdf", [128, 128], F32)
        lnt = fw.sb("lntf", [128, 128], F32)
        ots = [fw.sb(f"of{i}", [128, D], F32) for i in range(2)]
        it = 0
        for b in range(S.nb):
            for tt_ in range(TL // 128):
                xt, ot = xts[it % 2], ots[it % 2]
                it += 1
                t0 = TC + tt_ * 128
                fw.dma("sp", xt.v, S.XT[b, :, :, t0:t0 + 128])
                fw.act(sq.v, xt.v, AF.Square)
                ps = nps(S)
                for k in range(8):
                    fw.mm(ps.v[:, 0:128], G.ones_bf.v, sq.v[:, k, :], start=(k == 0), stop=(k == 7))
                rstd_from_sum(S, rstd.v, ps.v[:, 0:128], 1024.0, EPS, lnt.v)
                fw.tt(xn.v, xt.v, rstd.v.bc(1, 8), ALU.mult)
                fw.tt(xn.v, xn.v, gf.v.bc(2, 128), ALU.mult)
                for half in range(2):
                    p = nps(S)
                    for kk in range(4):
                        k = half * 4 + kk
                        fw.op("pe", lambda h: h.transpose(p.v[:, kk * 128:(kk + 1) * 128].ap, xn.v[:, k, :].ap,
                                                           G.ident.v.ap), [xn.v, G.ident.v], [p.v])
                    fw.copy(ot.v[:, half * 512:(half + 1) * 512], p.v, "act" if half else "dve")
                fw.dma("act", S.out[b, tt_ * 128:(tt_ + 1) * 128, :], ot.v)


RD = BF16
LDK = 0.6065306597126334
RK_EPS = 64e-5


def phase_rk(S, b, l, need_ctx):
    fw, I, G, P = S.fw, S.I, S.G, S.P
    SEG = 256
    NSEG = T // SEG
    with fw.phase():
        mu = fw.sb("mu", [128, 2, 7], F32)
        fw.dma("sp", mu.v, I.rk_mu[l])
        c0 = fw.sb("c0", [128, 7], F32)
        fw.tt(c0.v, mu.v[:, 0, :], mu.v[:, 1, :], ALU.add)
        fw.ts(c0.v, c0.v, -1.0, 1.0, ALU.mult, ALU.add)
        w0 = fw.sb("w0", [128, 2, 2], F32)
        a0 = fw.sb("a0", [128, 2, 2], F32)
        fw.dma("sp", w0.v, I.rk_w0[l])
        fw.dma("sp", a0.v, I.rk_a0[l])
        wup = fw.sb("wup", [32, 2, 256], F32)
        fw.dma("act", wup.v, I.rk_wup[l].r("d r c -> r d c"))
        aup = fw.sb("aup", [64, 2, 256], F32)
        fw.dma("act", aup.v[32:64], I.rk_aup[l].r("d r c -> r d c"))
        gup = fw.sb("gup", [128, 256], F32)
        fw.dma("act", gup.v[64:128], I.rk_gup[l])
        kkp = fw.sb("kkp", [128, 2], F32)
        ka = fw.sb("ka", [128, 2], F32)
        omka = fw.sb("omka", [128, 2], F32)
        rkp = fw.sb("rkp", [128, 2], F32)
        fw.dma("sp", kkp.v, I.rk_kk[l])
        fw.dma("sp", ka.v, I.rk_ka[l])
        fw.dma("sp", rkp.v, I.rk_rk[l])
        fw.ts(omka.v, ka.v, -1.0, 1.0, ALU.mult, ALU.add)
        lnw = fw.sb("lnw", [128, 2, 64], F32)
        lnb = fw.sb("lnb", [128, 2, 64], F32)
        for hh in range(2):
            fw.dma("sp", lnw.v[hh * 64:(hh + 1) * 64],
                   I.rk_lnw[l:l + 1, :].r("o (hp hh i) -> o hp hh i", hp=2, hh=2)[:, :, hh, :].f(
                       lambda a: a.broadcast_to([64, 2, 64])))
            fw.dma("act", lnb.v[hh * 64:(hh + 1) * 64],
                   I.rk_lnb[l:l + 1, :].r("o (hp hh i) -> o hp hh i", hp=2, hh=2)[:, :, hh, :].f(
                       lambda a: a.broadcast_to([64, 2, 64])))
        eye = fw.sb("eye", [128, 64], F32)
        fw.dma("sp", eye.v, I.c["rk_eye"].v)
        mkk = fw.sb("mkk", [128, 2, 2, 64], F32)
        fw.dma("sp", mkk.v, I.c["rk_mask"].v)
        bm = fw.sb("bm", [128, 2, 64], F32)
        fw.memset(bm.v, 0.0)
        fw.memset(bm.v[0:64, 0, :], 1.0)
        fw.memset(bm.v[64:128, 1, :], 1.0)
        mbd = fw.sb("mbd", [128, 2, 2, 2, 64], F32)
        for d in range(2):
            fw.tt(mbd.v[:, d], mkk.v[:, d].bc(2, 2), bm.v.bc(1, 2), ALU.mult)
        rm = fw.sb("rm", [128, 2, 512], F32)
        fw.dma("act", rm.v, I.c["rk_rm"].v)
        bg = fw.sb("bg", [128, 2, 2, T], BF16)
        Yacc = fw.sb("Yacc", [128, 36, 2, 64], F32)
        fw.memset(Yacc.v, 0.0, "pool")
        ST = fw.sb("ST", [128, 2, 2, 64], F32)
        fw.memset(ST.v, 0.0)
        if RD == F32:
            STb, identR = ST, G.ident
        else:
            STb = fw.sb("STb", [128, 2, 2, 64], RD)
            fw.memset(STb.v, 0.0)
            identR = fw.sb("identR", [128, 128], RD)
            fw.copy(identR.v, G.ident.v)

        with fw.phase():
            zw = fw.sb("zw", [128, 7, SEG + 2], F32)
            zs = fw.sb("zs", [128, 7, SEG], F32)
            zt = fw.sb("zt", [128, 7, SEG], F32)
            kk = fw.sb("kk", [128, 2, SEG], F32)
            kq = fw.sb("kq", [128, 2, SEG], F32)
            kkn = fw.sb("kkn", [128, 2, SEG], F32)
            lnk = fw.sb("lnk", [128, 2, SEG], F32)
            twd = fw.sb("twd", [32, SEG], F32)
            sgd = fw.sb("sgd", [128, SEG], F32)
            vbd = fw.sb("vbd", [128, 2, 4, 2, 64], F32)
            X = NS()
            for nm in ("sig", "asig", "Ls", "ta", "tb", "Bt", "Kt", "Bh", "Kh"):
                setattr(X, nm, fw.sb(nm, [128, 2, SEG], F32))
            X.Btr = X.Bt if RD == F32 else fw.sb("Btr", [128, 2, SEG], RD)
            X.Ktr = X.Kt if RD == F32 else fw.sb("Ktr", [128, 2, SEG], RD)
            X.AR = fw.sb("AR", [128, 2, 4, 2, 64], RD)
            X.Btbd = fw.sb("Btbd", [128, 2, 4, 2, 64], RD)
            X.Ktbd = fw.sb("Ktbd", [128, 2, 4, 2, 64], RD)
            X.Bhbd = fw.sb("Bhbd", [128, 2, 4, 2, 64], RD)
            X.Khbd = fw.sb("Khbd", [128, 2, 4, 2, 64], RD)
            X.Nn = [[fw.sb(f"Nn{h}{i}", [128, 4, 128], RD) for i in range(2)] for h in range(2)]
            X.NT = [[fw.sb(f"NT{h}{i}", [128, 4, 128], RD) for i in range(2)] for h in range(2)]
            X.Xx = [[fw.sb(f"Xx{h}{i}", [128, 4, 128], RD) for i in range(2)] for h in range(2)]
            X.LC = fw.sb("LC", [128, 2, 4], F32)
            SX = []
            for i in range(2):
                Y = NS()
                Y.ARbd = fw.sb(f"ARbd{i}", [128, 2, 4, 2, 2, 64], RD)
                Y.BhT = fw.sb(f"BhT{i}", [128, 2, 4, 128], RD)
                Y.KhT = fw.sb(f"KhT{i}", [128, 2, 4, 128], RD)
                Y.MB = fw.sb(f"MB{i}", [128, 2, 4, 2, 128], RD)
                Y.MK = fw.sb(f"MK{i}", [128, 2, 4, 2, 128], RD)
                Y.TT = fw.sb(f"TT{i}", [128, 2, 4, 128], RD)
                Y.PC = fw.sb(f"PC{i}", [128, 2, 4], F32)
                Y.VT = fw.sb(f"VT{i}", [128, 2, 4, 64], RD)
                SX.append(Y)
            RHSsb = fw.sb("RHSsb", [128, 2, 64], RD)
            Usb = fw.sb("Usb", [128, 2, 64], RD)

            def c4(v):
                return v.r("p a (c s) -> p a c s", s=64)

            def prep(seg, d, Y):
                t0 = seg * SEG
                s0, s1 = (0, TC) if seg == 0 else (TC, T)
                w0_, w1_ = max(t0 - 1, s0), min(t0 + SEG + 1, s1)
                cc, n = w0_ - (t0 - 1), w1_ - w0_
                if n < SEG + 2:
                    fw.memset(zw.v, 0.0, "pool")
                fw.dma("sp", zw.v[:, :, cc:cc + n], S.ZT_rk[:, :, w0_:w1_])
                yield
                fw.tt(zs.v, zw.v[:, :, 1:SEG + 1], c0.v.bc(2, SEG), ALU.mult)
                yield
                fw.tt(zt.v, zw.v[:, :, 0:SEG], mu.v[:, 0, :].bc(2, SEG), ALU.mult, eng="pool")
                yield
                fw.tt(zs.v, zs.v, zt.v, ALU.add)
                yield
                fw.tt(zt.v, zw.v[:, :, 2:SEG + 2], mu.v[:, 1, :].bc(2, SEG), ALU.mult, eng="pool")
                yield
                fw.tt(zs.v, zs.v, zt.v, ALU.add)
                yield
                yield
                r_, k_, v_ = zs.v[:, 0:2, :], zs.v[:, 2:4, :], zs.v[:, 4:6, :]
                fw.act(twd.v, zs.v[0:32, 6, :], AF.Tanh)
                yield
                fw.act(sgd.v[64:128], zs.v[64:128, 6, :], AF.Sigmoid)
                yield
                fw.tt(kk.v, k_, kkp.v.bc(2, SEG), ALU.mult)
                yield
                fw.tt(kq.v, kk.v, kk.v, ALU.mult, eng="pool")
                yield
                ps = P[0]
                fw.mm(ps.v[:, 0:2 * SEG], G.blockones.v, kq.v.r("p a t -> p (a t)"))
                yield
                fw.ts(lnk.v.r("p a t -> p (a t)"), ps.v[:, 0:2 * SEG], 1e-24, None, ALU.max)
                yield
                fw.act(lnk.v, lnk.v, AF.Ln)
                yield
                fw.act(lnk.v, lnk.v, AF.Exp, scale=-0.5)
                yield
                fw.tt(kkn.v, kk.v, lnk.v, ALU.mult)
                yield
                yield
                fw.tt(kq.v, r_, k_, ALU.mult, eng="pool")
                yield
                fw.tt(kq.v, kq.v, rkp.v.bc(2, SEG), ALU.mult, eng="pool")
                yield
                ps = P[1]
                fw.mm(ps.v[:, 0:2 * SEG], G.blockones.v, kq.v.r("p a t -> p (a t)"))
                yield
                fw.tt(bg.v[:, :, 0, t0:t0 + SEG], ps.v[:, 0:2 * SEG].r("p (a t) -> p a t", a=2), v_, ALU.mult)
                yield
                ps = P[2]
                for hp in range(2):
                    fw.mm(ps.v[:, hp * SEG:(hp + 1) * SEG], gup.v[64:128, hp * 128:(hp + 1) * 128], sgd.v[64:128, :])
                fw.copy(bg.v[:, :, 1, t0:t0 + SEG], ps.v[:, 0:2 * SEG].r("p (a t) -> p a t", a=2), "act")
                yield
                for hp in range(2):
                    fw.tt(vbd.v[:, hp], c4(v_)[:, hp].bc(2, 2), bm.v.bc(1, 4), ALU.mult, eng="pool")
                ps = P[3]
                for hp in range(2):
                    for c in range(4):
                        fw.mm(ps.v[:, (hp * 4 + c) * 64:(hp * 4 + c + 1) * 64],
                              vbd.v[:, hp, c].r("p a s -> p (a s)"), eye.v)
                fw.copy(Y.VT.v.r("p a c i -> p (a c i)"), ps.v[:, 0:512], "act")
                yield
                yield
                ps = P[0]
                for hp in range(2):
                    fw.mm(ps.v[:, hp * SEG:(hp + 1) * SEG], wup.v[0:32, d, hp * 128:(hp + 1) * 128], twd.v[0:32, :])
                for hp in range(2):
                    fw.act(X.sig.v[:, hp, :], ps.v[:, hp * SEG:(hp + 1) * SEG], AF.Sigmoid, bias=w0.v[:, d, hp:hp + 1])
                ps = P[1]
                for hp in range(2):
                    fw.mm(ps.v[:, hp * SEG:(hp + 1) * SEG], aup.v[32:64, d, hp * 128:(hp + 1) * 128], zs.v[32:64, 6, :])
                for hp in range(2):
                    fw.act(X.asig.v[:, hp, :], ps.v[:, hp * SEG:(hp + 1) * SEG], AF.Sigmoid, bias=a0.v[:, d, hp:hp + 1])
                for hp in range(2):
                    fw.ts(X.tb.v[:, hp, :], X.asig.v[:, hp, :], ka.v[:, hp:hp + 1], omka.v[:, hp:hp + 1], ALU.mult, ALU.add)
                fw.tt(X.tb.v, X.tb.v, k_, ALU.mult)
                fw.tt(X.ta.v, kkn.v, X.asig.v, ALU.mult, eng="pool")
                sf = X.sig.v.r("p a t -> p (a t)")
                lf = X.Ls.v.r("p a t -> p (a t)")
                if d == 0:
                    fw.scan(lf, rm.v[:, 0, :], sf)
                else:
                    fw.scan(lf[:, ::-1], rm.v[:, 1, ::-1], sf[:, ::-1])
                yield
                L4 = c4(X.Ls.v)
                lc = L4[:, :, :, 63:64] if d == 0 else L4[:, :, :, 0:1]
                fw.copy(X.LC.v, lc.r("p a c o -> p a (c o)"), "pool")
                yield
                fw.act(Y.PC.v, X.LC.v, AF.Exp, scale=-LDK)
                yield
                eN, eC, eP, ePm = X.Bh, X.Kh, X.Bt, X.Kt
                fw.act(eN.v, X.Ls.v, AF.Exp, scale=LDK)
                yield
                fw.act(eP.v, X.Ls.v, AF.Exp, scale=-LDK)
                yield
                fw.tt(X.asig.v, X.Ls.v, X.sig.v, ALU.subtract)
                fw.act(ePm.v, X.asig.v, AF.Exp, scale=-LDK)
                yield
                fw.tt(c4(X.asig.v), X.LC.v.bc(3, 64), L4, ALU.subtract, eng="pool")
                AR = X.AR.v
                fw.stt(AR[:, :, :, 0, :], c4(kkn.v), -1.0, c4(ePm.v), ALU.mult, ALU.mult)
                fw.tt(AR[:, :, :, 1, :], c4(r_), c4(eP.v), ALU.mult)
                yield
                fw.tt(X.Btr.v, X.ta.v, eN.v, ALU.mult)
                yield
                fw.tt(X.Ktr.v, X.tb.v, eN.v, ALU.mult)
                yield
                fw.act(eC.v, X.asig.v, AF.Exp, scale=-LDK)
                yield
                fw.tt(X.Bh.v, X.ta.v, eC.v, ALU.mult)
                yield
                fw.tt(X.Kh.v, X.tb.v, eC.v, ALU.mult, eng="pool")
                yield
                yield
                for hp in range(2):
                    bmc = bm.v.bc(1, 4)
                    fw.tt(X.Btbd.v[:, hp], c4(X.Btr.v)[:, hp].bc(2, 2), bmc, ALU.mult, eng="pool")
                    fw.tt(X.Ktbd.v[:, hp], c4(X.Ktr.v)[:, hp].bc(2, 2), bmc, ALU.mult)
                    for ar in range(2):
                        fw.tt(Y.ARbd.v[:, hp, :, ar], AR[:, hp, :, ar, :].bc(2, 2), bmc, ALU.mult,
                              eng="pool" if ar else "dve")
                yield
                banks = [(P[3], P[0], P[1]), (P[2], P[4], P[5])]
                for hp in range(2):
                    pB, pK, pN = banks[hp]
                    for c in range(4):
                        arc = AR[:, hp, c].r("p a t -> p (a t)")
                        fw.mm(pB.v[:, c * 128:(c + 1) * 128], X.Btbd.v[:, hp, c].r("p a s -> p (a s)"), arc)
                        fw.mm(pK.v[:, c * 128:(c + 1) * 128], X.Ktbd.v[:, hp, c].r("p a s -> p (a s)"), arc)
                        fw.mm(pN.v[:, c * 64:(c + 1) * 64], Y.ARbd.v[:, hp, c, 0].r("p a t -> p (a t)"),
                              X.Btr.v[:, hp, c * 64:(c + 1) * 64])
                    for ar in range(2):
                        mk_ = mbd.v[:, d, ar].bc(1, 4)
                        fw.tt(Y.MB.v[:, hp, :, ar].r("p c (a t) -> p c a t", a=2),
                              pB.v.r("p (c a t) -> p c a t", c=4, a=2)[:, :, ar, :].bc(2, 2), mk_, ALU.mult)
                        fw.tt(Y.MK.v[:, hp, :, ar].r("p c (a t) -> p c a t", a=2),
                              pK.v.r("p (c a t) -> p c a t", c=4, a=2)[:, :, ar, :].bc(2, 2), mk_, ALU.mult,
                              eng="dve")
                    fw.tt(X.NT[hp][0].v.r("p c (a t) -> p c a t", a=2), pN.v[:, 0:256].r("p (c t) -> p c t", c=4).bc(2, 2),
                          mbd.v[:, 1 - d, 0].bc(1, 4), ALU.mult)
                    fw.tt(X.Xx[hp][0].v, Y.MB.v[:, hp, :, 0], G.ident.v.bc(1, 4), ALU.add, eng="pool")
                yield
                Ncur = [Y.MB.v[:, hp, :, 0] for hp in range(2)]
                NTcur = [X.NT[hp][0].v for hp in range(2)]
                Xcur = [X.Xx[hp][0].v for hp in range(2)]
                for lev in range(1, 6):
                    NTn = [X.NT[hp][lev % 2].v for hp in range(2)]
                    Nn = [X.Nn[hp][lev % 2].v for hp in range(2)]
                    for hp in range(2):
                        p1, p2, p3 = banks[hp]
                        for c in range(4):
                            cs = slice(c * 128, (c + 1) * 128)
                            if lev < 5:
                                fw.mm(p1.v[:, cs], NTcur[hp][:, c, :], Ncur[hp][:, c, :])
                            fw.mm(p2.v[:, cs], Ncur[hp][:, c, :], NTcur[hp][:, c, :])
                    for hp in range(2):
                        p1, p2, p3 = banks[hp]
                        if lev < 5:
                            fw.copy(Nn[hp], p1.v.r("p (c t) -> p c t", c=4), "act")
                        fw.copy(NTn[hp], p2.v.r("p (c t) -> p c t", c=4), "dve")
                    for hp in range(2):
                        p1, p2, p3 = banks[hp]
                        for c in range(4):
                            fw.mm(p3.v[:, c * 128:(c + 1) * 128], NTn[hp][:, c, :], Xcur[hp][:, c, :])
                    for hp in range(2):
                        p1, p2, p3 = banks[hp]
                        Xn = X.Xx[hp][lev % 2].v if lev < 5 else Y.TT.v[:, hp]
                        fw.tt(Xn, p3.v.r("p (c t) -> p c t", c=4), Xcur[hp], ALU.add)
                        Xcur[hp] = Xn
                        if lev < 5:
                            Ncur[hp], NTcur[hp] = Nn[hp], NTn[hp]
                    yield
                for hp in range(2):
                    bmc = bm.v.bc(1, 4)
                    fw.tt(X.Bhbd.v[:, hp], c4(X.Bh.v)[:, hp].bc(2, 2), bmc, ALU.mult, eng="pool")
                    fw.tt(X.Khbd.v[:, hp], c4(X.Kh.v)[:, hp].bc(2, 2), bmc, ALU.mult)
                for hp in range(2):
                    pB, pK = banks[hp][0], banks[hp][1]
                    for c in range(4):
                        cs = slice(c * 128, (c + 1) * 128)
                        fw.mm(pB.v[:, cs], X.Bhbd.v[:, hp, c].r("p a s -> p (a s)"), identR.v)
                        fw.mm(pK.v[:, cs], X.Khbd.v[:, hp, c].r("p a s -> p (a s)"), identR.v)
                    fw.copy(Y.BhT.v[:, hp], pB.v.r("p (c t) -> p c t", c=4), "act")
                    fw.copy(Y.KhT.v[:, hp], pK.v.r("p (c t) -> p c t", c=4), "dve")
                yield

            def chunk_step(d, c, q, Y):
                pRU, pYS = P[6], P[7]
                pR, pU = pRU.v[:, 0:128], pRU.v[:, 128:256]
                pY, pS = pYS.v[:, 0:128], pYS.v[:, 128:256]
                for hp in range(2):
                    oc = slice(hp * 64, (hp + 1) * 64)
                    fw.mm(pR[:, oc], Y.ARbd.v[:, hp, c, 0].r("p a t -> p (a t)"), STb.v[:, d, hp, :], start=True, stop=False)
                    fw.mm(pR[:, oc], Y.MK.v[:, hp, c, 0], Y.VT.v[:, hp, c, :], start=False, stop=True)
                yield
                fw.copy(RHSsb.v, pR.r("p (a i) -> p a i", a=2), "act")
                yield
                for hp in range(2):
                    oc = slice(hp * 64, (hp + 1) * 64)
                    fw.mm(pU[:, oc], Y.TT.v[:, hp, c, :], RHSsb.v[:, hp, :])
                yield
                fw.copy(Usb.v, pU.r("p (a i) -> p a i", a=2), "dve")
                yield
                for hp in range(2):
                    oc = slice(hp * 64, (hp + 1) * 64)
                    fw.mm(pY[:, oc], Y.ARbd.v[:, hp, c, 1].r("p a t -> p (a t)"), STb.v[:, d, hp, :], start=True, stop=False)
                    fw.mm(pY[:, oc], Y.MB.v[:, hp, c, 1], Usb.v[:, hp, :], start=False, stop=False)
                    fw.mm(pY[:, oc], Y.MK.v[:, hp, c, 1], Y.VT.v[:, hp, c, :], start=False, stop=True)
                    fw.mm(pS[:, oc], Y.BhT.v[:, hp, c, :], Usb.v[:, hp, :], start=True, stop=False)
                    fw.mm(pS[:, oc], Y.KhT.v[:, hp, c, :], Y.VT.v[:, hp, c, :], start=False, stop=True)
                yield
                for hp in range(2):
                    oc = slice(hp * 64, (hp + 1) * 64)
                    fw.stt(ST.v[:, d, hp, :], ST.v[:, d, hp, :], Y.PC.v[:, hp, c:c + 1], pS[:, oc], ALU.mult, ALU.add)
                yield
                if RD != F32:
                    fw.copy(STb.v[:, d], ST.v[:, d], "act")
                fw.tt(Yacc.v[:, q], pY.r("p (a i) -> p a i", a=2), Yacc.v[:, q], ALU.add)
                yield

            work = [(0, seg) for seg in range(NSEG)] + [(1, seg) for seg in [0] + list(range(NSEG - 1, 0, -1))]

            def drain(g, k=None):
                n = 0
                for _ in g:
                    n += 1
                    if k is not None and n >= k:
                        return
            cur = prep(work[0][1], work[0][0], SX[0])
            drain(cur)
            for wi, (d, seg) in enumerate(work):
                Y = SX[wi % 2]
                nxt = prep(work[wi + 1][1], work[wi + 1][0], SX[(wi + 1) % 2]) if wi + 1 < len(work) else None
                cl = range(4) if d == 0 else range(3, -1, -1)
                for c in cl:
                    for _ in chunk_step(d, c, seg * 4 + c, Y):
                        if nxt is not None:
                            drain(nxt, 2)
                if nxt is not None:
                    drain(nxt)

        with fw.phase():
            Yf = Yacc.v.r("p q a i -> p (q a) i")
            sm = fw.sb("sm", [128, 72], F32)
            sq = fw.sb("sqy", [128, 72, 64], F32)
            s2 = fw.sb("s2", [128, 72], F32)
            var = fw.sb("var", [128, 72], F32)
            fw.reduce(sm.v, Yf)
            fw.ts(sm.v, sm.v, 1.0 / 64, None, ALU.mult)
            fw.tt(sq.v, Yf, sm.v.bc(2, 64), ALU.subtract)
            yn = fw.sb("yn", [128, 72, 64], F32)
            fw.tt(yn.v, sq.v, sq.v, ALU.mult, eng="pool")
            fw.reduce(s2.v, yn.v)
            fw.act(var.v, s2.v, AF.Ln, scale=1.0 / 64, bias=RK_EPS)
            fw.act(var.v, var.v, AF.Exp, scale=-0.5)
            fw.tt(yn.v, sq.v, var.v.bc(2, 64), ALU.mult)
            y4 = yn.v.r("p (q a) i -> p q a i", a=2)
            fw.tt(y4, y4, lnw.v.bc(1, 36), ALU.mult)
            fw.tt(y4, y4, lnb.v.bc(1, 36), ALU.add, eng="pool")
            ybd = [fw.sb(f"ybd{i}", [128, 8, 2, 64], F32) for i in range(2)]
            osts = [fw.sb(f"ork{i}", [128, 512], F32) for i in range(2)]
            ostb = [fw.sb(f"orkb{i}", [128, 512], BF16) for i in range(2)]
            it = 0
            nq = 36
            for hp in range(2):
                for q0 in range(0, nq, 8):
                    nqq = min(8, nq - q0)
                    ps = P[it % 4]
                    o1, o2, yb = osts[it % 2], ostb[it % 2], ybd[it % 2]
                    it += 1
                    fw.tt(yb.v[:, 0:nqq], y4[:, q0:q0 + nqq, hp, :].bc(2, 2), bm.v.bc(1, nqq), ALU.mult)
                    for qq in range(nqq):
                        fw.mm(ps.v[:, qq * 64:(qq + 1) * 64], yb.v[:, qq].r("p a i -> p (a i)"), eye.v)
                    n = nqq * 64
                    t0 = q0 * 64
                    fw.tt(o1.v[:, 0:n], ps.v[:, 0:n], bg.v[:, hp, 0, t0:t0 + n], ALU.add)
                    fw.tt(o2.v[:, 0:n], o1.v[:, 0:n], bg.v[:, hp, 1, t0:t0 + n], ALU.mult, eng="pool")
                    fw.dma("sp", S.MIXT[:, 4 + hp, t0:t0 + n], o2.v[:, 0:n])


NCORES = 8
_CACHE = {}


def kernel(**inputs):
    inp = {k: np.asarray(v) for k, v in inputs.items()}
    B = inp["x"].shape[0]
    nb = B // NCORES
    if "prog" not in _CACHE:
        _CACHE["prog"] = build_program(nb)
    nc, fw = _CACHE["prog"]
    in_maps = [_layout_inputs(inp, nb, c) for c in range(NCORES)]
    res = run_bass_kernel_spmd(nc, in_maps, core_ids=list(range(NCORES)))
    out = np.concatenate([np.asarray(r["out"]) for r in res.results], axis=0)
    return out.astype(np.float32)
```

```python
import contextlib
import numpy as np
import concourse.bass as bass
import concourse.mybir as mybir
from concourse.bass_utils import run_bass_kernel_spmd

F32 = mybir.dt.float32
BF16 = mybir.dt.bfloat16
AF = mybir.ActivationFunctionType
ALU = mybir.AluOpType
AX = mybir.AxisListType

D = 1024
TC = 256
TL = 2048
T = TC + TL
NLAY = 4
DFF = 2816
INC = 2848
C_NA, C_MLA, C_RK, C_DF = 0, 768, 1184, 2080


class V:
    def __init__(self, ap, buf):
        self.ap, self.buf = ap, buf

    def __getitem__(self, idx):
        return V(self.ap[idx], self.buf)

    def f(self, fn):
        return V(fn(self.ap), self.buf)

    def r(self, pat, **kw):
        return V(self.ap.rearrange(pat, **kw), self.buf)

    def bc(self, axis, n):
        a = self.ap.unsqueeze(axis)
        sh = list(a.shape)
        sh[axis] = n
        return V(a.broadcast_to(sh), self.buf)


class Buf:
    def __init__(self, name, t):
        self.name, self.t = name, t
        self.lw = None
        self.rd = {}

    def __getitem__(self, idx):
        return V(self.t[idx], self)

    @property
    def v(self):
        return V(self.t[:], self)


def _ap(x):
    return x.ap if isinstance(x, V) else x


class FW:
    def __init__(self, nc, n_dma_sems=20):
        self.nc = nc
        self.engs = {}
        for name, h in (("pe", nc.tensor), ("act", nc.scalar), ("dve", nc.vector), ("pool", nc.gpsimd),
                        ("sp", nc.sync)):
            self.engs[name] = dict(name=name, h=h, sem=nc.alloc_semaphore("s_" + name), cnt=0, waited={})
        self.dq = {}
        for q in ("sp", "act", "pool"):
            sems = [nc.alloc_semaphore(f"d_{q}_{i}") for i in range(n_dma_sems)]
            self.dq[q] = dict(sems=sems, vals=[0] * n_dma_sems, nxt=0)
        self.ninst = 0
        self.nwait = 0
        self.stack = None

    def sb(self, name, shape, dt):
        self.nsb = getattr(self, "nsb", 0) + 1
        name = f"{name}_{self.nsb}"
        if self.stack is not None:
            t = self.stack.enter_context(self.nc.sbuf_tensor(name, list(shape), dt))
        else:
            t = self.nc.alloc_sbuf_tensor(name, list(shape), dt)
        return Buf(name, t)

    def ps(self, name, shape, dt=F32):
        return Buf(name, self.nc.alloc_psum_tensor(name, list(shape), dt))

    def dram(self, name, shape, dt, kind="Internal"):
        return Buf(name, self.nc.dram_tensor(name, list(shape), dt, kind=kind))

    @contextlib.contextmanager
    def phase(self):
        old = self.stack
        with contextlib.ExitStack() as st:
            self.stack = st
            yield
            self.barrier()
        self.stack = old

    def _wait(self, e, key, sem, val):
        if e["waited"].get(key, 0) >= val:
            return
        e["h"].wait_ge(sem, val)
        e["waited"][key] = val
        self.nwait += 1

    def _deps(self, e, reads, writes, mykey):
        deps = {}

        def add(k, s, v):
            if k not in deps or deps[k][1] < v:
                deps[k] = (s, v)
        for b in reads:
            if b.lw is not None:
                add(*b.lw)
        for b in writes:
            if b.lw is not None:
                add(*b.lw)
            for k, (s, v) in b.rd.items():
                add(k, s, v)
        need = []
        for k, (s, v) in deps.items():
            if k == mykey and e["name"] == "pe":
                continue
            if e["waited"].get(k, 0) >= v:
                continue
            need.append((k, s, v))
        for (k, s, v) in need[:-1]:
            self._wait(e, k, s, v)
        return need[-1] if need else None

    def _fold(self, e, ins, last):
        if last is None:
            return
        k, s, v = last
        ins.wait_op(s, v, "sem-ge")
        e["waited"][k] = v

    def _mark(self, reads, writes, key, sem, val):
        for b in reads:
            b.rd[key] = (sem, val)
        for b in writes:
            b.lw = (key, sem, val)
            b.rd = {}

    def op(self, eng, fn, reads, writes):
        e = self.engs[eng]
        reads = [x.buf for x in reads if isinstance(x, V)]
        writes = [x.buf for x in writes if isinstance(x, V)]
        last = self._deps(e, reads, writes, eng)
        ins = fn(e["h"])
        self._fold(e, ins, last)
        e["cnt"] += 1
        ins.then_inc(e["sem"], 1)
        self._mark(reads, writes, eng, e["sem"], e["cnt"])
        self.ninst += 1
        return ins

    def dma(self, q, out, in_, **kw):
        e = self.engs[q]
        d = self.dq[q]
        i = d["nxt"]
        d["nxt"] = (i + 1) % len(d["sems"])
        sem = d["sems"][i]
        key = f"dma_{q}_{i}"
        if d["vals"][i] > 0:
            self._wait(e, key, sem, d["vals"][i])
        reads, writes = [in_.buf], [out.buf]
        last = self._deps(e, reads, writes, key)
        ins = e["h"].dma_start(out=out.ap, in_=in_.ap, **kw)
        self._fold(e, ins, last)
        d["vals"][i] += 16
        ins.then_inc(sem, 16)
        self._mark(reads, writes, key, sem, d["vals"][i])
        self.ninst += 1
        return ins

    def barrier(self):
        sp = self.engs["sp"]
        for q, d in self.dq.items():
            for i, sem in enumerate(d["sems"]):
                if d["vals"][i] > 0:
                    self._wait(sp, f"dma_{q}_{i}", sem, d["vals"][i])
        for n in ("pe", "act", "dve", "pool"):
            o = self.engs[n]
            if o["cnt"] > 0:
                self._wait(sp, n, o["sem"], o["cnt"])
        sp["cnt"] += 1
        sp["h"].sem_inc(sp["sem"], 1)
        for n in ("pe", "act", "dve", "pool"):
            self._wait(self.engs[n], "sp", sp["sem"], sp["cnt"])
            for m in ("pe", "act", "dve", "pool"):
                self.engs[n]["waited"][m] = max(self.engs[n]["waited"].get(m, 0), self.engs[m]["cnt"])
            for q, d in self.dq.items():
                for i in range(len(d["sems"])):
                    k = f"dma_{q}_{i}"
                    self.engs[n]["waited"][k] = max(self.engs[n]["waited"].get(k, 0), d["vals"][i])

    def mm(self, out, lhsT, rhs, start=True, stop=True):
        return self.op("pe", lambda h: h.matmul(out.ap, lhsT=lhsT.ap, rhs=rhs.ap, start=start, stop=stop),
                       [lhsT, rhs], [out])

    def act(self, out, in_, func, scale=None, bias=None, accum_out=None, eng="act"):
        kw = {}
        if scale is not None:
            kw["scale"] = _ap(scale)
        if bias is not None:
            kw["bias"] = _ap(bias)
        if accum_out is not None:
            kw["accum_out"] = _ap(accum_out)
        return self.op("act", lambda h: h.activation(out=out.ap, in_=in_.ap, func=func, **kw),
                       [in_, scale, bias], [out, accum_out])

    def tt(self, out, in0, in1, op, eng="dve"):
        return self.op(eng, lambda h: h.tensor_tensor(out=out.ap, in0=in0.ap, in1=in1.ap, op=op), [in0, in1], [out])

    def ts(self, out, in0, s1, s2, op0, op1=None, eng="dve"):
        kw = dict(op1=op1) if op1 is not None else {}
        return self.op(eng, lambda h: h.tensor_scalar(out=out.ap, in0=in0.ap, scalar1=_ap(s1), scalar2=_ap(s2),
                                                      op0=op0, **kw), [in0, s1, s2], [out])

    def stt(self, out, in0, scalar, in1, op0, op1):
        return self.op("dve", lambda h: h.scalar_tensor_tensor(out=out.ap, in0=in0.ap, scalar=_ap(scalar),
                                                               in1=in1.ap, op0=op0, op1=op1),
                       [in0, scalar, in1], [out])

    def copy(self, out, in_, eng="dve"):
        if eng == "act":
            return self.act(out, in_, AF.Copy)
        return self.op(eng, lambda h: h.tensor_copy(out=out.ap, in_=in_.ap), [in_], [out])

    def memset(self, out, val, eng="dve"):
        return self.op(eng, lambda h: h.memset(out.ap, val), [], [out])

    def scan(self, out, d0, d1, op0=ALU.mult, op1=ALU.add):
        return self.op("dve", lambda h: h.tensor_tensor_scan(out=out.ap, data0=d0.ap, data1=d1.ap, initial=0.0,
                                                             op0=op0, op1=op1), [d0, d1], [out])

    def recip(self, out, in_):
        return self.op("dve", lambda h: h.reciprocal(out=out.ap, in_=in_.ap), [in_], [out])

    def reduce(self, out, in_, op=ALU.add, axis=AX.X):
        return self.op("dve", lambda h: h.tensor_reduce(out=out.ap, in_=in_.ap, axis=axis, op=op), [in_], [out])


def _rope_tables():
    t = np.arange(TL)
    row = (t // 64).astype(np.float32)
    col = (t % 64).astype(np.float32)
    freqs = (np.float32(10000.0) ** (-np.arange(0, 16, 2, dtype=np.float32) / np.float32(16))).astype(np.float32)
    ar = row[:, None] * freqs[None, :]
    ac = col[:, None] * freqs[None, :]
    ang = np.concatenate([ar, ar, ac, ac], axis=-1).astype(np.float32)
    return np.cos(ang).astype(np.float32).T.copy(), np.sin(ang).astype(np.float32).T.copy()


def _rot32():
    R = np.zeros((32, 32), np.float32)
    for m in range(8):
        R[8 + m, m] = -1.0
        R[m, 8 + m] = 1.0
        R[24 + m, 16 + m] = -1.0
        R[16 + m, 24 + m] = 1.0
    return R


def _consts():
    cos, sin = _rope_tables()
    c = {}
    c["cos128"] = np.tile(cos, (4, 1))
    c["sin128"] = np.tile(sin, (4, 1))
    c["cos96"] = np.concatenate([np.ones((64, TL), np.float32), cos], 0)
    c["sin96"] = np.concatenate([np.zeros((64, TL), np.float32), sin], 0)
    R = _rot32()
    R128 = np.zeros((128, 128), np.float32)
    for g in range(4):
        R128[g * 32:(g + 1) * 32, g * 32:(g + 1) * 32] = R
    R96 = np.zeros((96, 96), np.float32)
    R96[64:, 64:] = R
    c["R128"] = R128
    c["R96"] = R96
    c["ident"] = np.eye(128, dtype=np.float32)
    bo = np.zeros((128, 128), np.float32)
    bo[:64, :64] = 1.0
    bo[64:, 64:] = 1.0
    c["blockones"] = bo
    es = np.zeros((65, 64), np.float32)
    es[64, :] = 1.0
    c["esel"] = es
    cpos = np.arange(64)
    cstart = np.clip(cpos - 8, 0, 48)
    m = (cpos[None, :] >= cstart[:, None]) & (cpos[None, :] < cstart[:, None] + 16)
    c["na_mask"] = np.ascontiguousarray(m.T.astype(np.float32))
    s = np.arange(64)[:, None]
    t = np.arange(64)[None, :]
    mk = np.zeros((128, 2, 2, 64), np.float32)
    for hh in range(2):
        mk[hh * 64:(hh + 1) * 64, 0, 0] = (s < t)
        mk[hh * 64:(hh + 1) * 64, 0, 1] = (s <= t)
        mk[hh * 64:(hh + 1) * 64, 1, 0] = (s > t)
        mk[hh * 64:(hh + 1) * 64, 1, 1] = (s >= t)
    c["rk_mask"] = mk
    gmm = np.zeros((128, 4), np.float32)
    for g in range(4):
        gmm[g * 32:(g + 1) * 32, g] = 1.0
    c["df_gm"] = gmm
    c["rk_eye"] = np.concatenate([np.eye(64, dtype=np.float32)] * 2, 0)
    rm = np.ones((128, 2, 512), np.float32)
    rm[:, 0, 0::64] = 0.0
    rm[:, 1, 63::64] = 0.0
    c["rk_rm"] = rm
    return c


CONST_SHAPES = dict(cos128=(128, TL), sin128=(128, TL), cos96=(96, TL), sin96=(96, TL), R128=(128, 128),
                    R96=(96, 96), ident=(128, 128), blockones=(128, 128), esel=(65, 64), na_mask=(64, 64),
                    rk_mask=(128, 2, 2, 64), rk_eye=(128, 64), rk_rm=(128, 2, 512), df_gm=(128, 4))


def _fm(a, k):
    sh = a.shape[:-1]
    return np.ascontiguousarray(np.swapaxes(a.reshape(sh + (k, 128)), -1, -2))


def _layout_inputs(inp, nb, core):
    b0 = core * nb
    m = {}
    m["x"] = np.ascontiguousarray(inp["x"][b0:b0 + nb])
    m["ctx"] = np.ascontiguousarray(inp["ctx"][b0:b0 + nb])
    cv = np.concatenate([inp["c"][b0:b0 + nb], inp["c_ctx"][None]], 0)
    m["cvec"] = np.ascontiguousarray(cv.reshape(nb + 1, 8, 128).transpose(2, 1, 0))
    m["ada_w"] = inp["ada_w"]
    m["ada_bT"] = _fm(inp["ada_b"], 48)
    m["g1T"] = _fm(inp["norm1_g"], 8)
    m["g2T"] = _fm(inp["norm2_g"], 8)
    m["gfT"] = _fm(inp["final_norm_g"], 8)
    m["w_in"] = inp["w_in"]
    m["w_out"] = inp["w_out"]
    m["w_up"] = inp["mlp_w_up"]
    m["w_down"] = inp["mlp_w_down"]
    m["mla_qn"] = _fm(inp["mla_q_norm"], 2)
    m["mla_kvn"] = _fm(inp["mla_kv_norm"], 1)
    m["w_uq"] = inp["mla_w_uq"]
    m["w_ukv"] = inp["mla_w_ukv"]
    m["rk_mu"] = np.ascontiguousarray(inp["rwkv_mu"].reshape(NLAY, 2, 7, 128).transpose(0, 3, 1, 2))
    m["rk_w0"] = np.ascontiguousarray(inp["rwkv_w0"].reshape(NLAY, 2, 2, 128).transpose(0, 3, 1, 2))
    m["rk_a0"] = np.ascontiguousarray(inp["rwkv_a0"].reshape(NLAY, 2, 2, 128).transpose(0, 3, 1, 2))
    m["rk_wup"] = inp["rwkv_w_up"]
    m["rk_aup"] = inp["rwkv_a_up"]
    m["rk_gup"] = inp["rwkv_g_up"]
    m["rk_kk"] = _fm(inp["rwkv_k_k"], 2)
    m["rk_ka"] = _fm(inp["rwkv_k_a"], 2)
    m["rk_rk"] = _fm(inp["rwkv_r_k"].reshape(NLAY, 256), 2)
    m["rk_lnw"] = inp["rwkv_ln_w"]
    m["rk_lnb"] = inp["rwkv_ln_b"]
    m["df_lam"] = inp["diff_lambda"].reshape(NLAY, 1, 128)
    m["df_sub"] = np.ascontiguousarray(inp["diff_subln"].reshape(NLAY, 64, 1))
    m["conv_w"] = np.ascontiguousarray(inp["mlp_conv_w"].reshape(NLAY, 3, 22, 128).transpose(0, 3, 1, 2))
    m["conv_b"] = _fm(inp["mlp_conv_b"], 22)
    rpb = inp["na_rpb"]
    kc = np.arange(64)[:, None]
    qc = np.arange(64)[None, :]
    cidx = np.clip(kc - qc + 15, 0, 30)
    g = rpb[:, :, ::-1, :][:, :, :, cidx]
    m["na_bias"] = np.ascontiguousarray(g.transpose(0, 3, 1, 2, 4))
    for k, v in _consts().items():
        m["c_" + k] = v
    return {k: np.ascontiguousarray(v) for k, v in m.items()}


class NS:
    pass


IN_SHAPES = None


def build_program(nb, nlay=NLAY, dbg=(), parts=("na", "mla", "rk", "df", "mlp"), upto="end"):
    nc = bass.Bass("TRN2", target_bir_lowering=False)
    fw = FW(nc)
    S = NS()
    S.nc, S.fw, S.nb, S.nlay, S.dbg, S.parts = nc, fw, nb, nlay, set(dbg), parts
    NV = nb + 1
    S.NV = NV

    def din(name, shape, dt=F32):
        return fw.dram(name, shape, dt, kind="ExternalInput")

    I = NS()
    S.I = I
    I.x = din("x", [nb, TL, D])
    I.ctx = din("ctx", [nb, TC, D])
    I.cvec = din("cvec", [128, 8, NV])
    I.ada_w = din("ada_w", [NLAY, D, 6 * D])
    I.ada_bT = din("ada_bT", [NLAY, 128, 48])
    I.g1T = din("g1T", [NLAY, 128, 8])
    I.g2T = din("g2T", [NLAY, 128, 8])
    I.gfT = din("gfT", [128, 8])
    I.w_in = din("w_in", [NLAY, D, INC])
    I.w_out = din("w_out", [NLAY, D, D])
    I.w_up = din("w_up", [NLAY, D, 2 * DFF])
    I.w_down = din("w_down", [NLAY, DFF, D])
    I.mla_qn = din("mla_qn", [NLAY, 128, 2])
    I.mla_kvn = din("mla_kvn", [NLAY, 128, 1])
    I.w_uq = din("w_uq", [NLAY, 256, 384])
    I.w_ukv = din("w_ukv", [NLAY, 128, 512])
    I.rk_mu = din("rk_mu", [NLAY, 128, 2, 7])
    I.rk_w0 = din("rk_w0", [NLAY, 128, 2, 2])
    I.rk_a0 = din("rk_a0", [NLAY, 128, 2, 2])
    I.rk_wup = din("rk_wup", [NLAY, 2, 32, 256])
    I.rk_aup = din("rk_aup", [NLAY, 2, 32, 256])
    I.rk_gup = din("rk_gup", [NLAY, 64, 256])
    I.rk_kk = din("rk_kk", [NLAY, 128, 2])
    I.rk_ka = din("rk_ka", [NLAY, 128, 2])
    I.rk_rk = din("rk_rk", [NLAY, 128, 2])
    I.rk_lnw = din("rk_lnw", [NLAY, 256])
    I.rk_lnb = din("rk_lnb", [NLAY, 256])
    I.df_lam = din("df_lam", [NLAY, 1, 128])
    I.df_sub = din("df_sub", [NLAY, 64, 1])
    I.conv_w = din("conv_w", [NLAY, 128, 3, 22])
    I.conv_b = din("conv_b", [NLAY, 128, 22])
    I.na_bias = din("na_bias", [NLAY, 64, 4, 15, 64])
    I.c = {}
    for k, sh in CONST_SHAPES.items():
        I.c[k] = din("c_" + k, list(sh))
    S.out = fw.dram("out", [nb, TL, D], F32, kind="ExternalOutput")

    def scratch(name, shape, dt):
        return fw.dram(name, shape, dt, kind="ExternalOutput" if name in S.dbg else "Internal")
    S.scratch = scratch
    S.w_in_bf = scratch("w_in_bf", [NLAY, D, INC], BF16)
    S.w_out_bf = scratch("w_out_bf", [NLAY, D, D], BF16)
    S.w_up_bf = scratch("w_up_bf", [NLAY, 22, 128, 2, 8, 128], BF16)
    S.w_down_bf = scratch("w_down_bf", [NLAY, 8, 128, 22, 128], BF16)
    S.w_uq_bf = scratch("w_uq_bf", [NLAY, 256, 384], BF16)
    S.w_ukv_bf = scratch("w_ukv_bf", [NLAY, 128, 512], BF16)
    S.XT = scratch("XT", [nb, 128, 8, T], F32)
    S.QK_na = scratch("QK_na", [128, 4, T], BF16)
    S.V_na = scratch("V_na", [18, 128, 4, 128], BF16)
    S.QT_mla = scratch("QT_mla", [96, 4, T], BF16)
    S.KT_mla = scratch("KT_mla", [96, 4, T], BF16)
    S.V_mla = scratch("V_mla", [18, 128, 4, 128], BF16)
    S.QK_df = scratch("QK_df", [128, 4, T], BF16)
    S.V_df = scratch("V_df", [18, 128, 4, 128], BF16)
    S.ZT_rk = scratch("ZT_rk", [128, 7, T], F32)
    S.MIXT = scratch("MIXT", [128, 8, T], BF16)

    S.P = [fw.ps(f"P{i}", [128, 512]) for i in range(8)]

    G = NS()
    S.G = G
    G.ident = fw.sb("ident", [128, 128], F32)
    G.ones_bf = fw.sb("ones_bf", [128, 128], BF16)
    G.blockones = fw.sb("blockones", [128, 128], F32)
    G.esel = fw.sb("esel", [65, 64], F32)
    G.mods = fw.sb("mods", [128, nlay, 48, NV], F32)
    G.A = fw.sb("Amod", [128, nlay, 2, 8, NV], F32)
    fw.dma("sp", G.ident.v, I.c["ident"].v)
    fw.dma("sp", G.blockones.v, I.c["blockones"].v)
    fw.dma("sp", G.esel.v, I.c["esel"].v)
    fw.memset(G.ones_bf.v, 1.0)

    phase_prep(S)
    phase_mod(S)
    for b in range(nb):
        for l in range(nlay):
            need_ctx = l < NLAY - 1
            phase_A(S, b, l)
            if upto == "A":
                continue
            if "mla" in parts:
                phase_mla(S, b, l, need_ctx)
            if "df" in parts:
                phase_df(S, b, l, need_ctx)
            if "na" in parts:
                phase_na(S, b, l, need_ctx)
            if "rk" in parts:
                phase_rk(S, b, l, need_ctx)
            if upto == "att":
                continue
            phase_O(S, b, l, need_ctx)
            if upto == "O":
                continue
            if "mlp" in parts:
                phase_mlp(S, b, l, need_ctx)
    if upto == "end":
        phase_final(S)
    fw.barrier()
    return nc, fw


def phase_prep(S):
    fw, I, nb = S.fw, S.I, S.nb
    for l in range(S.nlay):
        for j in range(22):
            for ab in range(2):
                fw.dma("pool", S.w_up_bf[l, j, :, ab, :, :],
                       I.w_up[l, :, ab * DFF + j * 128:ab * DFF + (j + 1) * 128].r("(k p) n -> p k n", p=128))
        for m in range(8):
            fw.dma("pool", S.w_down_bf[l, m], I.w_down[l, :, m * 128:(m + 1) * 128].r("(j p) n -> p j n", p=128))
        for src, dst, rows, cols in ((I.w_in, S.w_in_bf, D, INC), (I.w_out, S.w_out_bf, D, D),
                                     (I.w_uq, S.w_uq_bf, 256, 384), (I.w_ukv, S.w_ukv_bf, 128, 512)):
            r0 = 0
            while r0 < rows:
                nr = min(512, rows - r0)
                fw.dma("pool", dst[l, r0:r0 + nr, :], src[l, r0:r0 + nr, :], max_dma_last_dim=4096)
                r0 += nr
    with fw.phase():
        xin = [fw.sb(f"xin{i}", [128, D], F32) for i in range(2)]
        xo = [fw.sb(f"xo{i}", [128, 8, 128], F32) for i in range(2)]
        it = 0
        for b in range(nb):
            for tt_ in range(T // 128):
                xi, xoo = xin[it % 2], xo[it % 2]
                if tt_ < 2:
                    src = I.ctx[b, tt_ * 128:(tt_ + 1) * 128, :]
                else:
                    src = I.x[b, (tt_ - 2) * 128:(tt_ - 1) * 128, :]
                fw.dma("sp" if it % 2 == 0 else "act", xi.v, src)
                for half in range(2):
                    ps = S.P[(it * 2 + half) % 4]
                    for kk in range(4):
                        k = half * 4 + kk
                        fw.op("pe", lambda h: h.transpose(ps.v[:, kk * 128:(kk + 1) * 128].ap,
                                                           xi.v[:, k * 128:(k + 1) * 128].ap, S.G.ident.v.ap),
                              [xi.v, S.G.ident.v], [ps.v])
                    dst = xoo.v[:, half * 4:(half + 1) * 4, :]
                    srcp = ps.v.r("p (k t) -> p k t", k=4)
                    if half == 0:
                        fw.copy(dst, srcp, "dve")
                    else:
                        fw.copy(dst, srcp, "act")
                fw.dma("sp", S.XT[b, :, :, tt_ * 128:(tt_ + 1) * 128], xoo.v)
                it += 1


def phase_mod(S):
    fw, I, G, NV = S.fw, S.I, S.G, S.NV
    with fw.phase():
        cv = fw.sb("cv", [128, 8, NV], F32)
        sv = fw.sb("sv", [128, 8, NV], F32)
        fw.dma("sp", cv.v, I.cvec.v)
        fw.act(sv.v, cv.v, AF.Silu)
        wb = [fw.sb(f"adaw{i}", [128, 8, 768], F32) for i in range(2)]
        for l in range(S.nlay):
            abT = fw.sb(f"abT{l}", [128, 48], F32)
            fw.dma("act", abT.v, I.ada_bT[l])
            gT = fw.sb(f"gT{l}", [128, 2, 8], F32)
            fw.dma("act", gT.v[:, 0, :], I.g1T[l])
            fw.dma("act", gT.v[:, 1, :], I.g2T[l])
            ps = S.P[4 + (l % 2)]
            for blk in range(8):
                w = wb[(l * 8 + blk) % 2]
                fw.dma("sp" if blk % 2 == 0 else "act", w.v,
                       I.ada_w[l, :, blk * 768:(blk + 1) * 768].r("(k p) n -> p k n", p=128))
                for cc in range(6):
                    ch = blk * 6 + cc
                    for k in range(8):
                        fw.mm(ps.v[:, ch * NV:(ch + 1) * NV], w.v[:, k, cc * 128:(cc + 1) * 128], sv.v[:, k, :],
                              start=(k == 0), stop=(k == 7))
            fw.tt(G.mods.v[:, l], ps.v[:, 0:48 * NV].r("p (c v) -> p c v", v=NV), abT.v.bc(2, NV), ALU.add)
            for n, c0 in ((0, 8), (1, 32)):
                tmp = fw.sb(f"modtmp{l}{n}", [128, 8, NV], F32)
                fw.ts(tmp.v, G.mods.v[:, l, c0:c0 + 8, :], 1.0, None, ALU.add)
                fw.tt(G.A.v[:, l, n], tmp.v, gT.v[:, n, :].bc(2, NV), ALU.mult)


TOK_TILES = [(0, 256), (256, 512), (768, 512), (1280, 512), (1792, 512)]
EPS = 1e-6


def nps(S):
    S.pi = getattr(S, "pi", 0) + 1
    return S.P[S.pi % 8]


def rstd_from_sum(S, out, ssum, n, eps, tmp):
    fw = S.fw
    fw.act(tmp, ssum, AF.Ln, scale=1.0 / n, bias=float(eps))
    fw.act(out, tmp, AF.Exp, scale=-0.5)


def phase_A(S, b, l):
    fw, I, G = S.fw, S.I, S.G
    with fw.phase():
        win = fw.sb("win", [128, 8, INC], BF16)
        for k in range(8):
            fw.dma("sp" if k % 2 == 0 else "act", win.v[:, k, :], S.w_in_bf[l, k * 128:(k + 1) * 128, :])
        wuq = fw.sb("wuq", [128, 2, 384], BF16)
        fw.dma("sp", wuq.v, S.w_uq_bf[l].r("(k p) n -> p k n", p=128))
        wukv = fw.sb("wukv", [128, 512], BF16)
        fw.dma("act", wukv.v, S.w_ukv_bf[l])
        qn = fw.sb("qn", [128, 2], F32)
        kvn = fw.sb("kvn", [128, 1], F32)
        fw.dma("sp", qn.v, I.mla_qn[l])
        fw.dma("sp", kvn.v, I.mla_kvn[l])
        tabs = {}
        for nm, rows in (("cos128", 128), ("sin128", 128), ("cos96", 96), ("sin96", 96)):
            tabs[nm] = fw.sb(nm, [rows, TL], F32)
            fw.dma("act", tabs[nm].v, I.c[nm].v)
        R128 = fw.sb("R128", [128, 128], F32)
        R96 = fw.sb("R96", [96, 96], F32)
        fw.dma("sp", R128.v, I.c["R128"].v)
        fw.dma("sp", R96.v, I.c["R96"].v)

        xt = fw.sb("xt", [128, 8, 512], F32)
        sq = fw.sb("sq", [128, 8, 512], BF16)
        xn = fw.sb("xn", [128, 8, 512], F32)
        hT = fw.sb("hT", [128, 8, 512], BF16)
        rstd = fw.sb("rstd", [128, 512], F32)
        lnt = fw.sb("lnt", [128, 512], F32)
        st_na = fw.sb("st_na", [128, 4, 512], BF16)
        st_df = fw.sb("st_df", [128, 4, 512], BF16)
        st_rk = fw.sb("st_rk", [128, 7, 512], F32)
        vts = [fw.sb(f"vt{i}", [128, 4, 128], BF16) for i in range(6)]
        for vt in vts:
            fw.memset(vt.v, 1.0, "pool")
        cq = fw.sb("cq", [128, 3, 512], F32)
        sqm = fw.sb("sqm", [128, 3, 512], BF16)
        rq = fw.sb("rq", [128, 2, 512], F32)
        cqn = fw.sb("cqn", [128, 3, 512], BF16)
        qst = fw.sb("qst", [96, 4, 512], BF16)
        kst = fw.sb("kst", [96, 4, 512], BF16)
        xsb = [fw.sb(f"xsb{i}", [128, 512], F32) for i in range(2)]
        t1s = [fw.sb(f"t1s{i}", [128, 512], F32) for i in range(2)]
        t2s = [fw.sb(f"t2s{i}", [128, 512], F32) for i in range(2)]
        krt = fw.sb("krt", [96, 512], BF16)
        S.rc = 0
        S.vc = 0

        def rope(psv, M, N, tl0, Rm, cosn, sinn, outv):
            i = S.rc % 2
            S.rc += 1
            x, t1, t2 = xsb[i].v[0:M, 0:N], t1s[i].v[0:M, 0:N], t2s[i].v[0:M, 0:N]
            fw.copy(x, psv, "act")
            rp = nps(S).v[0:M, 0:N]
            fw.mm(rp, Rm.v[0:M, 0:M], x)
            fw.tt(t1, x, tabs[cosn].v[0:M, tl0:tl0 + N], ALU.mult)
            fw.tt(t2, rp, tabs[sinn].v[0:M, tl0:tl0 + N], ALU.mult)
            fw.tt(outv, t1, t2, ALU.add, eng="pool")

        for (t0, N) in TOK_TILES:
            lat = t0 >= TC
            vec = b if lat else S.nb
            tl0 = t0 - TC
            fw.dma("sp", xt.v[:, :, 0:N], S.XT[b, :, :, t0:t0 + N])
            fw.act(sq.v[:, :, 0:N], xt.v[:, :, 0:N], AF.Square)
            ps = nps(S)
            for k in range(8):
                fw.mm(ps.v[:, 0:N], G.ones_bf.v, sq.v[:, k, 0:N], start=(k == 0), stop=(k == 7))
            rstd_from_sum(S, rstd.v[:, 0:N], ps.v[:, 0:N], 1024.0, EPS, lnt.v[:, 0:N])
            fw.tt(xn.v[:, :, 0:N], xt.v[:, :, 0:N], rstd.v[:, 0:N].bc(1, 8), ALU.mult)
            for k in range(8):
                fw.act(hT.v[:, k, 0:N], xn.v[:, k, 0:N], AF.Identity, scale=G.A.v[:, l, 0, k, vec:vec + 1],
                       bias=G.mods.v[:, l, 0 + k, vec:vec + 1])

            def zchunk(c0, M):
                p = nps(S)
                for k in range(8):
                    fw.mm(p.v[0:M, 0:N], win.v[:, k, c0:c0 + M], hT.v[:, k, 0:N], start=(k == 0), stop=(k == 7))
                return p.v[0:M, 0:N]

            def vproj(c0, dst):
                for sub in range(N // 128):
                    p = nps(S)
                    for k in range(8):
                        fw.mm(p.v[:, 0:256], hT.v[:, k, sub * 128:(sub + 1) * 128], win.v[:, k, c0:c0 + 256],
                              start=(k == 0), stop=(k == 7))
                    vt = vts[S.vc % 6]
                    S.vc += 1
                    fw.copy(vt.v[:, :, 0:64], p.v[:, 0:256].r("p (h c) -> p h c", h=4), "dve")
                    fw.dma("act", dst[(t0 + sub * 128) // 128], vt.v)

            for c in range(4):
                p = zchunk(C_NA + c * 128, 128)
                fw.copy(st_na.v[:, c, 0:N], p, "act" if c % 2 else "dve")
            fw.dma("sp", S.QK_na[:, :, t0:t0 + N], st_na.v[:, :, 0:N])
            vproj(C_NA + 512, S.V_na)
            for c in range(4):
                p = zchunk(C_DF + c * 128, 128)
                if lat:
                    rope(p, 128, N, tl0, R128, "cos128", "sin128", st_df.v[:, c, 0:N])
                else:
                    fw.copy(st_df.v[:, c, 0:N], p, "act" if c % 2 else "dve")
            fw.dma("sp", S.QK_df[:, :, t0:t0 + N], st_df.v[:, :, 0:N])
            vproj(C_DF + 512, S.V_df)
            for c in range(7):
                p = zchunk(C_RK + c * 128, 128)
                fw.copy(st_rk.v[:, c, 0:N], p, "act" if c % 2 else "dve")
            fw.dma("sp", S.ZT_rk[:, :, t0:t0 + N], st_rk.v[:, :, 0:N])
            for c in range(3):
                p = zchunk(C_MLA + c * 128, 128)
                fw.copy(cq.v[:, c, 0:N], p, "act" if c % 2 else "dve")
            fw.act(sqm.v[:, :, 0:N], cq.v[:, :, 0:N], AF.Square)
            pq, pk = nps(S), nps(S)
            fw.mm(pq.v[:, 0:N], G.ones_bf.v, sqm.v[:, 0, 0:N], start=True, stop=False)
            fw.mm(pq.v[:, 0:N], G.ones_bf.v, sqm.v[:, 1, 0:N], start=False, stop=True)
            fw.mm(pk.v[:, 0:N], G.ones_bf.v, sqm.v[:, 2, 0:N])
            rstd_from_sum(S, rq.v[:, 0, 0:N], pq.v[:, 0:N], 256.0, EPS, lnt.v[:, 0:N])
            rstd_from_sum(S, rq.v[:, 1, 0:N], pk.v[:, 0:N], 128.0, EPS, lnt.v[:, 0:N])
            for j in range(3):
                sc = qn.v[:, j:j + 1] if j < 2 else kvn.v[:, 0:1]
                rr = rq.v[:, 0, 0:N] if j < 2 else rq.v[:, 1, 0:N]
                fw.stt(cqn.v[:, j, 0:N], cq.v[:, j, 0:N], sc, rr, ALU.mult, ALU.mult)
            for h in range(4):
                p = nps(S).v[0:96, 0:N]
                for j in range(2):
                    fw.mm(p, wuq.v[:, j, h * 96:(h + 1) * 96], cqn.v[:, j, 0:N], start=(j == 0), stop=(j == 1))
                if lat:
                    rope(p, 96, N, tl0, R96, "cos96", "sin96", qst.v[:, h, 0:N])
                else:
                    fw.copy(qst.v[:, h, 0:N], p, "act")
                p2 = nps(S).v[0:64, 0:N]
                fw.mm(p2, wukv.v[:, h * 128:h * 128 + 64], cqn.v[:, 2, 0:N])
                fw.copy(kst.v[0:64, h, 0:N], p2, "dve")
            p = zchunk(C_MLA + 320, 96)
            if lat:
                rope(p, 96, N, tl0, R96, "cos96", "sin96", krt.v[:, 0:N])
            else:
                fw.copy(krt.v[:, 0:N], p, "act")
            for h in range(4):
                fw.copy(kst.v[64:96, h, 0:N], krt.v[64:96, 0:N], "pool")
            fw.dma("sp", S.QT_mla[:, :, t0:t0 + N], qst.v[:, :, 0:N])
            fw.dma("sp", S.KT_mla[:, :, t0:t0 + N], kst.v[:, :, 0:N])
            for sub in range(N // 128):
                pv = nps(S)
                fw.mm(pv.v[:, 0:256].r("p (h c) -> p h c", h=4), cqn.v[:, 2, sub * 128:(sub + 1) * 128],
                      wukv.v.r("p (h c) -> p h c", h=4)[:, :, 64:128])
                vt = vts[S.vc % 6]
                S.vc += 1
                fw.copy(vt.v[:, :, 0:64], pv.v[:, 0:256].r("p (h c) -> p h c", h=4), "dve")
                fw.dma("act", S.V_mla[(t0 + sub * 128) // 128], vt.v)


class Skew:
    def __init__(self, L=2):
        self.L, self.q = L, []

    def push(self, fn):
        self.q.append(fn)
        while len(self.q) > self.L:
            self.q.pop(0)()

    def drain(self):
        while self.q:
            self.q.pop(0)()


def _qtiles(need_ctx):
    qt = [(TC + i * 512, 512, 18) for i in range(4)]
    if need_ctx:
        qt.append((0, 256, 2))
    return qt


def _normalize(S, fw, O, NQ, osb, lnr, rec, outv, bcp, eng="pool"):
    G = S.G
    fw.copy(osb.v[0:65, 0:NQ], O.v[0:65, 0:NQ], "dve")
    fw.mm(bcp.v[0:64, 0:NQ], G.esel.v[0:65, 0:64], osb.v[0:65, 0:NQ])
    fw.act(lnr.v[0:64, 0:NQ], bcp.v[0:64, 0:NQ], AF.Ln)
    fw.act(rec.v[0:64, 0:NQ], lnr.v[0:64, 0:NQ], AF.Exp, scale=-1.0)
    fw.tt(outv, osb.v[0:64, 0:NQ], rec.v[0:64, 0:NQ], ALU.mult, eng=eng)


def phase_mla(S, b, l, need_ctx):
    fw, I, G, P = S.fw, S.I, S.G, S.P
    scale = 96.0 ** -0.5
    with fw.phase():
        KT = fw.sb("KT", [128, 4, T], BF16)
        QT = fw.sb("QT", [128, 4, T], BF16)
        Vv = fw.sb("Vv", [128, 18, 4, 128], BF16)
        fw.memset(KT.v, 0.0)
        fw.memset(QT.v, 0.0, "pool")
        fw.dma("sp", KT.v[0:96], S.KT_mla.v)
        fw.dma("act", QT.v[0:96], S.QT_mla.v)
        fw.dma("sp", Vv.v, S.V_mla.v.r("j p h c -> p j h c"))
        pts = [fw.sb(f"pt{i}", [128, 512], BF16) for i in range(4)]
        osb = fw.sb("osb", [65, 512], F32)
        lnr = fw.sb("lnr", [64, 512], F32)
        rec = fw.sb("rec", [64, 512], F32)
        osts = [fw.sb(f"ost{i}", [64, 4, 512], BF16) for i in range(2)]
        cnt = 0
        pend = []
        sk = Skew(3)

        def flush():
            while pend:
                pend.pop(0)()
        for qi, (q0, NQ, nk) in enumerate(_qtiles(need_ctx)):
            ost = osts[qi % 2]
            for h in range(4):
                O = P[4 + (h % 2)]
                for j in range(nk):
                    sp = P[cnt % 4]
                    pt = pts[cnt % 4]
                    cnt += 1
                    fw.mm(sp.v[:, 0:NQ], KT.v[:, h, j * 128:(j + 1) * 128], QT.v[:, h, q0:q0 + NQ])
                    fw.act(pt.v[:, 0:NQ], sp.v[:, 0:NQ], AF.Exp, scale=scale)
                    sk.push(lambda O=O, j=j, h=h, pt=pt, NQ=NQ, nk=nk: fw.mm(
                        O.v[:, 0:NQ], Vv.v[:, j, h, :], pt.v[:, 0:NQ], start=(j == 0), stop=(j == nk - 1)))
                    if j == min(5, nk - 1):
                        flush()

                def norm(h=h, NQ=NQ, ost=ost, q0=q0):
                    bcp = P[6 + (h % 2)]
                    fw.mm(bcp.v[0:64, 0:NQ], G.esel.v[0:65, 0:64], osb.v[0:65, 0:NQ])
                    fw.act(lnr.v[0:64, 0:NQ], bcp.v[0:64, 0:NQ], AF.Ln)
                    fw.act(rec.v[0:64, 0:NQ], lnr.v[0:64, 0:NQ], AF.Exp, scale=-1.0)
                    fw.tt(ost.v[:, h, 0:NQ], osb.v[0:64, 0:NQ], rec.v[0:64, 0:NQ], ALU.mult, eng="pool")
                    if h == 3:
                        for hh in range(4):
                            hb = (hh % 2) * 64
                            fw.dma("sp" if hh % 2 else "act", S.MIXT[hb:hb + 64, 2 + hh // 2, q0:q0 + NQ], ost.v[:, hh, 0:NQ])

                def headend(O=O, NQ=NQ, norm=norm):
                    fw.copy(osb.v[0:65, 0:NQ], O.v[0:65, 0:NQ], "dve")
                    pend.append(norm)
                sk.push(headend)
        sk.drain()
        flush()


def phase_df(S, b, l, need_ctx):
    fw, I, G, P = S.fw, S.I, S.G, S.P
    import math
    scale = 32.0 ** -0.5
    lam_init = 0.8 - 0.6 * math.exp(-0.3 * l)
    with fw.phase():
        Kd = fw.sb("Kd", [128, 2, T], BF16)
        Qd = fw.sb("Qd", [128, 2, T], BF16)
        fw.dma("sp", Qd.v, S.QK_df[:, 0:2, :])
        fw.dma("act", Kd.v, S.QK_df[:, 2:4, :])
        gm = fw.sb("gm", [128, 4], F32)
        fw.dma("sp", gm.v, I.c["df_gm"].v)
        Qm = fw.sb("Qm", [128, 2, 4, T], BF16)
        for c in range(2):
            for g in range(4):
                if g % 2:
                    fw.ts(Qm.v[:, c, g, :], Qd.v[:, c, :], gm.v[:, g:g + 1], None, ALU.mult)
                else:
                    fw.act(Qm.v[:, c, g, :], Qd.v[:, c, :], AF.Identity, scale=gm.v[:, g:g + 1])
        Vv = fw.sb("Vv", [128, 18, 4, 128], BF16)
        fw.dma("sp", Vv.v, S.V_df.v.r("j p h c -> p j h c"))
        dl = fw.sb("dl", [64, 128], F32)
        fw.dma("act", dl.v, I.df_lam[l].f(lambda a: a.broadcast_to([64, 128])))
        pr = fw.sb("pr", [64, 2, 32], F32)
        fw.tt(pr.v, dl.v.r("p (a b d) -> p a b d", a=2, b=2)[:, :, 0, :], dl.v.r("p (a b d) -> p a b d", a=2, b=2)[:, :, 1, :],
              ALU.mult)
        ss = fw.sb("ss", [64, 2], F32)
        fw.reduce(ss.v, pr.v)
        ee = fw.sb("ee", [64, 2], F32)
        fw.act(ee.v, ss.v, AF.Exp)
        nlam = fw.sb("nlam", [64, 1], F32)
        fw.tt(nlam.v, ee.v[:, 1:2], ee.v[:, 0:1], ALU.subtract)
        fw.ts(nlam.v, nlam.v, -lam_init, None, ALU.add)
        sub = fw.sb("sub", [64, 1], F32)
        fw.dma("act", sub.v, I.df_sub[l])
        fw.ts(sub.v, sub.v, 1.0 - lam_init, None, ALU.mult)
        pts = [fw.sb(f"pt{i}", [128, 512], BF16) for i in range(4)]
        osb = [fw.sb(f"osb{i}", [65, 512], F32) for i in range(2)]
        lnr = fw.sb("lnr", [64, 512], F32)
        rec = fw.sb("rec", [64, 512], F32)
        o01 = [fw.sb(f"o01{i}", [64, 512], F32) for i in range(2)]
        oo = fw.sb("oo", [64, 512], F32)
        sqo = fw.sb("sqo", [64, 512], F32)
        osts = [fw.sb(f"ost{i}", [64, 4, 512], BF16) for i in range(2)]
        cnt = 0
        pend = []
        sk = Skew(3)

        def flush():
            while pend:
                pend.pop(0)()
        for qi, (q0, NQ, nk) in enumerate(_qtiles(need_ctx)):
            ost = osts[qi % 2]
            for h in range(4):
                c = h // 2
                for j in range(nk):
                    for n in range(2):
                        g = (h % 2) * 2 + n
                        sp = P[cnt % 4]
                        pt = pts[cnt % 4]
                        cnt += 1
                        fw.mm(sp.v[:, 0:NQ], Kd.v[:, c, j * 128:(j + 1) * 128], Qm.v[:, c, g, q0:q0 + NQ])
                        fw.act(pt.v[:, 0:NQ], sp.v[:, 0:NQ], AF.Exp, scale=scale)
                        sk.push(lambda n=n, j=j, h=h, pt=pt, NQ=NQ, nk=nk: fw.mm(
                            P[4 + n].v[:, 0:NQ], Vv.v[:, j, h, :], pt.v[:, 0:NQ], start=(j == 0), stop=(j == nk - 1)))
                    if j == min(3, nk - 1):
                        flush()

                def norm(h=h, NQ=NQ, ost=ost, q0=q0):
                    for n in range(2):
                        bcp = P[6 + n]
                        fw.mm(bcp.v[0:64, 0:NQ], G.esel.v[0:65, 0:64], osb[n].v[0:65, 0:NQ])
                        fw.act(lnr.v[0:64, 0:NQ], bcp.v[0:64, 0:NQ], AF.Ln)
                        fw.act(rec.v[0:64, 0:NQ], lnr.v[0:64, 0:NQ], AF.Exp, scale=-1.0)
                        fw.tt(o01[n].v[:, 0:NQ], osb[n].v[0:64, 0:NQ], rec.v[0:64, 0:NQ], ALU.mult, eng="pool")
                    fw.stt(oo.v[:, 0:NQ], o01[1].v[:, 0:NQ], nlam.v[:, 0:1], o01[0].v[:, 0:NQ], ALU.mult, ALU.add)
                    fw.tt(sqo.v[:, 0:NQ], oo.v[:, 0:NQ], oo.v[:, 0:NQ], ALU.mult, eng="pool")
                    fw.mm(P[6].v[0:64, 0:NQ], G.blockones.v[0:64, 0:64], sqo.v[:, 0:NQ])
                    rstd_from_sum(S, rec.v[:, 0:NQ], P[6].v[0:64, 0:NQ], 64.0, 1e-5, lnr.v[:, 0:NQ])
                    fw.stt(ost.v[:, h, 0:NQ], oo.v[:, 0:NQ], sub.v[:, 0:1], rec.v[:, 0:NQ], ALU.mult, ALU.mult)
                    if h == 3:
                        for hh in range(4):
                            hb = (hh % 2) * 64
                            fw.dma("sp" if hh % 2 else "act", S.MIXT[hb:hb + 64, 6 + hh // 2, q0:q0 + NQ], ost.v[:, hh, 0:NQ])

                def headend(NQ=NQ, norm=norm):
                    for n in range(2):
                        fw.copy(osb[n].v[0:65, 0:NQ], P[4 + n].v[0:65, 0:NQ], "dve")
                    pend.append(norm)
                sk.push(headend)
        sk.drain()
        flush()


def _na_rs(r):
    return min(max(r - 4, 0), 24)


def phase_na(S, b, l, need_ctx):
    fw, I, G, P = S.fw, S.I, S.G, S.P
    scale = 0.125
    with fw.phase():
        QK = fw.sb("QKn", [128, 4, T], BF16)
        fw.dma("sp", QK.v, S.QK_na.v)
        Vv = fw.sb("Vv", [128, 18, 4, 128], BF16)
        fw.dma("act", Vv.v, S.V_na.v.r("j p h c -> p j h c"))
        E = fw.sb("E", [128, 4, 15, 64], F32)
        msk = fw.sb("msk", [128, 64], F32)
        for hh in range(2):
            fw.dma("sp", E.v[hh * 64:(hh + 1) * 64], I.na_bias[l])
            fw.dma("act", msk.v[hh * 64:(hh + 1) * 64], I.c["na_mask"].v)
        Ef = E.v.r("p h d q -> p (h d) q")
        fw.act(Ef, Ef, AF.Exp)
        fw.tt(Ef, Ef, msk.v.bc(1, 60), ALU.mult)
        pts = [fw.sb(f"pt{i}", [128, 512], BF16) for i in range(4)]
        ptf = [fw.sb(f"ptf{i}", [128, 512], F32) for i in range(2)]
        osb = fw.sb("osb", [65, 512], F32)
        osm = fw.sb("osm", [65, 512], F32)
        lnr = fw.sb("lnr", [64, 512], F32)
        rec = fw.sb("rec", [64, 512], F32)
        osts = [fw.sb(f"ost{i}", [64, 4, 512], BF16) for i in range(2)]
        cnt = 0
        pend = []
        sk = Skew(3)

        def flush():
            while pend:
                pend.pop(0)()
        qtiles = [(i * 8, TC + i * 512, 512) for i in range(4)]
        if need_ctx:
            qtiles.append((None, 0, 256))
        for qi, (r0, q0, NQ) in enumerate(qtiles):
            ost = osts[qi % 2]
            work = []
            for i in range(4):
                work.append(((i % 2) * 64, i // 2, i * 64, 0, NQ, None))
            if r0 is not None:
                for kr in range(_na_rs(r0), _na_rs(r0 + 7) + 8):
                    rr = [r for r in range(r0, r0 + 8) if _na_rs(r) <= kr < _na_rs(r) + 8]
                    if not rr:
                        continue
                    ra, rb = rr[0], rr[-1]
                    work.append(((kr % 2) * 64, 2 + kr // 2, TC + kr * 64, (ra - r0) * 64, (rb - r0 + 1) * 64,
                                 (ra - kr + 7, rb - kr + 8)))
            last = {}
            for wi, w in enumerate(work):
                last[w[0]] = wi
            for h in range(4):
                hb = (h % 2) * 64
                cq, ck = h // 2, 2 + h // 2
                started = {0: False, 64: False}
                for wi, (kb, vtile, ktok, ca, cb, eidx) in enumerate(work):
                    sp = P[cnt % 4]
                    pt = pts[cnt % 4]
                    pf = ptf[cnt % 2]
                    cnt += 1
                    acc = P[4] if kb == 0 else P[5]
                    fw.mm(sp.v[kb:kb + 64, ca:cb], QK.v[hb:hb + 64, ck, ktok:ktok + 64],
                          QK.v[hb:hb + 64, cq, q0 + ca:q0 + cb])
                    if eidx is None:
                        fw.act(pt.v[kb:kb + 64, ca:cb], sp.v[kb:kb + 64, ca:cb], AF.Exp, scale=scale)
                    else:
                        fw.act(pf.v[kb:kb + 64, ca:cb], sp.v[kb:kb + 64, ca:cb], AF.Exp, scale=scale)
                        ev = E.v[kb:kb + 64, h, eidx[0]:eidx[1], :]
                        fw.tt(pt.v[kb:kb + 64, ca:cb].r("p (r q) -> p r q", q=64), pf.v[kb:kb + 64, ca:cb].r("p (r q) -> p r q", q=64),
                              ev, ALU.mult)
                    sk.push(lambda acc=acc, ca=ca, cb=cb, kb=kb, vtile=vtile, h=h, pt=pt, st=(not started[kb]), sp_=(last[kb] == wi):
                            fw.mm(acc.v[:, ca:cb], Vv.v[kb:kb + 64, vtile, h, :], pt.v[kb:kb + 64, ca:cb], start=st, stop=sp_))
                    started[kb] = True
                    if wi == min(5, len(work) - 1):
                        flush()

                def norm(h=h, NQ=NQ, ost=ost, q0=q0):
                    fw.mm(P[6].v[0:64, 0:NQ], G.esel.v[0:65, 0:64], osm.v[0:65, 0:NQ])
                    fw.act(lnr.v[0:64, 0:NQ], P[6].v[0:64, 0:NQ], AF.Ln)
                    fw.act(rec.v[0:64, 0:NQ], lnr.v[0:64, 0:NQ], AF.Exp, scale=-1.0)
                    fw.tt(ost.v[:, h, 0:NQ], osm.v[0:64, 0:NQ], rec.v[0:64, 0:NQ], ALU.mult, eng="pool")
                    if h == 3:
                        for hh in range(4):
                            hb = (hh % 2) * 64
                            fw.dma("sp" if hh % 2 else "act", S.MIXT[hb:hb + 64, 0 + hh // 2, q0:q0 + NQ], ost.v[:, hh, 0:NQ])

                def headend(NQ=NQ, norm=norm):
                    fw.copy(osb.v[0:65, 0:NQ], P[4].v[0:65, 0:NQ], "dve")
                    fw.tt(osm.v[0:65, 0:NQ], osb.v[0:65, 0:NQ], P[5].v[0:65, 0:NQ], ALU.add)
                    pend.append(norm)
                sk.push(headend)
        sk.drain()
        flush()


def phase_O(S, b, l, need_ctx):
    fw, I, G = S.fw, S.I, S.G
    with fw.phase():
        wo = fw.sb("wo", [128, 8, D], BF16)
        fw.dma("sp", wo.v, S.w_out_bf[l].r("(k p) n -> p k n", p=128))
        mxs = [fw.sb(f"mx{i}", [128, 8, 512], BF16) for i in range(2)]
        xts = [fw.sb(f"xto{i}", [128, 8, 512], F32) for i in range(2)]
        tiles = TOK_TILES[1:] + ([TOK_TILES[0]] if need_ctx else [])
        for ti, (t0, N) in enumerate(tiles):
            vec = b if t0 >= TC else S.nb
            mx, xt = mxs[ti % 2], xts[ti % 2]
            fw.dma("act", mx.v[:, :, 0:N], S.MIXT[:, :, t0:t0 + N])
            fw.dma("sp", xt.v[:, :, 0:N], S.XT[b, :, :, t0:t0 + N])
            for m in range(8):
                ps = nps(S)
                for k in range(8):
                    fw.mm(ps.v[:, 0:N], wo.v[:, k, m * 128:(m + 1) * 128], mx.v[:, k, 0:N], start=(k == 0), stop=(k == 7))
                fw.stt(xt.v[:, m, 0:N], ps.v[:, 0:N], G.mods.v[:, l, 16 + m, vec:vec + 1], xt.v[:, m, 0:N],
                       ALU.mult, ALU.add)
            fw.dma("sp", S.XT[b, :, :, t0:t0 + N], xt.v[:, :, 0:N])


def phase_mlp(S, b, l, need_ctx):
    fw, I, G, P = S.fw, S.I, S.G, S.P
    with fw.phase():
        cw = fw.sb("cw", [128, 3, 22], F32)
        cb = fw.sb("cb", [128, 22], F32)
        fw.dma("act", cw.v, I.conv_w[l])
        fw.dma("act", cb.v, I.conv_b[l])
        hwl = fw.sb("hwl", [128, 8, TL + 2], BF16)
        fw.memset(hwl.v[:, :, 0:1], 0.0, "pool")
        fw.memset(hwl.v[:, :, TL + 1:TL + 2], 0.0, "pool")
        if need_ctx:
            hwc = fw.sb("hwc", [128, 8, TC + 2], BF16)
            fw.memset(hwc.v[:, :, 0:1], 0.0, "pool")
            fw.memset(hwc.v[:, :, TC + 1:TC + 2], 0.0, "pool")
        tiles = TOK_TILES[1:] + ([TOK_TILES[0]] if need_ctx else [])
        with fw.phase():
            xw = fw.sb("xw", [128, 8, 512], F32)
            sq = fw.sb("sqw", [128, 8, 512], BF16)
            xn = fw.sb("xnw", [128, 8, 512], F32)
            rstd = fw.sb("rstdw", [128, 512], F32)
            lnt = fw.sb("lntw", [128, 512], F32)
            for (t0, n) in tiles:
                lat = t0 >= TC
                vec = b if lat else S.nb
                fw.dma("sp", xw.v[:, :, 0:n], S.XT[b, :, :, t0:t0 + n])
                fw.act(sq.v[:, :, 0:n], xw.v[:, :, 0:n], AF.Square)
                ps = nps(S)
                for k in range(8):
                    fw.mm(ps.v[:, 0:n], G.ones_bf.v, sq.v[:, k, 0:n], start=(k == 0), stop=(k == 7))
                rstd_from_sum(S, rstd.v[:, 0:n], ps.v[:, 0:n], 1024.0, EPS, lnt.v[:, 0:n])
                fw.tt(xn.v[:, :, 0:n], xw.v[:, :, 0:n], rstd.v[:, 0:n].bc(1, 8), ALU.mult)
                for k in range(8):
                    dst = hwl.v[:, k, 1 + t0 - TC:1 + t0 - TC + n] if lat else hwc.v[:, k, 1:1 + n]
                    fw.act(dst, xn.v[:, k, 0:n], AF.Identity, scale=G.A.v[:, l, 1, k, vec:vec + 1],
                           bias=G.mods.v[:, l, 24 + k, vec:vec + 1])
        gT = fw.sb("gT", [128, 22, TL], BF16)
        asb = fw.sb("asb", [128, TL + 2], F32)
        fw.memset(asb.v[:, 0:1], 0.0, "pool")
        fw.memset(asb.v[:, TL + 1:TL + 2], 0.0, "pool")
        c1 = fw.sb("c1", [128, TL], F32)
        c2 = fw.sb("c2", [128, TL], F32)
        if need_ctx:
            gTc = fw.sb("gTc", [128, 22, TC], BF16)
            asc = fw.sb("asc", [128, TC + 2], F32)
            fw.memset(asc.v[:, 0:1], 0.0, "pool")
            fw.memset(asc.v[:, TC + 1:TC + 2], 0.0, "pool")
            c1c = fw.sb("c1c", [128, TC], F32)
            c2c = fw.sb("c2c", [128, TC], F32)
        wus = [fw.sb(f"wu{i}", [128, 2, 8, 128], BF16) for i in range(2)]
        for j in range(22):
            wu = wus[j % 2]
            fw.dma("sp" if j % 2 else "act", wu.v, S.w_up_bf[l, j])
            for i in range(4):
                for k in range(8):
                    fw.mm(P[i].v, wu.v[:, 0, k, :], hwl.v[:, k, 1 + i * 512:1 + (i + 1) * 512], start=(k == 0), stop=(k == 7))
            for i in range(4):
                for k in range(8):
                    fw.mm(P[4 + i].v, wu.v[:, 1, k, :], hwl.v[:, k, 1 + i * 512:1 + (i + 1) * 512], start=(k == 0), stop=(k == 7))
            for i in range(4):
                fw.copy(asb.v[:, 1 + i * 512:1 + (i + 1) * 512], P[i].v, "act")
            fw.act(c1.v, asb.v[:, 0:TL], AF.Identity, scale=cw.v[:, 0, j:j + 1])
            fw.stt(c2.v, asb.v[:, 1:TL + 1], cw.v[:, 1, j:j + 1], c1.v, ALU.mult, ALU.add)
            fw.stt(c1.v, asb.v[:, 2:TL + 2], cw.v[:, 2, j:j + 1], c2.v, ALU.mult, ALU.add)
            fw.act(c2.v, c1.v, AF.Silu, bias=cb.v[:, j:j + 1])
            for i in range(4):
                fw.tt(gT.v[:, j, i * 512:(i + 1) * 512], c2.v[:, i * 512:(i + 1) * 512], P[4 + i].v, ALU.mult)
            if need_ctx:
                pc = P[0]
                for k in range(8):
                    fw.mm(pc.v[:, 0:TC], wu.v[:, 0, k, :], hwc.v[:, k, 1:TC + 1], start=(k == 0), stop=(k == 7))
                for k in range(8):
                    fw.mm(pc.v[:, TC:2 * TC], wu.v[:, 1, k, :], hwc.v[:, k, 1:TC + 1], start=(k == 0), stop=(k == 7))
                fw.copy(asc.v[:, 1:TC + 1], pc.v[:, 0:TC], "act")
                fw.act(c1c.v, asc.v[:, 0:TC], AF.Identity, scale=cw.v[:, 0, j:j + 1])
                fw.stt(c2c.v, asc.v[:, 1:TC + 1], cw.v[:, 1, j:j + 1], c1c.v, ALU.mult, ALU.add)
                fw.stt(c1c.v, asc.v[:, 2:TC + 2], cw.v[:, 2, j:j + 1], c2c.v, ALU.mult, ALU.add)
                fw.act(c2c.v, c1c.v, AF.Silu, bias=cb.v[:, j:j + 1])
                fw.tt(gTc.v[:, j, :], c2c.v, pc.v[:, TC:2 * TC], ALU.mult)
        wds = [fw.sb(f"wd{i}", [128, 22, 128], BF16) for i in range(2)]
        xt = fw.sb("xtm", [128, 8, 512], F32)
        it = 0
        for (t0, n) in tiles:
            lat = t0 >= TC
            vec = b if lat else S.nb
            fw.dma("act", xt.v[:, :, 0:n], S.XT[b, :, :, t0:t0 + n])
            for m in range(8):
                wd = wds[it % 2]
                it += 1
                fw.dma("sp", wd.v, S.w_down_bf[l, m])
                ps = nps(S)
                for j in range(22):
                    rhs = gT.v[:, j, t0 - TC:t0 - TC + n] if lat else gTc.v[:, j, 0:n]
                    fw.mm(ps.v[:, 0:n], wd.v[:, j, :], rhs, start=(j == 0), stop=(j == 21))
                fw.stt(xt.v[:, m, 0:n], ps.v[:, 0:n], G.mods.v[:, l, 40 + m, vec:vec + 1], xt.v[:, m, 0:n],
                       ALU.mult, ALU.add)
            fw.dma("sp", S.XT[b, :, :, t0:t0 + n], xt.v[:, :, 0:n])


def phase_final(S):
    fw, I, G = S.fw, S.I, S.G
    with fw.phase():
        gf = fw.sb("gf", [128, 8], F32)
        fw.dma("sp", gf.v, I.gfT.v)
        xts = [fw.sb(f"xf{i}", [128, 8, 128], F32) for i in range(2)]
        sq = fw.sb("sqf", [128, 8, 128], BF16)
        xn = fw.sb("xnf", [128, 8, 128], F32)
        rstd = fw.sb("rstdf", [128, 128], F32)
        lnt = fw.sb("lntf", [128, 128], F32)
        ots = [fw.sb(f"of{i}", [128, D], F32) for i in range(2)]
        it = 0
        for b in range(S.nb):
            for tt_ in range(TL // 128):
                xt, ot = xts[it % 2], ots[it % 2]
                it += 1
                t0 = TC + tt_ * 128
                fw.dma("sp", xt.v, S.XT[b, :, :, t0:t0 + 128])
                fw.act(sq.v, xt.v, AF.Square)
                ps = nps(S)
                for k in range(8):
                    fw.mm(ps.v[:, 0:128], G.ones_bf.v, sq.v[:, k, :], start=(k == 0), stop=(k == 7))
                rstd_from_sum(S, rstd.v, ps.v[:, 0:128], 1024.0, EPS, lnt.v)
                fw.tt(xn.v, xt.v, rstd.v.bc(1, 8), ALU.mult)
                fw.tt(xn.v, xn.v, gf.v.bc(2, 128), ALU.mult)
                for half in range(2):
                    p = nps(S)
                    for kk in range(4):
                        k = half * 4 + kk
                        fw.op("pe", lambda h: h.transpose(p.v[:, kk * 128:(kk + 1) * 128].ap, xn.v[:, k, :].ap,
                                                           G.ident.v.ap), [xn.v, G.ident.v], [p.v])
                    fw.copy(ot.v[:, half * 512:(half + 1) * 512], p.v, "act" if half else "dve")
                fw.dma("act", S.out[b, tt_ * 128:(tt_ + 1) * 128, :], ot.v)


RD = BF16
LDK = 0.6065306597126334
RK_EPS = 64e-5


def phase_rk(S, b, l, need_ctx):
    fw, I, G, P = S.fw, S.I, S.G, S.P
    SEG = 256
    NSEG = T // SEG
    with fw.phase():
        mu = fw.sb("mu", [128, 2, 7], F32)
        fw.dma("sp", mu.v, I.rk_mu[l])
        c0 = fw.sb("c0", [128, 7], F32)
        fw.tt(c0.v, mu.v[:, 0, :], mu.v[:, 1, :], ALU.add)
        fw.ts(c0.v, c0.v, -1.0, 1.0, ALU.mult, ALU.add)
        w0 = fw.sb("w0", [128, 2, 2], F32)
        a0 = fw.sb("a0", [128, 2, 2], F32)
        fw.dma("sp", w0.v, I.rk_w0[l])
        fw.dma("sp", a0.v, I.rk_a0[l])
        wup = fw.sb("wup", [32, 2, 256], F32)
        fw.dma("act", wup.v, I.rk_wup[l].r("d r c -> r d c"))
        aup = fw.sb("aup", [64, 2, 256], F32)
        fw.dma("act", aup.v[32:64], I.rk_aup[l].r("d r c -> r d c"))
        gup = fw.sb("gup", [128, 256], F32)
        fw.dma("act", gup.v[64:128], I.rk_gup[l])
        kkp = fw.sb("kkp", [128, 2], F32)
        ka = fw.sb("ka", [128, 2], F32)
        omka = fw.sb("omka", [128, 2], F32)
        rkp = fw.sb("rkp", [128, 2], F32)
        fw.dma("sp", kkp.v, I.rk_kk[l])
        fw.dma("sp", ka.v, I.rk_ka[l])
        fw.dma("sp", rkp.v, I.rk_rk[l])
        fw.ts(omka.v, ka.v, -1.0, 1.0, ALU.mult, ALU.add)
        lnw = fw.sb("lnw", [128, 2, 64], F32)
        lnb = fw.sb("lnb", [128, 2, 64], F32)
        for hh in range(2):
            fw.dma("sp", lnw.v[hh * 64:(hh + 1) * 64],
                   I.rk_lnw[l:l + 1, :].r("o (hp hh i) -> o hp hh i", hp=2, hh=2)[:, :, hh, :].f(
                       lambda a: a.broadcast_to([64, 2, 64])))
            fw.dma("act", lnb.v[hh * 64:(hh + 1) * 64],
                   I.rk_lnb[l:l + 1, :].r("o (hp hh i) -> o hp hh i", hp=2, hh=2)[:, :, hh, :].f(
                       lambda a: a.broadcast_to([64, 2, 64])))
        eye = fw.sb("eye", [128, 64], F32)
        fw.dma("sp", eye.v, I.c["rk_eye"].v)
        mkk = fw.sb("mkk", [128, 2, 2, 64], F32)
        fw.dma("sp", mkk.v, I.c["rk_mask"].v)
        bm = fw.sb("bm", [128, 2, 64], F32)
        fw.memset(bm.v, 0.0)
        fw.memset(bm.v[0:64, 0, :], 1.0)
        fw.memset(bm.v[64:128, 1, :], 1.0)
        mbd = fw.sb("mbd", [128, 2, 2, 2, 64], F32)
        for d in range(2):
            fw.tt(mbd.v[:, d], mkk.v[:, d].bc(2, 2), bm.v.bc(1, 2), ALU.mult)
        rm = fw.sb("rm", [128, 2, 512], F32)
        fw.dma("act", rm.v, I.c["rk_rm"].v)
        bg = fw.sb("bg", [128, 2, 2, T], BF16)
        Yacc = fw.sb("Yacc", [128, 36, 2, 64], F32)
        fw.memset(Yacc.v, 0.0, "pool")
        ST = fw.sb("ST", [128, 2, 2, 64], F32)
        fw.memset(ST.v, 0.0)
        if RD == F32:
            STb, identR = ST, G.ident
        else:
            STb = fw.sb("STb", [128, 2, 2, 64], RD)
            fw.memset(STb.v, 0.0)
            identR = fw.sb("identR", [128, 128], RD)
            fw.copy(identR.v, G.ident.v)

        with fw.phase():
            zw = fw.sb("zw", [128, 7, SEG + 2], F32)
            zs = fw.sb("zs", [128, 7, SEG], F32)
            zt = fw.sb("zt", [128, 7, SEG], F32)
            kk = fw.sb("kk", [128, 2, SEG], F32)
            kq = fw.sb("kq", [128, 2, SEG], F32)
            kkn = fw.sb("kkn", [128, 2, SEG], F32)
            lnk = fw.sb("lnk", [128, 2, SEG], F32)
            twd = fw.sb("twd", [32, SEG], F32)
            sgd = fw.sb("sgd", [128, SEG], F32)
            vbd = fw.sb("vbd", [128, 2, 4, 2, 64], F32)
            X = NS()
            for nm in ("sig", "asig", "Ls", "ta", "tb", "Bt", "Kt", "Bh", "Kh"):
                setattr(X, nm, fw.sb(nm, [128, 2, SEG], F32))
            X.Btr = X.Bt if RD == F32 else fw.sb("Btr", [128, 2, SEG], RD)
            X.Ktr = X.Kt if RD == F32 else fw.sb("Ktr", [128, 2, SEG], RD)
            X.AR = fw.sb("AR", [128, 2, 4, 2, 64], RD)
            X.Btbd = fw.sb("Btbd", [128, 2, 4, 2, 64], RD)
            X.Ktbd = fw.sb("Ktbd", [128, 2, 4, 2, 64], RD)
            X.Bhbd = fw.sb("Bhbd", [128, 2, 4, 2, 64], RD)
            X.Khbd = fw.sb("Khbd", [128, 2, 4, 2, 64], RD)
            X.Nn = [[fw.sb(f"Nn{h}{i}", [128, 4, 128], RD) for i in range(2)] for h in range(2)]
            X.NT = [[fw.sb(f"NT{h}{i}", [128, 4, 128], RD) for i in range(2)] for h in range(2)]
            X.Xx = [[fw.sb(f"Xx{h}{i}", [128, 4, 128], RD) for i in range(2)] for h in range(2)]
            X.LC = fw.sb("LC", [128, 2, 4], F32)
            SX = []
            for i in range(2):
                Y = NS()
                Y.ARbd = fw.sb(f"ARbd{i}", [128, 2, 4, 2, 2, 64], RD)
                Y.BhT = fw.sb(f"BhT{i}", [128, 2, 4, 128], RD)
                Y.KhT = fw.sb(f"KhT{i}", [128, 2, 4, 128], RD)
                Y.MB = fw.sb(f"MB{i}", [128, 2, 4, 2, 128], RD)
                Y.MK = fw.sb(f"MK{i}", [128, 2, 4, 2, 128], RD)
                Y.TT = fw.sb(f"TT{i}", [128, 2, 4, 128], RD)
                Y.PC = fw.sb(f"PC{i}", [128, 2, 4], F32)
                Y.VT = fw.sb(f"VT{i}", [128, 2, 4, 64], RD)
                SX.append(Y)
            RHSsb = fw.sb("RHSsb", [128, 2, 64], RD)
            Usb = fw.sb("Usb", [128, 2, 64], RD)

            def c4(v):
                return v.r("p a (c s) -> p a c s", s=64)

            def prep(seg, d, Y):
                t0 = seg * SEG
                s0, s1 = (0, TC) if seg == 0 else (TC, T)
                w0_, w1_ = max(t0 - 1, s0), min(t0 + SEG + 1, s1)
                cc, n = w0_ - (t0 - 1), w1_ - w0_
                if n < SEG + 2:
                    fw.memset(zw.v, 0.0, "pool")
                fw.dma("sp", zw.v[:, :, cc:cc + n], S.ZT_rk[:, :, w0_:w1_])
                yield
                fw.tt(zs.v, zw.v[:, :, 1:SEG + 1], c0.v.bc(2, SEG), ALU.mult)
                yield
                fw.tt(zt.v, zw.v[:, :, 0:SEG], mu.v[:, 0, :].bc(2, SEG), ALU.mult, eng="pool")
                yield
                fw.tt(zs.v, zs.v, zt.v, ALU.add)
                yield
                fw.tt(zt.v, zw.v[:, :, 2:SEG + 2], mu.v[:, 1, :].bc(2, SEG), ALU.mult, eng="pool")
                yield
                fw.tt(zs.v, zs.v, zt.v, ALU.add)
                yield
                yield
                r_, k_, v_ = zs.v[:, 0:2, :], zs.v[:, 2:4, :], zs.v[:, 4:6, :]
                fw.act(twd.v, zs.v[0:32, 6, :], AF.Tanh)
                yield
                fw.act(sgd.v[64:128], zs.v[64:128, 6, :], AF.Sigmoid)
                yield
                fw.tt(kk.v, k_, kkp.v.bc(2, SEG), ALU.mult)
                yield
                fw.tt(kq.v, kk.v, kk.v, ALU.mult, eng="pool")
                yield
                ps = P[0]
                fw.mm(ps.v[:, 0:2 * SEG], G.blockones.v, kq.v.r("p a t -> p (a t)"))
                yield
                fw.ts(lnk.v.r("p a t -> p (a t)"), ps.v[:, 0:2 * SEG], 1e-24, None, ALU.max)
                yield
                fw.act(lnk.v, lnk.v, AF.Ln)
                yield
                fw.act(lnk.v, lnk.v, AF.Exp, scale=-0.5)
                yield
                fw.tt(kkn.v, kk.v, lnk.v, ALU.mult)
                yield
                yield
                fw.tt(kq.v, r_, k_, ALU.mult, eng="pool")
                yield
                fw.tt(kq.v, kq.v, rkp.v.bc(2, SEG), ALU.mult, eng="pool")
                yield
                ps = P[1]
                fw.mm(ps.v[:, 0:2 * SEG], G.blockones.v, kq.v.r("p a t -> p (a t)"))
                yield
                fw.tt(bg.v[:, :, 0, t0:t0 + SEG], ps.v[:, 0:2 * SEG].r("p (a t) -> p a t", a=2), v_, ALU.mult)
                yield
                ps = P[2]
                for hp in range(2):
                    fw.mm(ps.v[:, hp * SEG:(hp + 1) * SEG], gup.v[64:128, hp * 128:(hp + 1) * 128], sgd.v[64:128, :])
                fw.copy(bg.v[:, :, 1, t0:t0 + SEG], ps.v[:, 0:2 * SEG].r("p (a t) -> p a t", a=2), "act")
                yield
                for hp in range(2):
                    fw.tt(vbd.v[:, hp], c4(v_)[:, hp].bc(2, 2), bm.v.bc(1, 4), ALU.mult, eng="pool")
                ps = P[3]
                for hp in range(2):
                    for c in range(4):
                        fw.mm(ps.v[:, (hp * 4 + c) * 64:(hp * 4 + c + 1) * 64],
                              vbd.v[:, hp, c].r("p a s -> p (a s)"), eye.v)
                fw.copy(Y.VT.v.r("p a c i -> p (a c i)"), ps.v[:, 0:512], "act")
                yield
                yield
                ps = P[0]
                for hp in range(2):
                    fw.mm(ps.v[:, hp * SEG:(hp + 1) * SEG], wup.v[0:32, d, hp * 128:(hp + 1) * 128], twd.v[0:32, :])
                for hp in range(2):
                    fw.act(X.sig.v[:, hp, :], ps.v[:, hp * SEG:(hp + 1) * SEG], AF.Sigmoid, bias=w0.v[:, d, hp:hp + 1])
                ps = P[1]
                for hp in range(2):
                    fw.mm(ps.v[:, hp * SEG:(hp + 1) * SEG], aup.v[32:64, d, hp * 128:(hp + 1) * 128], zs.v[32:64, 6, :])
                for hp in range(2):
                    fw.act(X.asig.v[:, hp, :], ps.v[:, hp * SEG:(hp + 1) * SEG], AF.Sigmoid, bias=a0.v[:, d, hp:hp + 1])
                for hp in range(2):
                    fw.ts(X.tb.v[:, hp, :], X.asig.v[:, hp, :], ka.v[:, hp:hp + 1], omka.v[:, hp:hp + 1], ALU.mult, ALU.add)
                fw.tt(X.tb.v, X.tb.v, k_, ALU.mult)
                fw.tt(X.ta.v, kkn.v, X.asig.v, ALU.mult, eng="pool")
                sf = X.sig.v.r("p a t -> p (a t)")
                lf = X.Ls.v.r("p a t -> p (a t)")
                if d == 0:
                    fw.scan(lf, rm.v[:, 0, :], sf)
                else:
                    fw.scan(lf[:, ::-1], rm.v[:, 1, ::-1], sf[:, ::-1])
                yield
                L4 = c4(X.Ls.v)
                lc = L4[:, :, :, 63:64] if d == 0 else L4[:, :, :, 0:1]
                fw.copy(X.LC.v, lc.r("p a c o -> p a (c o)"), "pool")
                yield
                fw.act(Y.PC.v, X.LC.v, AF.Exp, scale=-LDK)
                yield
                eN, eC, eP, ePm = X.Bh, X.Kh, X.Bt, X.Kt
                fw.act(eN.v, X.Ls.v, AF.Exp, scale=LDK)
                yield
                fw.act(eP.v, X.Ls.v, AF.Exp, scale=-LDK)
                yield
                fw.tt(X.asig.v, X.Ls.v, X.sig.v, ALU.subtract)
                fw.act(ePm.v, X.asig.v, AF.Exp, scale=-LDK)
                yield
                fw.tt(c4(X.asig.v), X.LC.v.bc(3, 64), L4, ALU.subtract, eng="pool")
                AR = X.AR.v
                fw.stt(AR[:, :, :, 0, :], c4(kkn.v), -1.0, c4(ePm.v), ALU.mult, ALU.mult)
                fw.tt(AR[:, :, :, 1, :], c4(r_), c4(eP.v), ALU.mult)
                yield
                fw.tt(X.Btr.v, X.ta.v, eN.v, ALU.mult)
                yield
                fw.tt(X.Ktr.v, X.tb.v, eN.v, ALU.mult)
                yield
                fw.act(eC.v, X.asig.v, AF.Exp, scale=-LDK)
                yield
                fw.tt(X.Bh.v, X.ta.v, eC.v, ALU.mult)
                yield
                fw.tt(X.Kh.v, X.tb.v, eC.v, ALU.mult, eng="pool")
                yield
                yield
                for hp in range(2):
                    bmc = bm.v.bc(1, 4)
                    fw.tt(X.Btbd.v[:, hp], c4(X.Btr.v)[:, hp].bc(2, 2), bmc, ALU.mult, eng="pool")
                    fw.tt(X.Ktbd.v[:, hp], c4(X.Ktr.v)[:, hp].bc(2, 2), bmc, ALU.mult)
                    for ar in range(2):
                        fw.tt(Y.ARbd.v[:, hp, :, ar], AR[:, hp, :, ar, :].bc(2, 2), bmc, ALU.mult,
                              eng="pool" if ar else "dve")
                yield
                banks = [(P[3], P[0], P[1]), (P[2], P[4], P[5])]
                for hp in range(2):
                    pB, pK, pN = banks[hp]
                    for c in range(4):
                        arc = AR[:, hp, c].r("p a t -> p (a t)")
                        fw.mm(pB.v[:, c * 128:(c + 1) * 128], X.Btbd.v[:, hp, c].r("p a s -> p (a s)"), arc)
                        fw.mm(pK.v[:, c * 128:(c + 1) * 128], X.Ktbd.v[:, hp, c].r("p a s -> p (a s)"), arc)
                        fw.mm(pN.v[:, c * 64:(c + 1) * 64], Y.ARbd.v[:, hp, c, 0].r("p a t -> p (a t)"),
                              X.Btr.v[:, hp, c * 64:(c + 1) * 64])
                    for ar in range(2):
                        mk_ = mbd.v[:, d, ar].bc(1, 4)
                        fw.tt(Y.MB.v[:, hp, :, ar].r("p c (a t) -> p c a t", a=2),
                              pB.v.r("p (c a t) -> p c a t", c=4, a=2)[:, :, ar, :].bc(2, 2), mk_, ALU.mult)
                        fw.tt(Y.MK.v[:, hp, :, ar].r("p c (a t) -> p c a t", a=2),
                              pK.v.r("p (c a t) -> p c a t", c=4, a=2)[:, :, ar, :].bc(2, 2), mk_, ALU.mult,
                              eng="dve")
                    fw.tt(X.NT[hp][0].v.r("p c (a t) -> p c a t", a=2), pN.v[:, 0:256].r("p (c t) -> p c t", c=4).bc(2, 2),
                          mbd.v[:, 1 - d, 0].bc(1, 4), ALU.mult)
                    fw.tt(X.Xx[hp][0].v, Y.MB.v[:, hp, :, 0], G.ident.v.bc(1, 4), ALU.add, eng="pool")
                yield
                Ncur = [Y.MB.v[:, hp, :, 0] for hp in range(2)]
                NTcur = [X.NT[hp][0].v for hp in range(2)]
                Xcur = [X.Xx[hp][0].v for hp in range(2)]
                for lev in range(1, 6):
                    NTn = [X.NT[hp][lev % 2].v for hp in range(2)]
                    Nn = [X.Nn[hp][lev % 2].v for hp in range(2)]
                    for hp in range(2):
                        p1, p2, p3 = banks[hp]
                        for c in range(4):
                            cs = slice(c * 128, (c + 1) * 128)
                            if lev < 5:
                                fw.mm(p1.v[:, cs], NTcur[hp][:, c, :], Ncur[hp][:, c, :])
                            fw.mm(p2.v[:, cs], Ncur[hp][:, c, :], NTcur[hp][:, c, :])
                    for hp in range(2):
                        p1, p2, p3 = banks[hp]
                        if lev < 5:
                            fw.copy(Nn[hp], p1.v.r("p (c t) -> p c t", c=4), "act")
                        fw.copy(NTn[hp], p2.v.r("p (c t) -> p c t", c=4), "dve")
                    for hp in range(2):
                        p1, p2, p3 = banks[hp]
                        for c in range(4):
                            fw.mm(p3.v[:, c * 128:(c + 1) * 128], NTn[hp][:, c, :], Xcur[hp][:, c, :])
                    for hp in range(2):
                        p1, p2, p3 = banks[hp]
                        Xn = X.Xx[hp][lev % 2].v if lev < 5 else Y.TT.v[:, hp]
                        fw.tt(Xn, p3.v.r("p (c t) -> p c t", c=4), Xcur[hp], ALU.add)
                        Xcur[hp] = Xn
                        if lev < 5:
                            Ncur[hp], NTcur[hp] = Nn[hp], NTn[hp]
                    yield
                for hp in range(2):
                    bmc = bm.v.bc(1, 4)
                    fw.tt(X.Bhbd.v[:, hp], c4(X.Bh.v)[:, hp].bc(2, 2), bmc, ALU.mult, eng="pool")
                    fw.tt(X.Khbd.v[:, hp], c4(X.Kh.v)[:, hp].bc(2, 2), bmc, ALU.mult)
                for hp in range(2):
                    pB, pK = banks[hp][0], banks[hp][1]
                    for c in range(4):
                        cs = slice(c * 128, (c + 1) * 128)
                        fw.mm(pB.v[:, cs], X.Bhbd.v[:, hp, c].r("p a s -> p (a s)"), identR.v)
                        fw.mm(pK.v[:, cs], X.Khbd.v[:, hp, c].r("p a s -> p (a s)"), identR.v)
                    fw.copy(Y.BhT.v[:, hp], pB.v.r("p (c t) -> p c t", c=4), "act")
                    fw.copy(Y.KhT.v[:, hp], pK.v.r("p (c t) -> p c t", c=4), "dve")
                yield

            def chunk_step(d, c, q, Y):
                pRU, pYS = P[6], P[7]
                pR, pU = pRU.v[:, 0:128], pRU.v[:, 128:256]
                pY, pS = pYS.v[:, 0:128], pYS.v[:, 128:256]
                for hp in range(2):
                    oc = slice(hp * 64, (hp + 1) * 64)
                    fw.mm(pR[:, oc], Y.ARbd.v[:, hp, c, 0].r("p a t -> p (a t)"), STb.v[:, d, hp, :], start=True, stop=False)
                    fw.mm(pR[:, oc], Y.MK.v[:, hp, c, 0], Y.VT.v[:, hp, c, :], start=False, stop=True)
                yield
                fw.copy(RHSsb.v, pR.r("p (a i) -> p a i", a=2), "act")
                yield
                for hp in range(2):
                    oc = slice(hp * 64, (hp + 1) * 64)
                    fw.mm(pU[:, oc], Y.TT.v[:, hp, c, :], RHSsb.v[:, hp, :])
                yield
                fw.copy(Usb.v, pU.r("p (a i) -> p a i", a=2), "dve")
                yield
                for hp in range(2):
                    oc = slice(hp * 64, (hp + 1) * 64)
                    fw.mm(pY[:, oc], Y.ARbd.v[:, hp, c, 1].r("p a t -> p (a t)"), STb.v[:, d, hp, :], start=True, stop=False)
                    fw.mm(pY[:, oc], Y.MB.v[:, hp, c, 1], Usb.v[:, hp, :], start=False, stop=False)
                    fw.mm(pY[:, oc], Y.MK.v[:, hp, c, 1], Y.VT.v[:, hp, c, :], start=False, stop=True)
                    fw.mm(pS[:, oc], Y.BhT.v[:, hp, c, :], Usb.v[:, hp, :], start=True, stop=False)
                    fw.mm(pS[:, oc], Y.KhT.v[:, hp, c, :], Y.VT.v[:, hp, c, :], start=False, stop=True)
                yield
                for hp in range(2):
                    oc = slice(hp * 64, (hp + 1) * 64)
                    fw.stt(ST.v[:, d, hp, :], ST.v[:, d, hp, :], Y.PC.v[:, hp, c:c + 1], pS[:, oc], ALU.mult, ALU.add)
                yield
                if RD != F32:
                    fw.copy(STb.v[:, d], ST.v[:, d], "act")
                fw.tt(Yacc.v[:, q], pY.r("p (a i) -> p a i", a=2), Yacc.v[:, q], ALU.add)
                yield

            work = [(0, seg) for seg in range(NSEG)] + [(1, seg) for seg in [0] + list(range(NSEG - 1, 0, -1))]

            def drain(g, k=None):
                n = 0
                for _ in g:
                    n += 1
                    if k is not None and n >= k:
                        return
            cur = prep(work[0][1], work[0][0], SX[0])
            drain(cur)
            for wi, (d, seg) in enumerate(work):
                Y = SX[wi % 2]
                nxt = prep(work[wi + 1][1], work[wi + 1][0], SX[(wi + 1) % 2]) if wi + 1 < len(work) else None
                cl = range(4) if d == 0 else range(3, -1, -1)
                for c in cl:
                    for _ in chunk_step(d, c, seg * 4 + c, Y):
                        if nxt is not None:
                            drain(nxt, 2)
                if nxt is not None:
                    drain(nxt)

        with fw.phase():
            Yf = Yacc.v.r("p q a i -> p (q a) i")
            sm = fw.sb("sm", [128, 72], F32)
            sq = fw.sb("sqy", [128, 72, 64], F32)
            s2 = fw.sb("s2", [128, 72], F32)
            var = fw.sb("var", [128, 72], F32)
            fw.reduce(sm.v, Yf)
            fw.ts(sm.v, sm.v, 1.0 / 64, None, ALU.mult)
            fw.tt(sq.v, Yf, sm.v.bc(2, 64), ALU.subtract)
            yn = fw.sb("yn", [128, 72, 64], F32)
            fw.tt(yn.v, sq.v, sq.v, ALU.mult, eng="pool")
            fw.reduce(s2.v, yn.v)
            fw.act(var.v, s2.v, AF.Ln, scale=1.0 / 64, bias=RK_EPS)
            fw.act(var.v, var.v, AF.Exp, scale=-0.5)
            fw.tt(yn.v, sq.v, var.v.bc(2, 64), ALU.mult)
            y4 = yn.v.r("p (q a) i -> p q a i", a=2)
            fw.tt(y4, y4, lnw.v.bc(1, 36), ALU.mult)
            fw.tt(y4, y4, lnb.v.bc(1, 36), ALU.add, eng="pool")
            ybd = [fw.sb(f"ybd{i}", [128, 8, 2, 64], F32) for i in range(2)]
            osts = [fw.sb(f"ork{i}", [128, 512], F32) for i in range(2)]
            ostb = [fw.sb(f"orkb{i}", [128, 512], BF16) for i in range(2)]
            it = 0
            nq = 36
            for hp in range(2):
                for q0 in range(0, nq, 8):
                    nqq = min(8, nq - q0)
                    ps = P[it % 4]
                    o1, o2, yb = osts[it % 2], ostb[it % 2], ybd[it % 2]
                    it += 1
                    fw.tt(yb.v[:, 0:nqq], y4[:, q0:q0 + nqq, hp, :].bc(2, 2), bm.v.bc(1, nqq), ALU.mult)
                    for qq in range(nqq):
                        fw.mm(ps.v[:, qq * 64:(qq + 1) * 64], yb.v[:, qq].r("p a i -> p (a i)"), eye.v)
                    n = nqq * 64
                    t0 = q0 * 64
                    fw.tt(o1.v[:, 0:n], ps.v[:, 0:n], bg.v[:, hp, 0, t0:t0 + n], ALU.add)
                    fw.tt(o2.v[:, 0:n], o1.v[:, 0:n], bg.v[:, hp, 1, t0:t0 + n], ALU.mult, eng="pool")
                    fw.dma("sp", S.MIXT[:, 4 + hp, t0:t0 + n], o2.v[:, 0:n])


NCORES = 8
_CACHE = {}


def kernel(**inputs):
    inp = {k: np.asarray(v) for k, v in inputs.items()}
    B = inp["x"].shape[0]
    nb = B // NCORES
    if "prog" not in _CACHE:
        _CACHE["prog"] = build_program(nb)
    nc, fw = _CACHE["prog"]
    in_maps = [_layout_inputs(inp, nb, c) for c in range(NCORES)]
    res = run_bass_kernel_spmd(nc, in_maps, core_ids=list(range(NCORES)))
    out = np.concatenate([np.asarray(r["out"]) for r in res.results], axis=0)
    return out.astype(np.float32)
```
